# Optimizing a Trainium2 kernel written in Bass

```python
import math
import jax
import jax.numpy as jnp
from jax import lax
import numpy as np

D_MODEL = 1024
BATCH = 2
SEQ = 8192
DEPTH = 4
DEC_BATCH = 128
DEC_SEQ = 8
PAST_LEN = 2048
PAGE_SIZE = 128

N_HEADS = 16
HEAD_DIM = D_MODEL // N_HEADS
ATTN_SCALE = HEAD_DIM ** -0.5
D_FF = 4 * D_MODEL
N_A_LAYERS = DEPTH // 2
N_B_LAYERS = DEPTH - N_A_LAYERS
EPS = 1e-6
N_BUCKETS = 32
MAX_DISTANCE = 2048
NSA_KV_HEADS = 4
CMP_LEN = 32
CMP_STRIDE = 16
CMP_HIDDEN = 2 * HEAD_DIM
SEL_BLOCK = 64
SEL_TOP_N = 16
SEL_Q_BLOCK = 64
NSA_WINDOW = 512
N_BRANCHES = 3
NSA_SPLITS = [N_HEADS * HEAD_DIM + i * NSA_KV_HEADS * HEAD_DIM for i in range(7)]
NSA_IN = N_HEADS * HEAD_DIM + 6 * NSA_KV_HEADS * HEAD_DIM + N_BRANCHES * N_HEADS
DIL_KV_HEADS = 4
DIL_GROUPS = ((128, 1), (512, 4), (2048, 16))
N_DIL_GROUPS = len(DIL_GROUPS)
DIL_WINDOW_MAX = max(w for w, _ in DIL_GROUPS)
BAND_BLOCK = 128

kernel_name = 'yoco_nsa_dilated_decoder_step'


def rms_norm(x, g):
    xf = x.astype(jnp.float32)
    y = xf * lax.rsqrt(jnp.mean(xf * xf, axis=-1, keepdims=True) + EPS)
    return (y * g.astype(jnp.float32)).astype(x.dtype)


def rel_bucket(dist):
    max_exact = N_BUCKETS // 2
    d = jnp.maximum(dist, 0)
    ratio = jnp.log(jnp.maximum(d, 1).astype(jnp.float32) / max_exact) / math.log(MAX_DISTANCE / max_exact)
    large = max_exact + (ratio * (N_BUCKETS - max_exact)).astype(jnp.int32)
    return jnp.where(d < max_exact, d, jnp.minimum(large, N_BUCKETS - 1))


def rel_bias_lookup(rel_bias, dist):
    return rel_bias[rel_bucket(dist)].astype(jnp.float32)


def masked_softmax(logits, mask):
    l = jnp.where(mask, logits, -jnp.inf)
    m = jnp.max(l, axis=-1)
    m = jnp.where(jnp.isfinite(m), m, 0.0)
    p = jnp.exp(l - m[..., None])
    s = jnp.sum(p, axis=-1)
    return p / jnp.maximum(s, 1e-30)[..., None], m, s


def sq_relu_mlp(x, g, w1, w2):
    h = jax.nn.relu(rms_norm(x, g) @ w1)
    return x + (h * h) @ w2


def banded_attention(q, k, v, rel_bias, window, dil):
    n, L, h, hd = q.shape
    kvh = k.shape[2]
    rep = h // kvh
    nb = L // BAND_BLOCK
    n_prev = -(-window // BAND_BLOCK)
    pad = n_prev * BAND_BLOCK
    kb_len = pad + BAND_BLOCK
    qi = jnp.arange(BAND_BLOCK)[:, None] + pad
    ki = jnp.arange(kb_len)[None, :]
    dist = qi - ki
    key_real = (jnp.arange(nb)[:, None, None] * BAND_BLOCK + ki[None]) >= pad
    valid = ((dist >= 0) & (dist <= window))[None] & key_real
    bias = jnp.transpose(rel_bias_lookup(rel_bias, dist * dil), (2, 0, 1)).reshape(kvh, rep, BAND_BLOCK, kb_len)

    def one(args):
        qs, ks, vs = args
        kp = jnp.pad(ks, ((pad, 0), (0, 0), (0, 0))).reshape(nb + n_prev, BAND_BLOCK, kvh, hd)
        vp = jnp.pad(vs, ((pad, 0), (0, 0), (0, 0))).reshape(nb + n_prev, BAND_BLOCK, kvh, hd)
        kband = jnp.concatenate([kp[o:o + nb] for o in range(n_prev + 1)], axis=1)
        vband = jnp.concatenate([vp[o:o + nb] for o in range(n_prev + 1)], axis=1)
        qb = qs.reshape(nb, BAND_BLOCK, kvh, rep, hd)
        logits = jnp.einsum('bqgrd,bkgd->bgrqk', qb, kband, preferred_element_type=jnp.float32) + bias
        p, m, s = masked_softmax(logits, valid[:, None, None])
        o = jnp.einsum('bgrqk,bkgd->bqgrd', p.astype(vs.dtype), vband)
        rows = lambda a: jnp.transpose(a, (0, 3, 1, 2)).reshape(L, h)
        return o.reshape(L, h, hd), rows(m), rows(s)

    return lax.map(one, (q, k, v))


def window_sample(q, k_all, v_all, buf_len, rel_bias, window):
    db, t, h, hd = q.shape
    kvh = k_all.shape[2]
    rep = h // kvh
    L = k_all.shape[1]
    dist = buf_len + jnp.arange(t)[:, None] - jnp.arange(L)[None, :]
    valid = (dist >= 0) & (dist <= window)
    bias = jnp.transpose(rel_bias_lookup(rel_bias, dist), (0, 2, 1)).reshape(t, kvh, rep, L)
    qg = q.reshape(db, t, kvh, rep, hd)
    logits = jnp.einsum('btgrd,blgd->btgrl', qg, k_all, preferred_element_type=jnp.float32) + bias
    p, _, _ = masked_softmax(logits, valid[None, :, None, None, :])
    o = jnp.einsum('btgrl,blgd->btgrd', p.astype(v_all.dtype), v_all)
    return o.reshape(db, t, h, hd)


def nsa_project(x, attn_norm, w_in, q_norm, k_norm):
    b, t, _ = x.shape
    z = rms_norm(x, attn_norm) @ w_in
    q, kc, vc, ks, vs, kw, vw, g = jnp.split(z, NSA_SPLITS, axis=-1)
    kv = lambda a: a.reshape(b, t, NSA_KV_HEADS, HEAD_DIM)
    q = rms_norm(q.reshape(b, t, N_HEADS, HEAD_DIM), q_norm) * ATTN_SCALE
    gates = jax.nn.sigmoid(g.astype(jnp.float32)).reshape(b, t, N_BRANCHES, N_HEADS)
    return (q, kv(kc), kv(vc), rms_norm(kv(ks), k_norm[1]), kv(vs),
            rms_norm(kv(kw), k_norm[2]), kv(vw), gates)


def compress(k, v, cmp_pe, cmp_w1, cmp_w2, k_norm_c):
    n_chunks = k.shape[1] // CMP_STRIDE
    n_sub = CMP_LEN // CMP_STRIDE
    n_cmp = n_chunks - n_sub + 1

    def phi(x, i):
        ch = x[:, :n_chunks * CMP_STRIDE].reshape(x.shape[0], n_chunks, CMP_STRIDE, *x.shape[2:])
        w1s = cmp_w1[i].reshape(n_sub, CMP_STRIDE, HEAD_DIM, CMP_HIDDEN)
        hid = jnp.einsum('ld,ldh->h', cmp_pe[i], cmp_w1[i])
        for o in range(n_sub):
            hid = hid + jnp.einsum('nclgd,ldh->ncgh', ch[:, o:o + n_cmp], w1s[o])
        return jnp.einsum('ncgh,hd->ncgd', jax.nn.silu(hid), cmp_w2[i])

    ck = rms_norm(phi(k, 0), k_norm_c)
    cv = phi(v, 1)
    c_end = jnp.arange(n_cmp) * CMP_STRIDE + CMP_LEN - 1
    return ck, cv, c_end


def to_sel_blocks(k):
    n, L, kvh, hd = k.shape
    n_sel = -(-L // SEL_BLOCK)
    k = jnp.pad(k, ((0, 0), (0, n_sel * SEL_BLOCK - L), (0, 0), (0, 0)))
    return k.reshape(n, n_sel, SEL_BLOCK, kvh, hd).transpose(0, 3, 1, 2, 4)


def selection_map(c_end, n_sel):
    c_start = c_end - CMP_LEN + 1
    j0 = jnp.arange(n_sel) * SEL_BLOCK
    return ((c_start[:, None] <= j0[None] + SEL_BLOCK - 1) & (c_end[:, None] >= j0[None])).astype(jnp.float32)


def nsa_cmp_sel_block(q, qpos, ck, cv, c_end, sel_map, kb, vb, rel_bias):
    n, nq, h, hd = q.shape
    kvh = ck.shape[2]
    rep = h // kvh
    n_sel = kb.shape[2]
    n_top = min(SEL_TOP_N, n_sel)
    qg = q.reshape(n, nq, kvh, rep, hd)
    dist_c = qpos[:, :, None] - c_end[None, None, :]
    bias_c = jnp.swapaxes(rel_bias_lookup(rel_bias, dist_c), 2, 3).reshape(qpos.shape[0], nq, kvh, rep, -1)
    logits_c = jnp.einsum('nqgrd,ncgd->nqgrc', qg, ck, preferred_element_type=jnp.float32) + bias_c
    p_c, _, _ = masked_softmax(logits_c, (dist_c >= 0)[:, :, None, None, :])
    o_c = jnp.einsum('nqgrc,ncgd->nqgrd', p_c.astype(cv.dtype), cv)
    imp = jnp.einsum('nqgrc,cj->nqgj', p_c, sel_map)
    j = jnp.arange(n_sel)
    cur = (qpos // SEL_BLOCK)[..., None]
    forced = (j == 0) | (j == cur) | (j == cur - 1)
    future = j * SEL_BLOCK > qpos[..., None]
    imp = jnp.where(forced[:, :, None], jnp.inf, jnp.where(future[:, :, None], -jnp.inf, imp))
    _, idx = lax.top_k(imp, n_top)
    ni = jnp.arange(n)[:, None, None, None]
    gi = jnp.arange(kvh)[None, None, :, None]
    k_s = kb[ni, gi, idx]
    v_s = vb[ni, gi, idx]
    kpos = idx[..., None] * SEL_BLOCK + jnp.arange(SEL_BLOCK)
    dist_s = qpos[:, :, None, None, None] - kpos
    bias_s = rel_bias.reshape(N_BUCKETS, kvh, rep)[rel_bucket(dist_s), jnp.arange(kvh)[:, None, None]]
    bias_s = jnp.moveaxis(bias_s, -1, 3).astype(jnp.float32)
    logits_s = jnp.einsum('nqgrd,nqgtkd->nqgrtk', qg, k_s, preferred_element_type=jnp.float32) + bias_s
    logits_s = logits_s.reshape(n, nq, kvh, rep, n_top * SEL_BLOCK)
    mask_s = (dist_s >= 0).reshape(n, nq, kvh, 1, n_top * SEL_BLOCK)
    p_s, _, _ = masked_softmax(logits_s, mask_s)
    o_s = jnp.einsum('nqgrk,nqgkd->nqgrd', p_s.astype(v_s.dtype), v_s.reshape(n, nq, kvh, n_top * SEL_BLOCK, hd))
    return o_c.reshape(n, nq, h, hd), o_s.reshape(n, nq, h, hd)


def map_query_blocks(fn, block, q, qpos):
    n, t = q.shape[:2]
    nb = t // block
    qs = jnp.moveaxis(q.reshape(n, nb, block, *q.shape[2:]), 1, 0)
    ps = jnp.moveaxis(qpos.reshape(qpos.shape[0], nb, block), 1, 0)
    o_c, o_s = lax.map(lambda a: fn(a[0], a[1]), (qs, ps))
    back = lambda o: jnp.moveaxis(o, 0, 1).reshape(n, t, *o.shape[3:])
    return back(o_c), back(o_s)


def nsa_output(x, gates, o_c, o_s, o_w, w_out):
    o = gates[:, :, 0, :, None] * o_c + gates[:, :, 1, :, None] * o_s + gates[:, :, 2, :, None] * o_w
    return x + o.astype(x.dtype).reshape(*x.shape[:2], -1) @ w_out


def nsa_prompt(x, rel_bias, attn_norm, w_in, q_norm, k_norm, cmp_pe, cmp_w1, cmp_w2, w_out):
    b, s, _ = x.shape
    q, kc, vc, ks, vs, kw, vw, gates = nsa_project(x, attn_norm, w_in, q_norm, k_norm)
    ck, cv, c_end = compress(kc, vc, cmp_pe, cmp_w1, cmp_w2, k_norm[0])
    kb, vb = to_sel_blocks(ks), to_sel_blocks(vs)
    sel_map = selection_map(c_end, kb.shape[2])
    fn = lambda qq, pp: nsa_cmp_sel_block(qq, pp, ck, cv, c_end, sel_map, kb, vb, rel_bias)
    o_c, o_s = map_query_blocks(fn, min(SEL_Q_BLOCK, s), q, jnp.arange(s)[None])
    o_w = banded_attention(q, kw, vw, rel_bias, NSA_WINDOW, 1)[0]
    y = nsa_output(x, gates, o_c, o_s, o_w, w_out)
    rows = jnp.stack([kc, vc, ks, vs], axis=2)
    win = jnp.stack([kw, vw], axis=2)[:, -min(NSA_WINDOW, s):]
    return y, rows, win


def nsa_sample(x, cache_l, page_table, win_buf, rel_bias, attn_norm, w_in, q_norm, k_norm, cmp_pe, cmp_w1, cmp_w2, w_out):
    db, t, _ = x.shape
    q, kc, vc, ks, vs, kw, vw, gates = nsa_project(x, attn_norm, w_in, q_norm, k_norm)
    past = cache_l[page_table]
    past_len = past.shape[1] * past.shape[2]
    past = past.reshape(db, past_len, *past.shape[3:])
    cat = lambda i, new: jnp.concatenate([past[:, :, i], new], axis=1)
    ck, cv, c_end = compress(cat(0, kc), cat(1, vc), cmp_pe, cmp_w1, cmp_w2, k_norm[0])
    kb, vb = to_sel_blocks(cat(2, ks)), to_sel_blocks(cat(3, vs))
    sel_map = selection_map(c_end, kb.shape[2])
    fn = lambda qq, pp: nsa_cmp_sel_block(qq, pp, ck, cv, c_end, sel_map, kb, vb, rel_bias)
    o_c, o_s = map_query_blocks(fn, 1, q, (past_len + jnp.arange(t))[None])
    wb = win_buf.shape[1]
    o_w = window_sample(q, jnp.concatenate([win_buf[:, :, 0], kw], axis=1),
                        jnp.concatenate([win_buf[:, :, 1], vw], axis=1), wb, rel_bias, NSA_WINDOW)
    y = nsa_output(x, gates, o_c, o_s, o_w, w_out)
    rows = jnp.stack([kc, vc, ks, vs], axis=2)
    win = jnp.concatenate([win_buf, jnp.stack([kw, vw], axis=2)], axis=1)[:, -min(NSA_WINDOW, wb + t):]
    return y, rows, win


def shared_kv(x, kv_norm, w_kv, k_norm):
    b, t, _ = x.shape
    kv = (rms_norm(x, kv_norm) @ w_kv).reshape(b, t, 2, DIL_KV_HEADS, HEAD_DIM)
    return rms_norm(kv[:, :, 0], k_norm), kv[:, :, 1]


def dil_queries(x, attn_norm, w_q, q_norm):
    b, t, _ = x.shape
    q = (rms_norm(x, attn_norm) @ w_q).reshape(b, t, N_DIL_GROUPS, N_HEADS, HEAD_DIM)
    return rms_norm(q, q_norm[:, None, :]) * ATTN_SCALE


def dilated_group_prompt(q, k, v, rel_bias, window, dil):
    b, s = q.shape[:2]
    L = s // dil
    Lp = -(-L // BAND_BLOCK) * BAND_BLOCK

    def split(t):
        t = t.reshape(b, L, dil, *t.shape[2:]).swapaxes(1, 2).reshape(b * dil, L, *t.shape[2:])
        return jnp.pad(t, ((0, 0), (0, Lp - L), (0, 0), (0, 0)))

    def merge(t):
        t = t[:, :L].reshape(b, dil, L, *t.shape[2:]).swapaxes(1, 2)
        return t.reshape(b, s, *t.shape[3:])

    o, m, den = banded_attention(split(q), split(k), split(v), rel_bias, window // dil, dil)
    return merge(o), merge(m), merge(den)


def dilated_group_sample(q, k_all, v_all, buf_len, rel_bias, window, dil):
    db, t, h, hd = q.shape
    kvh = k_all.shape[2]
    rep = h // kvh
    n_taps = window // dil + 1
    dist = dil * jnp.arange(n_taps)
    idx = buf_len + jnp.arange(t)[:, None] - dist[None, :]
    valid = idx >= 0
    idx_c = jnp.maximum(idx, 0)
    kg = k_all[:, idx_c]
    vg = v_all[:, idx_c]
    bias = rel_bias_lookup(rel_bias, dist).T.reshape(kvh, rep, n_taps)
    qg = q.reshape(db, t, kvh, rep, hd)
    logits = jnp.einsum('btgrd,btngd->btgrn', qg, kg, preferred_element_type=jnp.float32) + bias
    p, m, den = masked_softmax(logits, valid[None, :, None, None, :])
    o = jnp.einsum('btgrn,btngd->btgrd', p.astype(vg.dtype), vg)
    return o.reshape(db, t, h, hd), m.reshape(db, t, h), den.reshape(db, t, h)


def dil_output(x, outs, w_out):
    ms = jnp.stack([m for _, m, _ in outs])
    mx = jnp.max(ms, axis=0)
    w = jnp.stack([den for _, _, den in outs]) * jnp.exp(ms - mx)
    os_ = jnp.stack([o for o, _, _ in outs])
    o = jnp.sum(w[..., None] * os_, axis=0) / jnp.sum(w, axis=0)[..., None]
    return x + o.astype(x.dtype).reshape(*x.shape[:2], -1) @ w_out


def dil_prompt(x, k, v, rel_bias, attn_norm, w_q, q_norm, w_out):
    q = dil_queries(x, attn_norm, w_q, q_norm)
    outs = [dilated_group_prompt(q[:, :, g], k, v, rel_bias, w, d) for g, (w, d) in enumerate(DIL_GROUPS)]
    return dil_output(x, outs, w_out)


def dil_sample(x, k_all, v_all, buf_len, rel_bias, attn_norm, w_q, q_norm, w_out):
    q = dil_queries(x, attn_norm, w_q, q_norm)
    outs = [dilated_group_sample(q[:, :, g], k_all, v_all, buf_len, rel_bias, w, d) for g, (w, d) in enumerate(DIL_GROUPS)]
    return dil_output(x, outs, w_out)


def setup_inputs(seed: int = 0) -> dict:
    key = jax.random.key(seed)
    ks = jax.random.split(key, 24)
    f32 = jnp.float32
    n_pages = PAST_LEN // PAGE_SIZE
    n_pool = (DEC_BATCH * n_pages * 5) // 4
    win_buf = min(NSA_WINDOW, PAST_LEN)
    dil_buf = min(DIL_WINDOW_MAX, PAST_LEN)
    nrm = lambda k, shape, scale: jax.random.normal(k, shape, f32) * scale
    gain = lambda k, shape: 1.0 + 0.02 * jax.random.normal(k, shape, f32)
    page_table = jax.random.permutation(ks[5], n_pool)[:DEC_BATCH * n_pages].reshape(DEC_BATCH, n_pages).astype(jnp.int32)
    return {
        'x_prompt': nrm(ks[0], (BATCH, SEQ, D_MODEL), 1.0),
        'x_sample': nrm(ks[1], (DEC_BATCH, DEC_SEQ, D_MODEL), 1.0),
        'cache_nsa_kv': nrm(ks[2], (N_A_LAYERS, n_pool, PAGE_SIZE, 4, NSA_KV_HEADS, HEAD_DIM), 1.0),
        'cache_win_kv': nrm(ks[3], (N_A_LAYERS, DEC_BATCH, win_buf, 2, NSA_KV_HEADS, HEAD_DIM), 1.0),
        'cache_dil_kv': nrm(ks[4], (DEC_BATCH, dil_buf, 2, DIL_KV_HEADS, HEAD_DIM), 1.0),
        'page_table': page_table,
        'rel_bias': nrm(ks[6], (N_BUCKETS, N_HEADS), 0.5),
        'a_attn_norm': gain(ks[7], (N_A_LAYERS, D_MODEL)),
        'a_w_in': nrm(ks[8], (N_A_LAYERS, D_MODEL, NSA_IN), D_MODEL ** -0.5),
        'a_q_norm': gain(ks[9], (N_A_LAYERS, HEAD_DIM)),
        'a_k_norm': gain(ks[10], (N_A_LAYERS, N_BRANCHES, HEAD_DIM)),
        'a_cmp_pe': nrm(ks[11], (N_A_LAYERS, 2, CMP_LEN, HEAD_DIM), 0.1),
        'a_cmp_w1': nrm(ks[12], (N_A_LAYERS, 2, CMP_LEN, HEAD_DIM, CMP_HIDDEN), (CMP_LEN * HEAD_DIM) ** -0.5),
        'a_cmp_w2': nrm(ks[13], (N_A_LAYERS, 2, CMP_HIDDEN, HEAD_DIM), CMP_HIDDEN ** -0.5),
        'a_w_out': nrm(ks[14], (N_A_LAYERS, N_HEADS * HEAD_DIM, D_MODEL), (N_HEADS * HEAD_DIM) ** -0.5),
        'kv_norm': gain(ks[15], (D_MODEL,)),
        'w_kv_shared': nrm(ks[16], (D_MODEL, 2 * DIL_KV_HEADS * HEAD_DIM), D_MODEL ** -0.5),
        'k_norm_shared': gain(ks[17], (HEAD_DIM,)),
        'b_attn_norm': gain(ks[18], (N_B_LAYERS, D_MODEL)),
        'b_w_q': nrm(ks[19], (N_B_LAYERS, D_MODEL, N_DIL_GROUPS * N_HEADS * HEAD_DIM), D_MODEL ** -0.5),
        'b_q_norm': gain(ks[20], (N_B_LAYERS, N_DIL_GROUPS, HEAD_DIM)),
        'b_w_out': nrm(ks[21], (N_B_LAYERS, N_HEADS * HEAD_DIM, D_MODEL), (N_HEADS * HEAD_DIM) ** -0.5),
        'mlp_norm': gain(ks[22], (DEPTH, D_MODEL)),
        'mlp_w1': nrm(ks[23], (DEPTH, D_MODEL, D_FF), D_MODEL ** -0.5),
        'mlp_w2': nrm(jax.random.fold_in(key, 99), (DEPTH, D_FF, D_MODEL), D_FF ** -0.5),
    }


def reference(x_prompt, x_sample, cache_nsa_kv, cache_win_kv, cache_dil_kv, page_table, rel_bias,
              a_attn_norm, a_w_in, a_q_norm, a_k_norm, a_cmp_pe, a_cmp_w1, a_cmp_w2, a_w_out,
              kv_norm, w_kv_shared, k_norm_shared, b_attn_norm, b_w_q, b_q_norm, b_w_out,
              mlp_norm, mlp_w1, mlp_w2):
    yp, ys = x_prompt, x_sample
    rows_p, rows_s, wins_p, wins_s = [], [], [], []
    for l in range(DEPTH):
        if l < N_A_LAYERS:
            params = (a_attn_norm[l], a_w_in[l], a_q_norm[l], a_k_norm[l],
                      a_cmp_pe[l], a_cmp_w1[l], a_cmp_w2[l], a_w_out[l])
            yp, r, w = nsa_prompt(yp, rel_bias, *params)
            rows_p.append(r)
            wins_p.append(w)
            ys, r, w = nsa_sample(ys, cache_nsa_kv[l], page_table, cache_win_kv[l], rel_bias, *params)
            rows_s.append(r)
            wins_s.append(w)
        else:
            if l == N_A_LAYERS:
                kp, vp = shared_kv(yp, kv_norm, w_kv_shared, k_norm_shared)
                kn, vn = shared_kv(ys, kv_norm, w_kv_shared, k_norm_shared)
                buf_len = cache_dil_kv.shape[1]
                k_all = jnp.concatenate([cache_dil_kv[:, :, 0], kn], axis=1)
                v_all = jnp.concatenate([cache_dil_kv[:, :, 1], vn], axis=1)
                new_dil_kv_prompt = jnp.stack([kp, vp], axis=2)[:, -min(DIL_WINDOW_MAX, kp.shape[1]):]
                new_dil_kv_sample = jnp.concatenate([cache_dil_kv, jnp.stack([kn, vn], axis=2)], axis=1)[:, -min(DIL_WINDOW_MAX, buf_len + kn.shape[1]):]
            i = l - N_A_LAYERS
            params = (b_attn_norm[i], b_w_q[i], b_q_norm[i], b_w_out[i])
            yp = dil_prompt(yp, kp, vp, rel_bias, *params)
            ys = dil_sample(ys, k_all, v_all, buf_len, rel_bias, *params)
        yp = sq_relu_mlp(yp, mlp_norm[l], mlp_w1[l], mlp_w2[l])
        ys = sq_relu_mlp(ys, mlp_norm[l], mlp_w1[l], mlp_w2[l])
    new_nsa_kv_prompt = jnp.stack(rows_p)
    new_nsa_kv_sample = jnp.stack(rows_s)
    new_win_kv_prompt = jnp.stack(wins_p)
    new_win_kv_sample = jnp.stack(wins_s)
    return (yp, ys, new_nsa_kv_prompt, new_nsa_kv_sample, new_win_kv_prompt, new_win_kv_sample, new_dil_kv_prompt, new_dil_kv_sample)
```

```python
import math
import numpy as np
import ml_dtypes
import concourse.bass as bass
import concourse.mybir as mybir
from concourse.bass_utils import run_bass_kernel_spmd

F32 = mybir.dt.float32
BF16 = mybir.dt.bfloat16
I32 = mybir.dt.int32
AF = mybir.ActivationFunctionType
ALU = mybir.AluOpType
AX = mybir.AxisListType

D = 1024
NH = 16
HD = 64
KVH = 4
NSA_IN = 2608
DFF = 4096
EPS = 1e-6
NEG = -30000.0


class Buf:
    __slots__ = ("w", "r")

    def __init__(self):
        self.w = {}
        self.r = {}


class Prog:
    def __init__(self, nc):
        self.nc = nc
        self.eng = {"pe": nc.tensor, "act": nc.scalar, "dve": nc.vector, "pool": nc.gpsimd, "sp": nc.sync}
        self.sems = {}
        self.cnt = {}
        self.waited = {e: {} for e in self.eng}
        self._stack = []
        for e in ("pe", "act", "dve", "pool"):
            self.sems[e] = self._sem("s_" + e)
            self.cnt[e] = 0
        self.dslots = {"sp": 10, "pool": 6, "act": 4}
        self.dnext = {k: 0 for k in self.dslots}
        for q, n in self.dslots.items():
            for i in range(n):
                k = ("d", q, i)
                self.sems[k] = self._sem("d_%s%d" % (q, i))
                self.cnt[k] = 0
        self.n_ins = 0

    def _sem(self, name):
        cm = self.nc.semaphore(name)
        s = cm.__enter__()
        self._stack.append(cm)
        return s

    def close(self):
        for cm in reversed(self._stack):
            cm.__exit__(None, None, None)

    def _wait(self, e, deps):
        w = self.waited[e]
        for k, v in deps.items():
            if k == e and e == "pe":
                continue
            if w.get(k, 0) >= v:
                continue
            self.eng[e].wait_ge(self.sems[k], v)
            w[k] = v

    @staticmethod
    def _merge(d, s):
        for k, v in s.items():
            if d.get(k, 0) < v:
                d[k] = v

    def _deps(self, reads, writes, pw=()):
        deps = {}
        for b in reads:
            self._merge(deps, b.w)
        for b in writes:
            self._merge(deps, b.w)
            self._merge(deps, b.r)
        for b in pw:
            self._merge(deps, b.w)
            self._merge(deps, b.r)
        return deps

    def _commit(self, key, val, reads, writes, pw=()):
        for b in reads:
            if b.r.get(key, 0) < val:
                b.r[key] = val
        for b in writes:
            b.w = {key: val}
            b.r = {}
        for b in pw:
            b.w[key] = val
            b.r = {}

    def op(self, e, fn, reads=(), writes=(), pw=()):
        deps = self._deps(reads, writes, pw)
        self._wait(e, deps)
        ins = fn(self.eng[e])
        self.cnt[e] += 1
        ins.then_inc(self.sems[e], 1)
        self._commit(e, self.cnt[e], reads, writes, pw)
        self.n_ins += 1
        return ins

    def dma(self, out, in_, reads=(), writes=(), q="sp", pw=(), **kw):
        slot = self.dnext[q]
        self.dnext[q] = (slot + 1) % self.dslots[q]
        key = ("d", q, slot)
        deps = self._deps(reads, writes, pw)
        if self.cnt[key] > 0:
            deps[key] = max(deps.get(key, 0), self.cnt[key])
        self._wait(q, deps)
        ins = self.eng[q].dma_start(out=out, in_=in_, **kw)
        self.cnt[key] += 16
        ins.then_inc(self.sems[key], 16)
        self._commit(key, self.cnt[key], reads, writes, pw)
        self.n_ins += 1
        return ins

    def dma_custom(self, q, fn, reads=(), writes=(), pw=()):
        slot = self.dnext[q]
        self.dnext[q] = (slot + 1) % self.dslots[q]
        key = ("d", q, slot)
        deps = self._deps(reads, writes, pw)
        if self.cnt[key] > 0:
            deps[key] = max(deps.get(key, 0), self.cnt[key])
        self._wait(q, deps)
        ins = fn(self.eng[q])
        self.cnt[key] += 16
        ins.then_inc(self.sems[key], 16)
        self._commit(key, self.cnt[key], reads, writes, pw)
        self.n_ins += 1
        return ins

    def all_tokens(self):
        d = {}
        for k, v in self.cnt.items():
            if v > 0:
                d[k] = v
        return d

    def barrier(self):
        d = self.all_tokens()
        for e in self.eng:
            self._wait(e, dict(d))


class Alloc:
    def __init__(self, nc):
        self.nc = nc
        self.stack = []
        self.i = 0

    def sb(self, shape, dt, name=None):
        self.i += 1
        cm = self.nc.sbuf_tensor("%s_%d" % (name or "sb", self.i), list(shape), dt)
        t = cm.__enter__()
        self.stack.append(cm)
        return t

    def ps(self, shape, dt, name=None):
        self.i += 1
        cm = self.nc.psum_tensor("%s_%d" % (name or "ps", self.i), list(shape), dt)
        t = cm.__enter__()
        self.stack.append(cm)
        return t

    def mark(self):
        return len(self.stack)

    def release(self, m):
        while len(self.stack) > m:
            self.stack.pop().__exit__(None, None, None)


def rel_bucket_np(d):
    d = np.maximum(np.asarray(d, np.int64), 0)
    ratio = (np.log(np.maximum(d, 1).astype(np.float32) / np.float32(16)) / np.float32(math.log(2048 / 16))).astype(np.float32)
    large = 16 + (ratio * np.float32(16)).astype(np.int32)
    return np.where(d < 16, d, np.minimum(large, 31))


class K:
    def __init__(self, S, NSEQ):
        self.S = S
        self.NSEQ = NSEQ
        self.nc = bass.Bass("TRN2", target_bir_lowering=False)
        self.P = Prog(self.nc)
        self.A = Alloc(self.nc)
        self.dram = {}

    def din(self, name, shape, dt=F32):
        t = self.nc.dram_tensor(name, list(shape), dt, kind="ExternalInput").ap()
        self.dram[name] = t
        return t

    def dout(self, name, shape, dt=F32):
        t = self.nc.dram_tensor(name, list(shape), dt, kind="ExternalOutput").ap()
        self.dram[name] = t
        return t

    def dscr(self, name, shape, dt=F32):
        t = self.nc.dram_tensor(name, list(shape), dt, kind="Internal").ap()
        self.dram[name] = t
        return t


def load_consts(k):
    P, A, nc = k.P, k.A, k.nc
    c = {}
    identf = A.sb([128, 128], F32, "identf")
    ident = A.sb([128, 128], BF16, "ident")
    b = Buf()
    P.op("pool", lambda e: e.memset(identf[:], 0.0), writes=[b])
    P.op("pool", lambda e: e.affine_select(out=identf[:], in_=identf[:], pattern=[[-1, 128]], compare_op=ALU.not_equal,
                                           fill=1.0, base=0, channel_multiplier=1), reads=[b], writes=[b])
    P.op("dve", lambda e: e.tensor_copy(out=ident[:], in_=identf[:]), reads=[b], writes=[b])
    c["ident"] = ident
    c["identf"] = identf
    c["ident_b"] = b
    return c


def load_weight_bf16(k, w_ap, rows, cols, gcol=None, gbuf=None, name="w"):
    P, A = k.P, k.A
    nk = rows // 128
    wt = A.sb([128, nk, cols], BF16, name)
    wb = Buf()
    CH = 2048
    m0 = A.mark()
    stg = [A.sb([128, min(CH, cols)], F32, name + "_stg") for _ in range(2)]
    sb = [Buf(), Buf()]
    i = 0
    for kc in range(nk):
        for c0 in range(0, cols, CH):
            cw = min(CH, cols - c0)
            s, b = stg[i % 2], sb[i % 2]
            P.dma(s[:, 0:cw], w_ap[kc * 128:(kc + 1) * 128, c0:c0 + cw], writes=[b], q="sp" if i % 2 == 0 else "act")
            if gcol is not None:
                P.op("dve" if i % 2 == 0 else "pool",
                     lambda e, s=s, cw=cw, kc=kc, c0=c0: e.tensor_scalar(out=wt[:, kc, c0:c0 + cw], in0=s[:, 0:cw],
                                                                         scalar1=gcol[:, kc:kc + 1], scalar2=None, op0=ALU.mult),
                     reads=[b] + ([gbuf] if gbuf else []), pw=[wb])
            else:
                P.op("dve" if i % 2 == 0 else "pool",
                     lambda e, s=s, cw=cw, kc=kc, c0=c0: e.tensor_copy(out=wt[:, kc, c0:c0 + cw], in_=s[:, 0:cw]),
                     reads=[b], pw=[wb])
            i += 1
    P.barrier()
    A.release(m0)
    return wt, wb


def load_gcol(k, g_ap, n, name="g"):
    P, A = k.P, k.A
    t = A.sb([128, n // 128], F32, name)
    b = Buf()
    P.dma(t[:], g_ap.rearrange("(kc p) -> p kc", p=128), writes=[b], allow_slow_non_contiguous=True)
    return t, b


def load_bcast(k, v_ap, n, name="bc"):
    P, A = k.P, k.A
    t = A.sb([128, n], F32, name)
    b = Buf()
    src = bass.AP(tensor=v_ap.tensor, offset=v_ap.offset, ap=[[0, 128], [1, n]])
    P.dma(t[:], src, writes=[b])
    return t, b


def rms_rstd(P, ss, rstd, n, bufs_r, bufs_w, eng="dve", scale=1.0):
    P.op(eng, lambda e: e.tensor_scalar(out=rstd, in0=ss, scalar1=1.0 / n, scalar2=EPS, op0=ALU.mult, op1=ALU.add),
         reads=bufs_r, writes=bufs_w)
    P.op("act", lambda e: e.sqrt(out=rstd, in_=rstd), reads=bufs_w, writes=bufs_w)
    P.op(eng, lambda e: e.reciprocal(out=rstd, in_=rstd), reads=bufs_w, writes=bufs_w)
    if scale != 1.0:
        P.op(eng, lambda e: e.tensor_scalar(out=rstd, in0=rstd, scalar1=scale, scalar2=None, op0=ALU.mult),
             reads=bufs_w, writes=bufs_w)


def nsa_project_phase(k, c, x_ap, T, w, dst):
    P, A = k.P, k.A
    m0 = A.mark()
    ident, ib = c["ident"], c["ident_b"]
    gcol, gb = load_gcol(k, w["attn_norm"], D, "gcol")
    wt, wb = load_weight_bf16(k, w["w_in"], D, NSA_IN, gcol, gb, "w_in")
    qg, qgb = load_bcast(k, w["q_norm"], HD, "qg")
    kg1, kg1b = load_bcast(k, w["k_norm1"], HD, "kg1")
    kg2, kg2b = load_bcast(k, w["k_norm2"], HD, "kg2")
    NB = 2
    xt = [A.sb([128, D], F32, "xt") for _ in range(NB)]
    xn = [A.sb([128, D], BF16, "xn") for _ in range(NB)]
    junk = A.sb([128, D], BF16, "junk")
    xnT = [A.sb([128, D], BF16, "xnT") for _ in range(NB)]
    z = [A.sb([128, NSA_IN], F32, "z") for _ in range(NB)]
    sq = A.sb([128, D], F32, "sq")
    qb = [A.sb([128, D], BF16, "qb") for _ in range(NB)]
    kb = [A.sb([128, 6 * 256], BF16, "kb") for _ in range(NB)]
    st = [A.sb([128, 40], F32, "st") for _ in range(NB)]
    gat = [A.sb([128, 48], F32, "gat") for _ in range(NB)]
    qTs = [A.sb([64, 16, 128], BF16, "qTs") for _ in range(NB)]
    kTs = [A.sb([64, 16, 128], BF16, "kTs") for _ in range(NB)]
    ps_t = A.ps([128, D], BF16, "ps_t")
    ps_z = [A.ps([128, 512], F32, "ps_z") for _ in range(2)]
    ps_q = [A.ps([64, 8, 128], BF16, "ps_q") for _ in range(2)]
    ps_k = [A.ps([64, 8, 128], BF16, "ps_k") for _ in range(2)]
    B = lambda: [Buf() for _ in range(NB)]
    b_xt, b_xn, b_xnT, b_z, b_qb, b_kb, b_st, b_gat, b_qTs, b_kTs = B(), B(), B(), B(), B(), B(), B(), B(), B(), B()
    b_junk, b_sq, b_pst = Buf(), Buf(), Buf()
    b_psz = [Buf(), Buf()]
    b_psq = [Buf(), Buf()]
    b_psk = [Buf(), Buf()]
    nt = T // 128
    zc = 0
    for t in range(nt):
        i = t % NB
        r0 = t * 128
        P.dma(xt[i][:], x_ap[r0:r0 + 128, :], writes=[b_xt[i]])
        P.op("act", lambda e: e.activation(out=junk[:], in_=xt[i][:], func=AF.Square, accum_out=st[i][:, 0:1]),
             reads=[b_xt[i]], writes=[b_junk, b_st[i]])
        rms_rstd(P, st[i][:, 0:1], st[i][:, 1:2], D, [b_st[i]], [b_st[i]])
        P.op("dve", lambda e: e.tensor_scalar(out=xn[i][:], in0=xt[i][:], scalar1=st[i][:, 1:2], scalar2=None, op0=ALU.mult),
             reads=[b_xt[i], b_st[i]], writes=[b_xn[i]])
        for kc in range(8):
            P.op("pe", lambda e, kc=kc: e.transpose(out=ps_t[:, kc * 128:(kc + 1) * 128], in_=xn[i][:, kc * 128:(kc + 1) * 128], identity=ident[:]),
                 reads=[b_xn[i], ib], pw=[b_pst] if kc else (), writes=() if kc else [b_pst])
        P.op("act", lambda e: e.copy(out=xnT[i][:], in_=ps_t[:]), reads=[b_pst], writes=[b_xnT[i]])
        for c0 in range(0, NSA_IN, 512):
            cw = min(512, NSA_IN - c0)
            pz, bz = ps_z[zc % 2], b_psz[zc % 2]
            zc += 1
            for kc in range(8):
                P.op("pe", lambda e, kc=kc, c0=c0, cw=cw, pz=pz: e.matmul(pz[:, 0:cw], lhsT=xnT[i][:, kc * 128:(kc + 1) * 128],
                                                                        rhs=wt[:, kc, c0:c0 + cw], start=(kc == 0), stop=(kc == 7)),
                     reads=[b_xnT[i], wb], pw=[bz] if kc else (), writes=() if kc else [bz])
            P.op("dve" if (c0 // 512) % 2 == 0 else "act",
                 (lambda e, c0=c0, cw=cw, pz=pz: e.tensor_copy(out=z[i][:, c0:c0 + cw], in_=pz[:, 0:cw])) if (c0 // 512) % 2 == 0 else
                 (lambda e, c0=c0, cw=cw, pz=pz: e.copy(out=z[i][:, c0:c0 + cw], in_=pz[:, 0:cw])),
                 reads=[bz], pw=[b_z[i]] if c0 else (), writes=() if c0 else [b_z[i]])
        zi = z[i]
        P.op("pool", lambda e: e.tensor_tensor(out=sq[:], in0=zi[:, 0:1024], in1=zi[:, 0:1024], op=ALU.mult), reads=[b_z[i]], writes=[b_sq])
        P.op("dve", lambda e: e.tensor_reduce(out=st[i][:, 2:18], in_=sq[:].rearrange("p (h d) -> p h d", d=64), axis=AX.X, op=ALU.add),
             reads=[b_sq], pw=[b_st[i]])
        P.op("pool", lambda e: e.tensor_tensor(out=sq[:, 0:256], in0=zi[:, 1536:1792], in1=zi[:, 1536:1792], op=ALU.mult), reads=[b_z[i]], writes=[b_sq])
        P.op("pool", lambda e: e.tensor_tensor(out=sq[:, 256:512], in0=zi[:, 2048:2304], in1=zi[:, 2048:2304], op=ALU.mult), reads=[b_z[i]], pw=[b_sq])
        P.op("dve", lambda e: e.tensor_reduce(out=st[i][:, 18:26], in_=sq[:, 0:512].rearrange("p (h d) -> p h d", d=64), axis=AX.X, op=ALU.add),
             reads=[b_sq], pw=[b_st[i]])
        rms_rstd(P, st[i][:, 2:18], st[i][:, 2:18], HD, [b_st[i]], [b_st[i]], scale=HD ** -0.5)
        rms_rstd(P, st[i][:, 18:26], st[i][:, 18:26], HD, [b_st[i]], [b_st[i]])
        P.op("dve", lambda e: e.tensor_tensor(out=sq[:].rearrange("p (h d) -> p h d", d=64), in0=zi[:, 0:1024].rearrange("p (h d) -> p h d", d=64),
                                              in1=st[i][:, 2:18].unsqueeze(2).to_broadcast([128, 16, 64]), op=ALU.mult),
             reads=[b_z[i], b_st[i]], writes=[b_sq])
        P.op("pool", lambda e: e.tensor_tensor(out=qb[i][:].rearrange("p (h d) -> p h d", d=64), in0=sq[:].rearrange("p (h d) -> p h d", d=64),
                                               in1=qg[:].unsqueeze(1).to_broadcast([128, 16, 64]), op=ALU.mult),
             reads=[b_sq, qgb], writes=[b_qb[i]])
        for (c0, sc, g_t, g_b) in ((1536, 18, kg1, kg1b), (2048, 22, kg2, kg2b)):
            P.op("dve", lambda e, c0=c0, sc=sc: e.tensor_tensor(out=zi[:, c0:c0 + 256].rearrange("p (h d) -> p h d", d=64),
                                                                in0=zi[:, c0:c0 + 256].rearrange("p (h d) -> p h d", d=64),
                                                                in1=st[i][:, sc:sc + 4].unsqueeze(2).to_broadcast([128, 4, 64]), op=ALU.mult),
                 reads=[b_st[i]], writes=[b_z[i]])
            P.op("dve", lambda e, c0=c0, g_t=g_t: e.tensor_tensor(out=zi[:, c0:c0 + 256].rearrange("p (h d) -> p h d", d=64),
                                                                  in0=zi[:, c0:c0 + 256].rearrange("p (h d) -> p h d", d=64),
                                                                  in1=g_t[:].unsqueeze(1).to_broadcast([128, 4, 64]), op=ALU.mult),
                 reads=[g_b], writes=[b_z[i]])
        P.op("act", lambda e: e.activation(out=gat[i][:], in_=zi[:, 2560:2608], func=AF.Sigmoid), reads=[b_z[i]], writes=[b_gat[i]])
        P.op("act", lambda e: e.copy(out=kb[i][:], in_=zi[:, 1024:2560]), reads=[b_z[i]], writes=[b_kb[i]])
        db = dst["bufs"]
        P.dma(dst["rows"][r0:r0 + 128, :], zi[:, 1024:2048], reads=[b_z[i]], writes=[db["rows"][t]], q="pool")
        for (ti, ap, lo, hi) in dst.get("win", []):
            if ti == t:
                P.dma(ap, zi[lo:hi, 2048:2560], reads=[b_z[i]], pw=[db["win"][t]], q="pool")
        P.dma(dst["gates"][r0:r0 + 128, :], gat[i][:], reads=[b_gat[i]], writes=[db["gates"][t]], q="pool")
        P.dma(dst["vs"][r0:r0 + 128, :], kb[i][:, 768:1024], reads=[b_kb[i]], writes=[db["vs"][t]], q="pool")
        P.dma(dst["vw"][r0:r0 + 128, :], kb[i][:, 1280:1536], reads=[b_kb[i]], writes=[db["vw"][t]], q="pool")
        for hh in range(2):
            for h8 in range(8):
                h = hh * 8 + h8
                P.op("pe", lambda e, h=h, h8=h8, hh=hh: e.transpose(out=ps_q[hh][:, h8, :], in_=qb[i][:, h * 64:(h + 1) * 64], identity=ident[:]),
                     reads=[b_qb[i], ib], pw=[b_psq[hh]] if h8 else (), writes=() if h8 else [b_psq[hh]])
            P.op("dve" if hh == 0 else "act",
                 (lambda e, hh=hh: e.tensor_copy(out=qTs[i][:, hh * 8:(hh + 1) * 8, :], in_=ps_q[hh][:])) if hh == 0 else
                 (lambda e, hh=hh: e.copy(out=qTs[i][:, hh * 8:(hh + 1) * 8, :], in_=ps_q[hh][:])),
                 reads=[b_psq[hh]], pw=[b_qTs[i]] if hh else (), writes=() if hh else [b_qTs[i]])
        P.dma(dst["qT"][:, :, r0:r0 + 128].rearrange("h d t -> d h t"), qTs[i][:], reads=[b_qTs[i]], writes=[db["qT"][t]], q="sp")
        srcs = [0, 256, 512, 1024]
        for hh in range(2):
            for h8 in range(8):
                j = hh * 8 + h8
                off = srcs[j // 4] + (j % 4) * 64
                P.op("pe", lambda e, off=off, h8=h8, hh=hh: e.transpose(out=ps_k[hh][:, h8, :], in_=kb[i][:, off:off + 64], identity=ident[:]),
                     reads=[b_kb[i], ib], pw=[b_psk[hh]] if h8 else (), writes=() if h8 else [b_psk[hh]])
            P.op("dve" if hh == 0 else "act",
                 (lambda e, hh=hh: e.tensor_copy(out=kTs[i][:, hh * 8:(hh + 1) * 8, :], in_=ps_k[hh][:])) if hh == 0 else
                 (lambda e, hh=hh: e.copy(out=kTs[i][:, hh * 8:(hh + 1) * 8, :], in_=ps_k[hh][:])),
                 reads=[b_psk[hh]], pw=[b_kTs[i]] if hh else (), writes=() if hh else [b_kTs[i]])
        for j, nm in enumerate(("kcT", "vcT", "ksT", "kwT")):
            P.dma(dst[nm][:, :, r0:r0 + 128].rearrange("h d t -> d h t"), kTs[i][:, j * 4:(j + 1) * 4, :], reads=[b_kTs[i]], writes=[db[nm][t]], q="sp")
    P.barrier()
    A.release(m0)


N_CORES = 8
S_FULL = 8192
NSEQ = 16


def nsa_weights(k, l):
    g = k.dram
    return {"attn_norm": g["a_attn_norm"][l], "w_in": g["a_w_in"][l], "q_norm": g["a_q_norm"][l],
            "k_norm0": g["a_k_norm"][l, 0], "k_norm1": g["a_k_norm"][l, 1], "k_norm2": g["a_k_norm"][l, 2],
            "cmp_pe": g["a_cmp_pe"][l], "cmp_w1": g["a_cmp_w1"][l], "cmp_w2": g["a_cmp_w2"][l], "w_out": g["a_w_out"][l]}


def make_scratch(k, pfx, T):
    d = {"qT": k.dscr(pfx + "qT", [16, 64, T], BF16), "vs": k.dscr(pfx + "vs", [T, 256], BF16), "vw": k.dscr(pfx + "vw", [T, 256], BF16),
         "gates": k.dscr(pfx + "gates", [T, 48], F32)}
    for nm in ("kcT", "vcT", "ksT", "kwT"):
        d[nm] = k.dscr(pfx + nm, [4, 64, T], BF16)
    return d


def fresh_bufs(T):
    nt = T // 128
    return {nm: [Buf() for _ in range(nt)] for nm in ("rows", "gates", "qT", "vs", "vw", "kcT", "vcT", "ksT", "kwT", "win")}


def build(S=S_FULL, stage=5, npool=2560):
    k = K(S, NSEQ)
    P, A = k.P, k.A
    TS = NSEQ * 8
    ncmp = S // 16 - 1
    nct = (ncmp + 127) // 128
    k.din("xp", [S, D])
    k.din("xs", [TS, D])
    k.din("cache_win", [2, NSEQ, 512, 512])
    k.din("cache_dil", [NSEQ, 2048, 512])
    k.din("rel_bias", [32, 16])
    k.din("a_attn_norm", [2, D]); k.din("a_w_in", [2, D, NSA_IN]); k.din("a_q_norm", [2, HD]); k.din("a_k_norm", [2, 3, HD])
    k.din("a_cmp_pe", [2, 2, 32, HD]); k.din("a_cmp_w1", [2, 2, 32 * HD, 128]); k.din("a_cmp_w2", [2, 2, 128, HD]); k.din("a_w_out", [2, D, D])
    k.din("mlp_norm", [4, D]); k.din("mlp_w1", [4, D, DFF]); k.din("mlp_w2", [4, DFF, D])
    k.din("kv_norm", [D]); k.din("w_kv_shared", [D, 512]); k.din("k_norm_shared", [HD])
    names = ["causal", "win", "cmp", "dil1", "dil4", "dil16", "sd1", "sd4", "sd16"]
    for nm in names:
        k.din("oh_" + nm, list(onehot_np(nm).shape))
    k.din("selmap", [nct * 128, 128])
    k.din("b_attn_norm", [2, D]); k.din("b_w_q", [2, D, 3072]); k.din("b_q_norm", [2, 3, HD]); k.din("b_w_out", [2, D, D])
    if stage >= 3:
        k.din("pool0", [npool * 128, 1024]); k.din("pool1", [npool * 128, 1024]); k.din("pt", [NSEQ * NPG], I32); k.din("piota", [128, 1], I32); k.din("selmap_s", [128, 33])
    yp = k.dout("yp", [S, D]); ys = k.dout("ys", [TS, D])
    rows_p = k.dout("rows_p", [2, S, 1024]); rows_s = k.dout("rows_s", [2, TS, 1024])
    win_p = k.dout("win_p", [2, 512, 512]); win_s = k.dout("win_s", [2, NSEQ, 512, 512])
    dil_p = k.dout("dil_p", [2048, 512]); dil_s = k.dout("dil_s", [NSEQ, 2048, 512])
    c = load_consts(k)
    make_antiident(k, c)
    g = k.dram
    cb = Buf()
    for l in range(2):
        for s in range(NSEQ):
            P.dma(win_s[l, s, 0:504, :], g["cache_win"][l, s, 8:512, :], pw=[cb], q="act")
    for s in range(NSEQ):
        P.dma(dil_s[s, 0:2040, :], g["cache_dil"][s, 8:2048, :], pw=[cb], q="act")
    ftabs = build_ftabs(k, c, g["rel_bias"], names)
    W, W_b = make_W(k, c, max(S, 2048))
    scr_p = make_scratch(k, "p_", S)
    scr_s = make_scratch(k, "s_", TS)
    oc = k.dscr("oc", [S, D], BF16); osw = k.dscr("osw", [S, D], BF16)
    xa = k.dscr("xa", [S, D]); xb = k.dscr("xb", [S, D])
    xp_cur, xs_cur = g["xp"], g["xs"]
    xp_bufs = []
    if stage >= 3:
        ball, ball_b = build_sample_bias(k, c, ftabs)
        idx, idx_b = build_page_idx(k, c, g["pt"], NSEQ)
        mk = k.dout if DEBUG_SAMPLE else k.dscr
        so_c = mk("so_c", [TS, D]); so_s = mk("so_s", [TS, D]); so_w = mk("so_w", [TS, D])
        sog = k.dscr("sog", [TS, D], BF16)
        xsa = k.dscr("xsa", [TS, D]); xsb = k.dscr("xsb", [TS, D])
    for l in range(2):
        w = nsa_weights(k, l)
        dst = dict(scr_p)
        dst["rows"] = rows_p[l]
        nt = S // 128
        nwin = min(4, nt)
        dst["win"] = [(nt - nwin + j, win_p[l, (4 - nwin + j) * 128:(4 - nwin + j + 1) * 128, :], 0, 128) for j in range(nwin)]
        dst["bufs"] = fresh_bufs(S)
        nsa_project_phase(k, c, xp_cur, S, w, dst)
        dsts = dict(scr_s)
        dsts["rows"] = rows_s[l]
        dsts["win"] = [(0, win_s[l, s_, 504:512, :], s_ * 8, s_ * 8 + 8) for s_ in range(NSEQ)]
        dsts["bufs"] = fresh_bufs(TS)
        nsa_project_phase(k, c, xs_cur, TS, w, dsts)
        if stage < 2:
            break
        dst["bufs_all"] = []
        nsa_attention(k, c, ftabs, w, dst, S, g["selmap"], oc, osw, W, W_b)
        outproj_phase(k, c, xp_cur, [oc, osw], [], w["w_out"], xa, S)
        mlp_phase(k, c, xa, [], g["mlp_norm"][l], g["mlp_w1"][l], g["mlp_w2"][l], [xb] if l == 0 else [xa, yp], S)
        xp_cur = xb if l == 0 else xa
        if stage >= 3:
            dsts["bufs_all"] = []
            sample_nsa_attention(k, c, w, dsts, NSEQ, g["pool%d" % l], g["cache_win"][l], idx, idx_b, ball, ball_b, W, W_b, so_c, so_s, so_w, g["selmap_s"])
            sample_gate_sum(k, c, dsts["gates"], so_c, so_s, so_w, sog)
            outproj_phase(k, c, xs_cur, [sog], [], w["w_out"], xsa, TS, wname="w_out_s")
            mlp_phase(k, c, xsa, [], g["mlp_norm"][l], g["mlp_w1"][l], g["mlp_w2"][l], [xsb] if l == 0 else [xsa, ys], TS)
            xs_cur = xsb if l == 0 else xsa
    if stage >= 3:
        scr_sd = {"vd": k.dscr("s_vd", [TS, 256], BF16), "kdT": k.dscr("s_kdT", [4, 64, TS], BF16)}
        shared_kv_phase(k, c, xs_cur, TS, g["kv_norm"], g["w_kv_shared"], g["k_norm_shared"],
                        lambda t: [(dil_s[s_, 2040:2048, :], s_ * 8, s_ * 8 + 8) for s_ in range(NSEQ)], scr_sd)
    if stage >= 5:
        balld, balld_b, sd_offs = build_sample_dil_bias(k, c, ftabs)
        qdT_s = k.dscr("s_qdT", [3, 16, 64, TS], BF16)
        for i in range(2):
            dil_project_phase(k, c, xs_cur, TS, g["b_attn_norm"][i], g["b_w_q"][i], g["b_q_norm"][i], qdT_s)
            sample_dil_attention(k, c, scr_sd, qdT_s, NSEQ, g["cache_dil"], balld, balld_b, sd_offs, so_c)
            cast_phase(k, so_c, sog, TS)
            x1 = xsb if xs_cur is xsa else xsa
            outproj_phase(k, c, xs_cur, [sog], [], g["b_w_out"][i], x1, TS, wname="b_w_out_s")
            x2 = xs_cur
            mlp_phase(k, c, x1, [], g["mlp_norm"][2 + i], g["mlp_w1"][2 + i], g["mlp_w2"][2 + i], [x2] if i == 0 else [x2, ys], TS)
            xs_cur = x2
    if stage >= 2:
        ntl = S // 128
        nd = min(16, ntl)
        scr_d = {"vd": k.dscr("p_vd", [S, 256], BF16), "kdT": k.dscr("p_kdT", [4, 64, S], BF16)}
        shared_kv_phase(k, c, xp_cur, S, g["kv_norm"], g["w_kv_shared"], g["k_norm_shared"],
                        lambda t: [(dil_p[(16 - nd + t - (ntl - nd)) * 128:(16 - nd + t - (ntl - nd) + 1) * 128, :], 0, 128)] if t >= ntl - nd else [], scr_d)
    if stage >= 4:
        qdT = k.dscr("p_qdT", [3, 16, 64, S], BF16)
        accd = k.dscr("p_accd", [3, 16, S, 65], F32)
        for i in range(2):
            dil_project_phase(k, c, xp_cur, S, g["b_attn_norm"][i], g["b_w_q"][i], g["b_q_norm"][i], qdT)
            dil_attention_prompt(k, c, ftabs, scr_d, qdT, S, accd)
            dil_merge(k, c, accd, S, oc)
            x1 = xb if xp_cur is xa else xa
            outproj_phase(k, c, xp_cur, [oc], [], g["b_w_out"][i], x1, S, wname="b_w_out")
            x2 = xp_cur
            mlp_phase(k, c, x1, [], g["mlp_norm"][2 + i], g["mlp_w1"][2 + i], g["mlp_w2"][2 + i], [x2] if i == 0 else [x2, yp], S)
            xp_cur = x2
    P.barrier()
    return k


def const_inputs(S):
    ncmp = S // 16 - 1
    nct = (ncmp + 127) // 128
    d = {"oh_" + nm: onehot_np(nm) for nm in ("causal", "win", "cmp", "dil1", "dil4", "dil16", "sd1", "sd4", "sd16")}
    smn = np.zeros((nct * 128, 128), np.float32)
    nsel = min(128, S // 64)
    smn[:ncmp, :nsel] = selmap_np(ncmp, nsel)
    d["selmap"] = smn
    d["piota"] = np.arange(128, dtype=np.int32).reshape(128, 1)
    d["selmap_s"] = selmap_np(128, 33)
    return d


def _prep_inputs(inputs):
    f = lambda a: np.ascontiguousarray(np.asarray(a))
    shared = {
        "a_attn_norm": f(inputs["a_attn_norm"]), "a_w_in": f(inputs["a_w_in"]), "a_q_norm": f(inputs["a_q_norm"]),
        "a_k_norm": f(inputs["a_k_norm"]), "a_cmp_pe": f(inputs["a_cmp_pe"]),
        "a_cmp_w1": f(inputs["a_cmp_w1"]).reshape(2, 2, 32 * HD, 128), "a_cmp_w2": f(inputs["a_cmp_w2"]), "a_w_out": f(inputs["a_w_out"]),
        "rel_bias": f(inputs["rel_bias"]), "mlp_norm": f(inputs["mlp_norm"]), "mlp_w1": f(inputs["mlp_w1"]), "mlp_w2": f(inputs["mlp_w2"]),
        "kv_norm": f(inputs["kv_norm"]), "w_kv_shared": f(inputs["w_kv_shared"]), "k_norm_shared": f(inputs["k_norm_shared"]),
        "b_attn_norm": f(inputs["b_attn_norm"]), "b_w_q": f(inputs["b_w_q"]), "b_q_norm": f(inputs["b_q_norm"]), "b_w_out": f(inputs["b_w_out"]),
    }
    shared.update(const_inputs(S_FULL))
    maps = []
    xs = f(inputs["x_sample"]).reshape(N_CORES, NSEQ * 8, D)
    cw = f(inputs["cache_win_kv"]).reshape(2, N_CORES, NSEQ, 512, 512)
    cd = f(inputs["cache_dil_kv"]).reshape(N_CORES, NSEQ, 2048, 512)
    pool = f(inputs["cache_nsa_kv"]).reshape(2, 2560 * 128, 1024)
    shared["pool0"] = pool[0]
    shared["pool1"] = pool[1]
    pt = f(inputs["page_table"]).astype(np.int32).reshape(N_CORES, NSEQ * NPG)
    for cidx in range(N_CORES):
        m = dict(shared)
        m["xp"] = f(inputs["x_prompt"][cidx % 2])
        m["xs"] = xs[cidx]
        m["cache_win"] = np.ascontiguousarray(cw[:, cidx])
        m["cache_dil"] = cd[cidx]
        m["pt"] = pt[cidx]
        maps.append(m)
    return maps


def kernel(**inputs):
    S = S_FULL
    k = build(S, stage=5)
    maps = _prep_inputs(inputs)
    res = run_bass_kernel_spmd(k.nc, maps, core_ids=list(range(N_CORES)))
    r = res.results
    cat = lambda nm: [np.asarray(r[cidx][nm]) for cidx in range(N_CORES)]
    yp = np.stack(cat("yp")[:2]).reshape(2, S, D)
    ys = np.concatenate(cat("ys"), 0).reshape(128, 8, D)
    rp = np.stack(cat("rows_p")[:2], axis=1).reshape(2, 2, S, 4, 4, 64)
    rs = np.concatenate([a.reshape(2, NSEQ, 8, 4, 4, 64) for a in cat("rows_s")], axis=1)
    wp = np.stack(cat("win_p")[:2], axis=1).reshape(2, 2, 512, 2, 4, 64)
    ws = np.concatenate([a.reshape(2, NSEQ, 512, 2, 4, 64) for a in cat("win_s")], axis=1)
    dp = np.stack(cat("dil_p")[:2]).reshape(2, 2048, 2, 4, 64)
    ds = np.concatenate([a.reshape(NSEQ, 2048, 2, 4, 64) for a in cat("dil_s")], axis=0)
    return (yp.astype(np.float32), ys.astype(np.float32), rp.astype(np.float32), rs.astype(np.float32),
            wp.astype(np.float32), ws.astype(np.float32), dp.astype(np.float32), ds.astype(np.float32))


def selmap_np(ncb, nsel):
    c = np.arange(ncb)
    cs, ce = c * 16, c * 16 + 31
    j0 = np.arange(nsel) * 64
    return ((cs[:, None] <= j0[None] + 63) & (ce[:, None] >= j0[None])).astype(np.float32)


def compress_setup(k, c, w):
    P, A = k.P, k.A
    cw = {}
    m0 = A.mark()
    w1 = [A.sb([64, 32, 128], BF16, "cw1") for _ in range(2)]
    w2 = [A.sb([128, 64], BF16, "cw2") for _ in range(2)]
    hb = A.sb([128, 2], F32, "chb")
    kg0, kg0b = load_bcast(k, w["k_norm0"], HD, "kg0")
    b = Buf()
    m1 = A.mark()
    stg = A.sb([64, 32, 128], F32, "cw1s")
    stg2 = A.sb([128, 64], F32, "cw2s")
    peT = A.sb([64, 32], F32, "peT")
    peTb = A.sb([64, 32], BF16, "peTb")
    ps = A.ps([128, 2], F32, "ps_hb")
    bs, bs2, bp, bps = Buf(), Buf(), Buf(), Buf()
    for i in range(2):
        P.dma(stg[:], w["cmp_w1"][i].rearrange("(l d) h -> d l h", d=64), writes=[bs])
        P.op("dve", lambda e: e.tensor_copy(out=w1[i][:], in_=stg[:]), reads=[bs], pw=[b])
        P.dma(stg2[:], w["cmp_w2"][i], writes=[bs2])
        P.op("dve", lambda e: e.tensor_copy(out=w2[i][:], in_=stg2[:]), reads=[bs2], pw=[b])
        P.dma(peT[:], w["cmp_pe"][i].rearrange("l d -> d l"), writes=[bp], allow_slow_non_contiguous=True)
        P.op("dve", lambda e: e.tensor_copy(out=peTb[:], in_=peT[:]), reads=[bp], writes=[bp])
        for l in range(32):
            P.op("pe", lambda e: e.matmul(ps[:, i:i + 1], lhsT=w1[i][:, l, :], rhs=peTb[:, l:l + 1], start=(l == 0), stop=(l == 31)),
                 reads=[b, bp], pw=[bps])
    P.op("dve", lambda e: e.tensor_copy(out=hb[:], in_=ps[:]), reads=[bps], pw=[b])
    P.barrier()
    A.release(m1)
    cw.update(w1=w1, w2=w2, hb=hb, kg0=kg0, b=b, kg0b=kg0b, mark=m0)
    return cw


def compress_run(k, c, cw, kT_sb, kT_buf, ncmp, i, out_fn):
    P, A = k.P, k.A
    st = cw["st"]
    hid_ps, hid_b = st["hid_ps"], st["hid_b"]
    sil, sil_b = st["sil"], st["sil_b"]
    o_ps, o_b = st["o_ps"], st["o_b"]
    n = st["n"]
    st["n"] += 1
    hp, hpb = hid_ps[n % 2], hid_b[n % 2]
    sl, slb = sil[n % 2], sil_b[n % 2]
    for l in range(32):
        P.op("pe", lambda e: e.matmul(hp[:, 0:ncmp], lhsT=cw["w1"][i][:, l, :], rhs=kT_sb[:, l:l + 16 * (ncmp - 1) + 1:16], start=(l == 0), stop=(l == 31)),
             reads=[cw["b"], kT_buf], pw=[hpb] if l else (), writes=() if l else [hpb])
    P.op("act", lambda e: e.activation(out=sl[:, 0:ncmp], in_=hp[:, 0:ncmp], func=AF.Silu, bias=cw["hb"][:, i:i + 1]),
         reads=[hpb, cw["b"]], writes=[slb])
    for ct in range((ncmp + 127) // 128):
        nrow = min(128, ncmp - ct * 128)
        m = st["m"]
        st["m"] += 1
        op_, opb = o_ps[m % 2], o_b[m % 2]
        P.op("pe", lambda e: e.matmul(op_[0:nrow, :], lhsT=sl[:, ct * 128:ct * 128 + nrow], rhs=cw["w2"][i][:], start=True, stop=True),
             reads=[slb, cw["b"]], writes=[opb])
        out_fn(ct, nrow, op_, opb)


def compress_state(k, misc=None):
    A = k.A
    st = {"hid_ps": [A.ps([128, 512], F32, "hid") for _ in range(2)], "hid_b": [Buf(), Buf()],
          "sil": [A.sb([128, 512], BF16, "sil") for _ in range(2)], "sil_b": [Buf(), Buf()],
          "o_ps": [A.ps([128, 64], F32, "cps") for _ in range(2)] if misc is None else [misc[:, 0:64], misc[:, 64:128]],
          "o_b": [Buf(), Buf()], "n": 0, "m": 0}
    return st


def compress_prompt(k, c, w, scr, T, ckT, ckT_b, cvx, cvx_b, selmap_ap):
    P, A = k.P, k.A
    ncmp = T // 16 - 1
    nct = (ncmp + 127) // 128
    m0 = A.mark()
    cw = compress_setup(k, c, w)
    cw["st"] = compress_state(k)
    kT = [A.sb([64, T], BF16, "ckT_in") for _ in range(2)]
    kTb = [Buf(), Buf()]
    tmp = A.sb([128, 64], F32, "ctmp")
    tmpb = A.sb([128, 64], BF16, "ctmpb")
    junk = A.sb([128, 64], F32, "cjunk")
    stt = A.sb([128, 2], F32, "cstt")
    smf = A.sb([128, nct, 128], F32, "smf")
    ps_tr = A.ps([64, 128], BF16, "ps_ctr")
    b_tmp, b_stt, b_ptr, b_sm, b_junk = Buf(), Buf(), Buf(), Buf(), Buf()
    ident, ib = c["ident"], c["ident_b"]
    P.op("pool", lambda e: e.memset(cvx[:], 0.0), writes=[cvx_b])
    P.op("pool", lambda e: e.memset(ckT[:], 0.0), writes=[ckT_b])
    P.dma(smf[:], selmap_ap.rearrange("(ct p) j -> p ct j", p=128), writes=[b_sm])
    for g in range(4):
        P.op("pool", lambda e: e.memset(cvx[:, :, g, 64:65], 1.0), pw=[cvx_b])
        P.op("dve", lambda e: e.tensor_copy(out=cvx[:, :, g, 65:193], in_=smf[:]), reads=[b_sm], pw=[cvx_b])
    n = 0
    for i in range(2):
        for g in range(4):
            kt, ktb = kT[n % 2], kTb[n % 2]
            n += 1
            P.dma(kt[:], scr["kcT" if i == 0 else "vcT"][g], reads=scr["bufs_all"], writes=[ktb])

            def out_fn(ct, nrow, op_, opb):
                if i == 1:
                    P.op("dve", lambda e: e.tensor_copy(out=cvx[0:nrow, ct, g, 0:64], in_=op_[0:nrow, :]), reads=[opb], pw=[cvx_b])
                    return
                P.op("act", lambda e: e.activation(out=junk[0:nrow, :], in_=op_[0:nrow, :], func=AF.Square, accum_out=stt[0:nrow, 0:1]),
                     reads=[opb], writes=[b_junk, b_stt])
                rms_rstd(P, stt[0:nrow, 0:1], stt[0:nrow, 1:2], HD, [b_stt], [b_stt])
                P.op("dve", lambda e: e.tensor_scalar(out=tmp[0:nrow, :], in0=op_[0:nrow, :], scalar1=stt[0:nrow, 1:2], scalar2=None, op0=ALU.mult),
                     reads=[opb, b_stt], writes=[b_tmp])
                P.op("dve", lambda e: e.tensor_tensor(out=tmpb[0:nrow, :], in0=tmp[0:nrow, :], in1=cw["kg0"][0:nrow, :], op=ALU.mult),
                     reads=[cw["kg0b"]], writes=[b_tmp])
                P.op("pe", lambda e: e.transpose(out=ps_tr[:, 0:nrow], in_=tmpb[0:nrow, :], identity=ident[0:nrow, 0:nrow]),
                     reads=[b_tmp, ib], writes=[b_ptr])
                P.op("act", lambda e: e.copy(out=ckT[:, g, ct * 128:ct * 128 + nrow], in_=ps_tr[:, 0:nrow]), reads=[b_ptr], pw=[ckT_b])

            compress_run(k, c, cw, kt, ktb, ncmp, i, out_fn)
    P.barrier()
    A.release(m0)


VARIANTS = {
    "causal": (1, 384, 2560, 0, 10 ** 9, 1),
    "win": (1, 384, 1408, 0, 512, 1),
    "cmp": (16, 31, 4096, 0, 10 ** 9, 1),
    "dil1": (1, 384, 1024, 0, 128, 1),
    "dil4": (1, 384, 1024, 0, 128, 4),
    "dil16": (1, 384, 1024, 0, 128, 16),
    "sd1": (1, 384, 2560, 0, 128, 1, 1),
    "sd4": (1, 384, 2560, 0, 512, 1, 4),
    "sd16": (1, 384, 2560, 0, 2048, 1, 16),
}


def variant_geom(name):
    s, OFF, U, lo, hi, mul = VARIANTS[name][:6]
    off = OFF + 127 * s
    L = off - OFF + U
    L = (L + 511) // 512 * 512
    base = off - OFF - 127 * s
    return s, OFF, U, off, L, base


def onehot_np(name):
    s, OFF, U, lo, hi, mul = VARIANTS[name][:6]
    mod = VARIANTS[name][6] if len(VARIANTS[name]) > 6 else 1
    _, _, _, off, L, _ = variant_geom(name)
    delta = np.arange(L) - off
    valid = (delta >= lo) & (delta <= hi) & (delta % mod == 0)
    bk = rel_bucket_np(np.maximum(delta, 0) * mul)
    oh = np.zeros((33, L), np.float32)
    oh[bk[valid], np.nonzero(valid)[0]] = 1.0
    oh[32, ~valid] = 1.0
    return oh


def build_ftabs(k, c, rel_bias_ap, names):
    P, A = k.P, k.A
    m0 = A.mark()
    rbx = A.sb([33, 16], F32, "rbx")
    b_rb = Buf()
    P.op("pool", lambda e: e.memset(rbx[:], NEG), writes=[b_rb])
    P.dma(rbx[0:32, :], rel_bias_ap, pw=[b_rb])
    oh = [A.sb([33, 512], F32, "oh") for _ in range(2)]
    ob = [A.sb([16, 512], F32, "ftc") for _ in range(2)]
    ps = [A.ps([16, 512], F32, "ps_ft") for _ in range(2)]
    b_oh, b_ob, b_ps = [Buf(), Buf()], [Buf(), Buf()], [Buf(), Buf()]
    out = {}
    n = 0
    for nm in names:
        _, _, _, off, L, _ = variant_geom(nm)
        ft = k.dscr("ftab_" + nm, [16, L], F32)
        fb = Buf()
        src = k.dram["oh_" + nm]
        for c0 in range(0, L, 512):
            i = n % 2
            n += 1
            P.dma(oh[i][:], src[:, c0:c0 + 512], writes=[b_oh[i]])
            P.op("pe", lambda e: e.matmul(ps[i][:], lhsT=rbx[:], rhs=oh[i][:], start=True, stop=True), reads=[b_rb, b_oh[i]], writes=[b_ps[i]])
            P.op("dve", lambda e: e.tensor_copy(out=ob[i][:], in_=ps[i][:]), reads=[b_ps[i]], writes=[b_ob[i]])
            P.dma(ft[:, c0:c0 + 512], ob[i][:], reads=[b_ob[i]], pw=[fb], q="pool")
        out[nm] = (ft, fb)
    P.barrier()
    A.release(m0)
    return out


def make_antiident(k, c):
    P, A = k.P, k.A
    J = A.sb([128, 128], F32, "antiI")
    b = Buf()
    P.op("pool", lambda e: e.memset(J[:], 0.0), writes=[b])
    P.op("pool", lambda e: e.affine_select(out=J[:], in_=J[:], pattern=[[1, 128]], compare_op=ALU.not_equal,
                                           fill=1.0, base=-127, channel_multiplier=1), reads=[b], writes=[b])
    c["J"] = J
    c["J_b"] = b


def build_strip(k, c, ftabs, nm, h, strip, strip_b, stage, stage_b, ps_list, ps_bufs, U=None):
    P = k.P
    s, OFF, U0, off, L, base = variant_geom(nm)
    U = U or U0
    ft, fb = ftabs[nm]
    src = bass.AP(tensor=ft.tensor, offset=ft.offset + h * L + base, ap=[[s, 128], [1, U]])
    P.dma(stage[:, 0:U], src, reads=[fb], writes=[stage_b])
    for n, c0 in enumerate(range(0, U, 512)):
        cw = min(512, U - c0)
        ps, pb = ps_list[n % len(ps_list)], ps_bufs[n % len(ps_list)]
        P.op("pe", lambda e: e.matmul(ps[:, 0:cw], lhsT=c["J"][:], rhs=stage[:, c0:c0 + cw], start=True, stop=True),
             reads=[stage_b, c["J_b"]], writes=[pb])
        P.op("act" if n % 2 else "dve",
             (lambda e: e.copy(out=strip[:, c0:c0 + cw], in_=ps[:, 0:cw])) if n % 2 else (lambda e: e.tensor_copy(out=strip[:, c0:c0 + cw], in_=ps[:, 0:cw])),
             reads=[pb], pw=[strip_b] if n else (), writes=() if n else [strip_b])


def make_W(k, c, T):
    P, A = k.P, k.A
    W = A.sb([128, T], BF16, "W")
    m0 = A.mark()
    Wf = A.sb([128, T], F32, "Wf")
    b = Buf()
    P.op("pool", lambda e: e.memset(Wf[:], 1.0), writes=[b])
    P.op("pool", lambda e: e.affine_select(out=Wf[:], in_=Wf[:], pattern=[[1, T]], compare_op=ALU.is_ge, fill=0.0, base=0, channel_multiplier=-64),
         reads=[b], writes=[b])
    P.op("pool", lambda e: e.affine_select(out=Wf[:], in_=Wf[:], pattern=[[-1, T]], compare_op=ALU.is_ge, fill=0.0, base=63, channel_multiplier=64),
         reads=[b], writes=[b])
    P.op("dve", lambda e: e.tensor_copy(out=W[:], in_=Wf[:]), reads=[b], writes=[b])
    P.barrier()
    A.release(m0)
    return W, b


def attn_tiles(k, st, lhsT_fn, q_ap, strip, strip_u0_fn, vx_fn, kts, acc, acc_b, ncol, bufs_r, mask_fn=None, qw=512):
    P = k.P
    nk = len(kts)
    for n, kt in enumerate(kts):
        i = st["n"] % 2
        st["n"] += 1
        sp, spb = st["s_ps"][i], st["s_b"][i]
        tmp, tb = st["tmp"][i], st["tmp_b"][i]
        pt, pb = st["pt"][i], st["pt_b"][i]
        P.op("pe", lambda e: e.matmul(sp[:, 0:qw], lhsT=lhsT_fn(kt), rhs=q_ap, start=True, stop=(mask_fn is None)), reads=bufs_r, writes=[spb])
        if mask_fn is not None:
            ml, mr = mask_fn(kt)
            P.op("pe", lambda e: e.matmul(sp[:, 0:qw], lhsT=ml, rhs=mr, start=False, stop=True), reads=bufs_r, pw=[spb])
        u0 = strip_u0_fn(kt)
        P.op("dve", lambda e: e.tensor_tensor(out=tmp[:, 0:qw], in0=sp[:, 0:qw], in1=strip[:, u0:u0 + qw], op=ALU.add), reads=[spb] + bufs_r, writes=[tb])
        P.op("act", lambda e: e.activation(out=pt[:, 0:qw], in_=tmp[:, 0:qw], func=AF.Exp), reads=[tb], writes=[pb])
        for qs in range(qw // 128):
            P.op("pe", lambda e: e.matmul(acc[qs], lhsT=pt[:, qs * 128:(qs + 1) * 128], rhs=vx_fn(kt), start=(n == 0 and qs % 2 == 0), stop=(n == nk - 1), skip_group_check=True),
                 reads=[pb] + bufs_r, pw=[acc_b] if (n or qs) else (), writes=() if (n or qs) else [acc_b])


def attn_state(k):
    A = k.A
    return {"n": 0, "s_ps": [A.ps([128, 512], F32, "s_ps") for _ in range(2)], "s_b": [Buf(), Buf()],
            "tmp": [A.sb([128, 512], F32, "atmp") for _ in range(2)], "tmp_b": [Buf(), Buf()],
            "pt": [A.sb([128, 512], BF16, "apt") for _ in range(2)], "pt_b": [Buf(), Buf()]}


def load_gates(k, scr, T):
    P, A = k.P, k.A
    gat = A.sb([128, T // 128, 48], F32, "gat_all")
    b = Buf()
    for k0 in range(0, T // 128, 8):
        k1 = min(T // 128, k0 + 8)
        P.dma(gat[:, k0:k1, :], scr["gates"][k0 * 128:k1 * 128, :].rearrange("(tt p) c -> p tt c", p=128), reads=scr["bufs_all"], pw=[b])
    return gat, b


def nsa_pass1(k, c, ftabs, scr, T, g, ckT, ckT_b, cvx, cvx_b, gat, gat_b, imp, imp_b, oc_dram, oc_bufs):
    P, A = k.P, k.A
    m0 = A.mark()
    ncmp = T // 16 - 1
    nct = (ncmp + 127) // 128
    st = attn_state(k)
    qT = [A.sb([64, T], BF16, "qT_h") for _ in range(2)]
    qTb = [Buf(), Buf()]
    U = min(VARIANTS["cmp"][2], max(1024, T))
    strip = A.sb([128, U], F32, "strip_c")
    stage = A.sb([128, U], F32, "stage_c")
    strip_b, stage_b = Buf(), Buf()
    accs = [A.ps([128, 2, 256], F32, "acc") for _ in range(4)]
    acc_b = [Buf(), Buf()]
    osg = [A.sb([128, 4, 64], BF16, "osg") for _ in range(2)]
    osg_b = [Buf(), Buf()]
    rs = [A.sb([128, 8], F32, "rs") for _ in range(2)]
    rs_b = [Buf(), Buf()]
    umax = U - 512
    na = 0
    for r in range(4):
        h = 4 * g + r
        q, qb = qT[r % 2], qTb[r % 2]
        P.dma(q[:], scr["qT"][h], reads=scr["bufs_all"], writes=[qb])
        build_strip(k, c, ftabs, "cmp", h, strip, strip_b, stage, stage_b, st["s_ps"], st["s_b"], U=U)
        for qt in range(T // 512):
            t0 = qt * 512
            kts = [ct for ct in range(nct) if t0 - 2048 * ct >= 0]
            a = na % 2
            na += 1
            acc = [accs[2 * a + qs // 2][:, qs % 2, 0:193] for qs in range(4)]
            attn_tiles(k, st, lambda ct: ckT[:, g, ct * 128:(ct + 1) * 128], q[:, t0:t0 + 512], strip,
                       lambda ct: min(t0 - 2048 * ct, umax), lambda ct: cvx[:, ct, g, :], kts, acc, acc_b[a], 193,
                       [ckT_b, cvx_b, qb, strip_b])
            for qs in range(4):
                tt = qt * 4 + qs
                rr = rs[a][:, qs:qs + 1]
                P.op("dve", lambda e: e.tensor_scalar(out=rr, in0=acc[qs][:, 64:65], scalar1=1e-30, scalar2=None, op0=ALU.max),
                     reads=[acc_b[a]], pw=[rs_b[a]] if qs else (), writes=() if qs else [rs_b[a]])
                P.op("dve", lambda e: e.reciprocal(out=rr, in_=rr), pw=[rs_b[a]])
                P.op("dve", lambda e: e.tensor_scalar(out=osg[a][:, qs, :], in0=acc[qs][:, 0:64], scalar1=rr, scalar2=gat[:, tt, h:h + 1],
                                                      op0=ALU.mult, op1=ALU.mult),
                     reads=[acc_b[a], rs_b[a], gat_b], pw=[osg_b[a]] if qs else (), writes=() if qs else [osg_b[a]])
                if r == 0:
                    P.op("dve", lambda e: e.tensor_scalar(out=imp[:, tt, :], in0=acc[qs][:, 65:193], scalar1=rr, scalar2=None, op0=ALU.mult),
                         reads=[acc_b[a], rs_b[a]], pw=[imp_b])
                else:
                    P.op("dve", lambda e: e.scalar_tensor_tensor(out=imp[:, tt, :], in0=acc[qs][:, 65:193], scalar=rr, in1=imp[:, tt, :],
                                                                 op0=ALU.mult, op1=ALU.add),
                         reads=[acc_b[a], rs_b[a]], pw=[imp_b])
            P.dma(oc_dram[t0:t0 + 512, h * 64:(h + 1) * 64].rearrange("(qs p) d -> p qs d", p=128), osg[a][:], reads=[osg_b[a]],
                  pw=[oc_bufs[qt]], q="pool")
    P.barrier()
    A.release(m0)


def topk_masks(k, c, T, imp, imp_b, nmT, nmT_b):
    P, A = k.P, k.A
    m0 = A.mark()
    ntt = T // 128
    BIG = 1.0e9
    Ab = A.sb([128, 384], F32, "Abase")
    Bb = A.sb([128, 384], F32, "Bbase")
    NFb = A.sb([128, 384], F32, "NFbase")
    pb = Buf()
    for half in range(2):
        ps_ = slice(half * 64, half * 64 + 64)
        e0 = 128 + half
        P.op("pool", lambda e: e.memset(Ab[ps_, :], 0.0), pw=[pb])
        P.op("pool", lambda e: e.memset(Ab[ps_, 0:e0 - 1], 1.0), pw=[pb])
        P.op("pool", lambda e: e.memset(Bb[ps_, :], 0.0), pw=[pb])
        P.op("pool", lambda e: e.memset(Bb[ps_, e0 - 1:e0 + 1], BIG), pw=[pb])
        P.op("pool", lambda e: e.memset(Bb[ps_, e0 + 1:384], -1.0), pw=[pb])
        P.op("pool", lambda e: e.memset(NFb[ps_, :], 0.0), pw=[pb])
        P.op("pool", lambda e: e.memset(NFb[ps_, 0:e0 + 1], 1.0), pw=[pb])
    val = [A.sb([128, 128], F32, "tk_val") for _ in range(2)]
    wrk = [A.sb([128, 128], F32, "tk_wrk") for _ in range(2)]
    mx = [A.sb([128, 8], F32, "tk_mx") for _ in range(2)]
    nmb = [A.sb([128, 128], BF16, "tk_nm") for _ in range(2)]
    ps = [A.ps([128, 128], BF16, "tk_ps") for _ in range(2)]
    vb, wb, mb, nb, psb = [Buf(), Buf()], [Buf(), Buf()], [Buf(), Buf()], [Buf(), Buf()], [Buf(), Buf()]
    ident, ib = c["ident"], c["ident_b"]
    for tt in range(ntt):
        i = tt % 2
        o = 128 - 2 * tt
        if o < 0:
            raise ValueError("T too large for topk pattern")
        P.op("dve", lambda e: e.tensor_tensor(out=val[i][:], in0=imp[:, tt, :], in1=Ab[:, o:o + 128], op=ALU.mult), reads=[imp_b, pb], writes=[vb[i]])
        P.op("dve", lambda e: e.tensor_tensor(out=val[i][:], in0=val[i][:], in1=Bb[:, o:o + 128], op=ALU.add), reads=[pb], writes=[vb[i]])
        P.op("dve", lambda e: e.memset(val[i][:, 0:1], BIG), writes=[vb[i]])
        src = val[i]
        for rnd in range(2):
            P.op("dve", lambda e: e.max(out=mx[i][:], in_=src[:]), reads=[vb[i], wb[i]], writes=[mb[i]])
            P.op("dve", lambda e: e.match_replace(out=wrk[i][:], in_to_replace=mx[i][:], in_values=src[:], imm_value=-2.0),
                 reads=[mb[i], vb[i]], writes=[wb[i]])
            src = wrk[i]
        P.op("dve", lambda e: e.tensor_scalar(out=wrk[i][:], in0=wrk[i][:], scalar1=-2.0, scalar2=None, op0=ALU.is_equal), writes=[wb[i]])
        P.op("dve", lambda e: e.tensor_tensor(out=wrk[i][:], in0=wrk[i][:], in1=NFb[:, o:o + 128], op=ALU.mult), reads=[pb], writes=[wb[i]])
        P.op("dve", lambda e: e.tensor_scalar(out=nmb[i][:], in0=wrk[i][:], scalar1=-NEG, scalar2=NEG, op0=ALU.mult, op1=ALU.add),
             reads=[wb[i]], writes=[nb[i]])
        P.op("pe", lambda e: e.transpose(out=ps[i][:], in_=nmb[i][:], identity=ident[:]), reads=[nb[i], ib], writes=[psb[i]])
        P.op("act", lambda e: e.copy(out=nmT[:, tt * 128:(tt + 1) * 128], in_=ps[i][:]), reads=[psb[i]], pw=[nmT_b])
    P.barrier()
    A.release(m0)


def load_vx(k, v_dram, g, T, bufs_r, name):
    P, A = k.P, k.A
    vx = A.sb([128, T // 128, 65], BF16, name)
    b = Buf()
    P.op("pool", lambda e: e.memset(vx[:, :, 64:65], 1.0), pw=[b])
    for k0 in range(0, T // 128, 8):
        k1 = min(T // 128, k0 + 8)
        P.dma(vx[:, k0:k1, 0:64], v_dram[k0 * 128:k1 * 128, g * 64:(g + 1) * 64].rearrange("(kt p) d -> p kt d", p=128), reads=bufs_r, pw=[b])
    return vx, b


def nsa_pass2(k, c, ftabs, scr, T, g, W, W_b, nmT, nmT_b, gat, gat_b, osw_dram, osw_bufs):
    P, A = k.P, k.A
    m0 = A.mark()
    st = attn_state(k)
    ksT = A.sb([64, T], BF16, "ksT_g")
    kwT = A.sb([64, T], BF16, "kwT_g")
    kb_ = Buf()
    P.dma(ksT[:], scr["ksT"][g], reads=scr["bufs_all"], pw=[kb_])
    P.dma(kwT[:], scr["kwT"][g], reads=scr["bufs_all"], pw=[kb_])
    vsx, vsb = load_vx(k, scr["vs"], g, T, scr["bufs_all"], "vsx")
    vwx, vwb = load_vx(k, scr["vw"], g, T, scr["bufs_all"], "vwx")
    qT = [A.sb([64, T], BF16, "qT_h2") for _ in range(2)]
    qTb = [Buf(), Buf()]
    Uc = min(VARIANTS["causal"][2], T + 512 + 384)
    Uw = min(VARIANTS["win"][2], T + 512 + 384)
    strip_c = A.sb([128, Uc], F32, "strip_s")
    strip_w = A.sb([128, Uw], F32, "strip_w")
    stage = A.sb([128, max(Uc, Uw)], F32, "stage_s")
    sc_b, sw_b, stage_b = Buf(), Buf(), Buf()
    accs = [A.ps([128, 2, 256], F32, "acc2") for _ in range(4)]
    acc_b = [Buf(), Buf()]
    osg = [A.sb([128, 4, 64], F32, "osg2") for _ in range(2)]
    osgb = [A.sb([128, 4, 64], BF16, "osg2b") for _ in range(2)]
    osg_b = [Buf(), Buf()]
    rs = [A.sb([128, 8], F32, "rs2") for _ in range(2)]
    rs_b = [Buf(), Buf()]
    na = 0
    no = 0
    ucmax = Uc - 512
    for r in range(4):
        h = 4 * g + r
        q, qb = qT[r % 2], qTb[r % 2]
        P.dma(q[:], scr["qT"][h], reads=scr["bufs_all"], writes=[qb])
        build_strip(k, c, ftabs, "causal", h, strip_c, sc_b, stage, stage_b, st["s_ps"], st["s_b"], U=Uc)
        build_strip(k, c, ftabs, "win", h, strip_w, sw_b, stage, stage_b, st["s_ps"], st["s_b"], U=Uw)
        for qt in range(T // 512):
            t0 = qt * 512
            o_i = no % 2
            no += 1
            for br in range(2):
                a = na % 2
                na += 1
                acc = [accs[2 * a + qs // 2][:, qs % 2, 0:65] for qs in range(4)]
                if br == 0:
                    kts = list(range(0, (t0 + 512) // 128))
                    attn_tiles(k, st, lambda kt: ksT[:, kt * 128:(kt + 1) * 128], q[:, t0:t0 + 512], strip_c,
                               lambda kt: min(t0 - kt * 128 + 384, ucmax), lambda kt: vsx[:, kt, :], kts, acc, acc_b[a], 65,
                               [kb_, vsb, qb, sc_b, W_b, nmT_b],
                               mask_fn=lambda kt: (W[:, kt * 128:(kt + 1) * 128], nmT[:, t0:t0 + 512]))
                else:
                    kts = list(range(max(0, t0 - 512) // 128, (t0 + 512) // 128))
                    attn_tiles(k, st, lambda kt: kwT[:, kt * 128:(kt + 1) * 128], q[:, t0:t0 + 512], strip_w,
                               lambda kt: t0 - kt * 128 + 384, lambda kt: vwx[:, kt, :], kts, acc, acc_b[a], 65,
                               [kb_, vwb, qb, sw_b])
                for qs in range(4):
                    tt = qt * 4 + qs
                    rr = rs[a][:, qs:qs + 1]
                    gcol = gat[:, tt, (1 + br) * 16 + h:(1 + br) * 16 + h + 1]
                    P.op("dve", lambda e: e.tensor_scalar(out=rr, in0=acc[qs][:, 64:65], scalar1=1e-30, scalar2=None, op0=ALU.max),
                         reads=[acc_b[a]], pw=[rs_b[a]] if qs else (), writes=() if qs else [rs_b[a]])
                    P.op("dve", lambda e: e.reciprocal(out=rr, in_=rr), pw=[rs_b[a]])
                    if br == 0:
                        P.op("dve", lambda e: e.tensor_scalar(out=osg[o_i][:, qs, :], in0=acc[qs][:, 0:64], scalar1=rr, scalar2=gcol,
                                                              op0=ALU.mult, op1=ALU.mult),
                             reads=[acc_b[a], rs_b[a], gat_b], pw=[osg_b[o_i]] if qs else (), writes=() if qs else [osg_b[o_i]])
                    else:
                        P.op("dve", lambda e: e.tensor_tensor(out=rr, in0=rr, in1=gcol, op=ALU.mult), reads=[gat_b], pw=[rs_b[a]])
                        P.op("dve", lambda e: e.scalar_tensor_tensor(out=osgb[o_i][:, qs, :], in0=acc[qs][:, 0:64], scalar=rr, in1=osg[o_i][:, qs, :],
                                                                     op0=ALU.mult, op1=ALU.add),
                             reads=[acc_b[a], rs_b[a]], pw=[osg_b[o_i]])
            P.dma(osw_dram[t0:t0 + 512, h * 64:(h + 1) * 64].rearrange("(qs p) d -> p qs d", p=128), osgb[o_i][:], reads=[osg_b[o_i]],
                  pw=[osw_bufs[qt]], q="pool")
    P.barrier()
    A.release(m0)


def nsa_attention(k, c, ftabs, w, scr, T, selmap_ap, oc_dram, osw_dram, W, W_b):
    P, A = k.P, k.A
    m0 = A.mark()
    ncmp = T // 16 - 1
    nct = (ncmp + 127) // 128
    ckT = A.sb([64, 4, nct * 128], BF16, "ckT")
    cvx = A.sb([128, nct, 4, 193], BF16, "cvx")
    ckT_b, cvx_b = Buf(), Buf()
    compress_prompt(k, c, w, scr, T, ckT, ckT_b, cvx, cvx_b, selmap_ap)
    gat, gat_b = load_gates(k, scr, T)
    nmT = A.sb([128, T], BF16, "nmT")
    nmT_b = Buf()
    nq = T // 512
    oc_bufs = [Buf() for _ in range(nq)]
    osw_bufs = [Buf() for _ in range(nq)]
    for g in range(4):
        m1 = A.mark()
        imp = A.sb([128, T // 128, 128], F32, "imp")
        imp_b = Buf()
        nsa_pass1(k, c, ftabs, scr, T, g, ckT, ckT_b, cvx, cvx_b, gat, gat_b, imp, imp_b, oc_dram, oc_bufs)
        topk_masks(k, c, T, imp, imp_b, nmT, nmT_b)
        A.release(m1)
        nsa_pass2(k, c, ftabs, scr, T, g, W, W_b, nmT, nmT_b, gat, gat_b, osw_dram, osw_bufs)
    P.barrier()
    A.release(m0)
    return oc_bufs + osw_bufs


def outproj_phase(k, c, x_in, srcs, src_bufs, w_out_ap, x_out, T, xin_bufs=(), wname="w_out"):
    P, A = k.P, k.A
    m0 = A.mark()
    ident, ib = c["ident"], c["ident_b"]
    wt, wb = load_weight_bf16(k, w_out_ap, D, D, None, None, wname)
    NB = 2
    xt = [A.sb([128, D], F32, "op_x") for _ in range(NB)]
    ot = [[A.sb([128, D], BF16, "op_o") for _ in range(NB)] for _ in srcs]
    oT = [A.sb([128, D], BF16, "op_oT") for _ in range(NB)]
    ps_t = A.ps([128, D], BF16, "op_pst")
    ps_y = [A.ps([128, 512], F32, "op_psy") for _ in range(2)]
    b_x, b_oT = [Buf(), Buf()], [Buf(), Buf()]
    b_o = [[Buf(), Buf()] for _ in srcs]
    b_pst, b_psy = Buf(), [Buf(), Buf()]
    out_bufs = []
    for t in range(T // 128):
        i = t % NB
        r0 = t * 128
        P.dma(xt[i][:], x_in[r0:r0 + 128, :], reads=list(xin_bufs), writes=[b_x[i]])
        for s_i, sd in enumerate(srcs):
            P.dma(ot[s_i][i][:], sd[r0:r0 + 128, :], reads=list(src_bufs), writes=[b_o[s_i][i]], q="act")
        for s_i in range(1, len(srcs)):
            P.op("pool", lambda e: e.tensor_tensor(out=ot[0][i][:], in0=ot[0][i][:], in1=ot[s_i][i][:], op=ALU.add),
                 reads=[b_o[s_i][i]], writes=[b_o[0][i]])
        for kc in range(8):
            P.op("pe", lambda e: e.transpose(out=ps_t[:, kc * 128:(kc + 1) * 128], in_=ot[0][i][:, kc * 128:(kc + 1) * 128], identity=ident[:]),
                 reads=[b_o[0][i], ib], pw=[b_pst] if kc else (), writes=() if kc else [b_pst])
        P.op("act", lambda e: e.copy(out=oT[i][:], in_=ps_t[:]), reads=[b_pst], writes=[b_oT[i]])
        for nc_ in range(2):
            for kc in range(8):
                P.op("pe", lambda e: e.matmul(ps_y[nc_][:], lhsT=oT[i][:, kc * 128:(kc + 1) * 128], rhs=wt[:, kc, nc_ * 512:(nc_ + 1) * 512],
                                              start=(kc == 0), stop=(kc == 7)),
                     reads=[b_oT[i], wb], pw=[b_psy[nc_]] if kc else (), writes=() if kc else [b_psy[nc_]])
            P.op("dve", lambda e: e.tensor_tensor(out=xt[i][:, nc_ * 512:(nc_ + 1) * 512], in0=xt[i][:, nc_ * 512:(nc_ + 1) * 512], in1=ps_y[nc_][:], op=ALU.add),
                 reads=[b_psy[nc_]], writes=[b_x[i]])
        ob = Buf()
        P.dma(x_out[r0:r0 + 128, :], xt[i][:], reads=[b_x[i]], writes=[ob], q="pool")
        out_bufs.append(ob)
    P.barrier()
    A.release(m0)
    return out_bufs


def mlp_phase(k, c, x_in, xin_bufs, norm_ap, w1_ap, w2_ap, x_outs, T):
    P, A = k.P, k.A
    m0 = A.mark()
    ident, ib = c["ident"], c["ident_b"]
    gcol, gb = load_gcol(k, norm_ap, D, "mgcol")
    w1, w1b = load_weight_bf16(k, w1_ap, D, DFF, gcol, gb, "mw1")
    w2, w2b = load_weight_bf16(k, w2_ap, DFF, D, None, None, "mw2")
    TT = 256 if T % 256 == 0 else 128
    ns = TT // 128
    xt = A.sb([128, ns, D], F32, "m_x")
    xn = A.sb([128, D], BF16, "m_xn")
    junk = A.sb([128, D], BF16, "m_junk")
    xnT = A.sb([128, 8, TT], BF16, "m_xnT")
    hT = A.sb([128, 32, TT], BF16, "m_hT")
    hr = [A.sb([128, TT], F32, "m_hr") for _ in range(2)]
    stt = A.sb([128, 4], F32, "m_st")
    ps_t = A.ps([128, D], BF16, "m_pst")
    ps_h = [A.ps([128, TT], F32, "m_psh") for _ in range(2)]
    ps_y = [A.ps([128, 512], F32, "m_psy") for _ in range(2 * ns)]
    b_x, b_xn, b_junk, b_xnT, b_hT, b_st, b_pst = Buf(), Buf(), Buf(), Buf(), Buf(), Buf(), Buf()
    b_hr, b_psh = [Buf(), Buf()], [Buf(), Buf()]
    b_psy = [Buf() for _ in range(2 * ns)]
    out_bufs = []
    nh = 0
    for t in range(T // TT):
        r0 = t * TT
        P.dma(xt[:], x_in[r0:r0 + TT, :].rearrange("(j p) d -> p j d", p=128), reads=list(xin_bufs), writes=[b_x])
        for j in range(ns):
            P.op("act", lambda e: e.activation(out=junk[:], in_=xt[:, j, :], func=AF.Square, accum_out=stt[:, 0:1]), reads=[b_x], writes=[b_junk, b_st])
            rms_rstd(P, stt[:, 0:1], stt[:, 1:2], D, [b_st], [b_st])
            P.op("dve", lambda e: e.tensor_scalar(out=xn[:], in0=xt[:, j, :], scalar1=stt[:, 1:2], scalar2=None, op0=ALU.mult),
                 reads=[b_x, b_st], writes=[b_xn])
            for kc in range(8):
                P.op("pe", lambda e: e.transpose(out=ps_t[:, kc * 128:(kc + 1) * 128], in_=xn[:, kc * 128:(kc + 1) * 128], identity=ident[:]),
                     reads=[b_xn, ib], pw=[b_pst] if kc else (), writes=() if kc else [b_pst])
            P.op("act", lambda e: e.copy(out=xnT[:, :, j * 128:(j + 1) * 128], in_=ps_t[:].rearrange("p (kc t) -> p kc t", t=128)),
                 reads=[b_pst], pw=[b_xnT] if j else (), writes=() if j else [b_xnT])
        for fc in range(32):
            i = nh % 2
            nh += 1
            for kc in range(8):
                P.op("pe", lambda e: e.matmul(ps_h[i][:], lhsT=w1[:, kc, fc * 128:(fc + 1) * 128], rhs=xnT[:, kc, :], start=(kc == 0), stop=(kc == 7)),
                     reads=[b_xnT, w1b], pw=[b_psh[i]] if kc else (), writes=() if kc else [b_psh[i]])
            P.op("act", lambda e: e.activation(out=hr[i][:], in_=ps_h[i][:], func=AF.Relu), reads=[b_psh[i]], writes=[b_hr[i]])
            P.op("dve" if fc % 2 else "pool", lambda e: e.tensor_tensor(out=hT[:, fc, :], in0=hr[i][:], in1=hr[i][:], op=ALU.mult),
                 reads=[b_hr[i]], pw=[b_hT] if fc else (), writes=() if fc else [b_hT])
        for j in range(ns):
            for nc_ in range(2):
                py, pyb = ps_y[j * 2 + nc_], b_psy[j * 2 + nc_]
                for fc in range(32):
                    P.op("pe", lambda e: e.matmul(py[:], lhsT=hT[:, fc, j * 128:(j + 1) * 128], rhs=w2[:, fc, nc_ * 512:(nc_ + 1) * 512],
                                                  start=(fc == 0), stop=(fc == 31)),
                         reads=[b_hT, w2b], pw=[pyb] if fc else (), writes=() if fc else [pyb])
                P.op("dve", lambda e: e.tensor_tensor(out=xt[:, j, nc_ * 512:(nc_ + 1) * 512], in0=xt[:, j, nc_ * 512:(nc_ + 1) * 512], in1=py[:], op=ALU.add),
                     reads=[pyb], writes=[b_x])
        ob = Buf()
        for xo in x_outs:
            P.dma(xo[r0:r0 + TT, :].rearrange("(j p) d -> p j d", p=128), xt[:], reads=[b_x], pw=[ob], q="pool")
        out_bufs.append(ob)
    P.barrier()
    A.release(m0)
    return out_bufs


def shared_kv_phase(k, c, x_ap, T, kv_norm_ap, w_kv_ap, k_norm_ap, out_rows_fn, scr=None):
    P, A = k.P, k.A
    m0 = A.mark()
    ident, ib = c["ident"], c["ident_b"]
    gcol, gb = load_gcol(k, kv_norm_ap, D, "kvg")
    wt, wb = load_weight_bf16(k, w_kv_ap, D, 512, gcol, gb, "w_kv")
    kg, kgb = load_bcast(k, k_norm_ap, HD, "kgs")
    NB = 2
    xt = [A.sb([128, D], F32, "kv_x") for _ in range(NB)]
    xn = [A.sb([128, D], BF16, "kv_xn") for _ in range(NB)]
    junk = A.sb([128, D], BF16, "kv_junk")
    xnT = [A.sb([128, D], BF16, "kv_xnT") for _ in range(NB)]
    z = [A.sb([128, 512], F32, "kv_z") for _ in range(NB)]
    zb = [A.sb([128, 512], BF16, "kv_zb") for _ in range(NB)]
    sq = A.sb([128, 256], F32, "kv_sq")
    st = [A.sb([128, 8], F32, "kv_st") for _ in range(NB)]
    kTs = [A.sb([64, 4, 128], BF16, "kv_kTs") for _ in range(NB)]
    ps_t = A.ps([128, D], BF16, "kv_pst")
    ps_z = [A.ps([128, 512], F32, "kv_psz") for _ in range(2)]
    ps_k = A.ps([64, 4, 128], BF16, "kv_psk")
    B2 = lambda: [Buf(), Buf()]
    b_x, b_xn, b_xnT, b_z, b_zb, b_st, b_kTs, b_psz = B2(), B2(), B2(), B2(), B2(), B2(), B2(), B2()
    b_junk, b_sq, b_pst, b_psk = Buf(), Buf(), Buf(), Buf()
    outb = Buf()
    for t in range(T // 128):
        i = t % NB
        r0 = t * 128
        P.dma(xt[i][:], x_ap[r0:r0 + 128, :], writes=[b_x[i]])
        P.op("act", lambda e: e.activation(out=junk[:], in_=xt[i][:], func=AF.Square, accum_out=st[i][:, 0:1]), reads=[b_x[i]], writes=[b_junk, b_st[i]])
        rms_rstd(P, st[i][:, 0:1], st[i][:, 1:2], D, [b_st[i]], [b_st[i]])
        P.op("dve", lambda e: e.tensor_scalar(out=xn[i][:], in0=xt[i][:], scalar1=st[i][:, 1:2], scalar2=None, op0=ALU.mult),
             reads=[b_x[i], b_st[i]], writes=[b_xn[i]])
        for kc in range(8):
            P.op("pe", lambda e: e.transpose(out=ps_t[:, kc * 128:(kc + 1) * 128], in_=xn[i][:, kc * 128:(kc + 1) * 128], identity=ident[:]),
                 reads=[b_xn[i], ib], pw=[b_pst] if kc else (), writes=() if kc else [b_pst])
        P.op("act", lambda e: e.copy(out=xnT[i][:], in_=ps_t[:]), reads=[b_pst], writes=[b_xnT[i]])
        pz, bz = ps_z[t % 2], b_psz[t % 2]
        for kc in range(8):
            P.op("pe", lambda e: e.matmul(pz[:], lhsT=xnT[i][:, kc * 128:(kc + 1) * 128], rhs=wt[:, kc, :], start=(kc == 0), stop=(kc == 7)),
                 reads=[b_xnT[i], wb], pw=[bz] if kc else (), writes=() if kc else [bz])
        P.op("dve", lambda e: e.tensor_copy(out=z[i][:], in_=pz[:]), reads=[bz], writes=[b_z[i]])
        zi = z[i]
        P.op("pool", lambda e: e.tensor_tensor(out=sq[:], in0=zi[:, 0:256], in1=zi[:, 0:256], op=ALU.mult), reads=[b_z[i]], writes=[b_sq])
        P.op("dve", lambda e: e.tensor_reduce(out=st[i][:, 2:6], in_=sq[:].rearrange("p (h d) -> p h d", d=64), axis=AX.X, op=ALU.add),
             reads=[b_sq], pw=[b_st[i]])
        rms_rstd(P, st[i][:, 2:6], st[i][:, 2:6], HD, [b_st[i]], [b_st[i]])
        P.op("dve", lambda e: e.tensor_tensor(out=zi[:, 0:256].rearrange("p (h d) -> p h d", d=64), in0=zi[:, 0:256].rearrange("p (h d) -> p h d", d=64),
                                              in1=st[i][:, 2:6].unsqueeze(2).to_broadcast([128, 4, 64]), op=ALU.mult),
             reads=[b_st[i]], writes=[b_z[i]])
        P.op("dve", lambda e: e.tensor_tensor(out=zi[:, 0:256].rearrange("p (h d) -> p h d", d=64), in0=zi[:, 0:256].rearrange("p (h d) -> p h d", d=64),
                                              in1=kg[:].unsqueeze(1).to_broadcast([128, 4, 64]), op=ALU.mult),
             reads=[kgb], writes=[b_z[i]])
        for (ap, lo, hi) in out_rows_fn(t):
            P.dma(ap, zi[lo:hi, :], reads=[b_z[i]], pw=[outb], q="pool")
        if scr is not None:
            P.op("act", lambda e: e.copy(out=zb[i][:], in_=zi[:]), reads=[b_z[i]], writes=[b_zb[i]])
            P.dma(scr["vd"][r0:r0 + 128, :], zb[i][:, 256:512], reads=[b_zb[i]], pw=[outb], q="pool")
            for h in range(4):
                P.op("pe", lambda e: e.transpose(out=ps_k[:, h, :], in_=zb[i][:, h * 64:(h + 1) * 64], identity=ident[:]),
                     reads=[b_zb[i], ib], pw=[b_psk] if h else (), writes=() if h else [b_psk])
            P.op("dve", lambda e: e.tensor_copy(out=kTs[i][:], in_=ps_k[:]), reads=[b_psk], writes=[b_kTs[i]])
            P.dma(scr["kdT"][:, :, r0:r0 + 128].rearrange("h d t -> d h t"), kTs[i][:], reads=[b_kTs[i]], pw=[outb], q="sp")
    P.barrier()
    A.release(m0)


PAST = 2048
NPG = 16
DEBUG_SAMPLE = False


def build_sample_bias(k, c, ftabs):
    P, A = k.P, k.A
    ball = A.sb([128, 16, 184], F32, "ball")
    bb = Buf()
    m0 = A.mark()
    stage = [A.sb([128, 184], F32, "sb_stage") for _ in range(2)]
    ps = [A.ps([128, 184], F32, "sb_ps") for _ in range(2)]
    sb_, pb_ = [Buf(), Buf()], [Buf(), Buf()]
    for h in range(16):
        i = h % 2
        st = stage[i]

        def src(nm, uoff, dims):
            s, OFF, U0, off, L, base = variant_geom(nm)
            ft, fb = ftabs[nm]
            return bass.AP(tensor=ft.tensor, offset=ft.offset + h * L + base + uoff, ap=[[s, 128]] + dims), fb
        a, fb = src("cmp", PAST, [[1, 8]])
        P.dma(st[:, 0:8], a, reads=[fb], writes=[sb_[i]])
        a, fb = src("causal", 512, [[128, 16], [1, 8]])
        P.dma(st[:, 8:136].rearrange("p (a b) -> p a b", b=8), a, reads=[fb], pw=[sb_[i]])
        a, fb = src("causal", 384, [[1, 8]])
        P.dma(st[:, 136:144], a, reads=[fb], pw=[sb_[i]])
        a, fb = src("win", 512, [[128, 4], [1, 8]])
        P.dma(st[:, 144:176].rearrange("p (a b) -> p a b", b=8), a, reads=[fb], pw=[sb_[i]])
        a, fb = src("win", 384, [[1, 8]])
        P.dma(st[:, 176:184], a, reads=[fb], pw=[sb_[i]])
        P.op("pe", lambda e: e.matmul(ps[i][:], lhsT=c["J"][:], rhs=st[:], start=True, stop=True), reads=[sb_[i], c["J_b"]], writes=[pb_[i]])
        P.op("dve", lambda e: e.tensor_copy(out=ball[:, h, :], in_=ps[i][:]), reads=[pb_[i]], pw=[bb])
    P.barrier()
    A.release(m0)
    return ball, bb


def build_page_idx(k, c, pt_ap, nseq):
    P, A = k.P, k.A
    n = nseq * NPG
    idx = A.sb([128, n], I32, "pg_idx")
    b = Buf()
    m0 = A.mark()
    io = A.sb([128, 1], I32, "pg_iota")
    iof = A.sb([128, 1], F32, "pg_iotaf")
    idf = A.sb([128, n], F32, "pg_idf")
    src = bass.AP(tensor=pt_ap.tensor, offset=pt_ap.offset, ap=[[0, 128], [1, n]])
    P.dma(idx[:], src, writes=[b])
    P.dma(io[:], k.dram["piota"], writes=[b])
    P.op("dve", lambda e: e.tensor_copy(out=iof[:], in_=io[:]), writes=[b])
    P.op("dve", lambda e: e.tensor_copy(out=idf[:], in_=idx[:]), writes=[b])
    P.op("dve", lambda e: e.tensor_scalar(out=idf[:], in0=idf[:], scalar1=128.0, scalar2=iof[:, 0:1], op0=ALU.mult, op1=ALU.add), writes=[b])
    P.op("dve", lambda e: e.tensor_copy(out=idx[:], in_=idf[:]), writes=[b])
    P.barrier()
    A.release(m0)
    return idx, b


def sample_nsa_attention(k, c, w, scr, nseq, pool_ap, cwin_ap, idx, idx_b, ball, ball_b, W, W_b, o_c, o_s, o_w, selmap_ap):
    P, A = k.P, k.A
    m0 = A.mark()
    ident, ib = c["ident"], c["ident_b"]
    TS = nseq * 8
    cw = compress_setup(k, c, w)
    misc = A.ps([128, 512], F32, "s_misc")
    cw["st"] = compress_state(k, misc)
    qn = A.sb([64, 16, TS], BF16, "s_qn")
    ksn = A.sb([64, 4, TS], BF16, "s_ksn")
    kwn = A.sb([64, 4, TS], BF16, "s_kwn")
    nb = Buf()
    P.dma(qn[:], scr["qT"].rearrange("h d t -> d h t"), reads=scr["bufs_all"], pw=[nb])
    P.dma(ksn[:], scr["ksT"].rearrange("h d t -> d h t"), reads=scr["bufs_all"], pw=[nb])
    P.dma(kwn[:], scr["kwT"].rearrange("h d t -> d h t"), reads=scr["bufs_all"], pw=[nb])
    ones = A.sb([128, 1], BF16, "s_ones")
    P.op("pool", lambda e: e.memset(ones[:], 1.0), pw=[nb])
    Rsel = A.sb([32, 8], F32, "s_rsel")
    P.op("pool", lambda e: e.memset(Rsel[:], 0.0), pw=[nb])
    for r in range(4):
        P.op("pool", lambda e: e.affine_select(out=Rsel[:], in_=Rsel[:], pattern=[[-1, 8]], compare_op=ALU.not_equal, fill=1.0,
                                               base=-8 * r, channel_multiplier=1), pw=[nb])
    smf = A.sb([128, 34], F32, "s_smf")
    P.dma(smf[:, 0:33], selmap_ap[0:128, 0:33], pw=[nb])
    Wn = A.sb([64, 8], BF16, "s_Wn")
    P.op("pool", lambda e: e.memset(Wn[:], 0.0), pw=[nb])
    P.op("pool", lambda e: e.memset(Wn[32:33, :], 1.0), pw=[nb])
    stg = [A.sb([128, 1024], F32, "s_stg") for _ in range(2)]
    stg_b = [Buf(), Buf()]
    pgb = A.sb([128, NPG, 1024], BF16, "s_pgb")
    pgb_b = Buf()
    kcT = A.sb([64, 4, PAST], BF16, "s_kcT")
    vcT = A.sb([64, 4, PAST], BF16, "s_vcT")
    ksT = A.sb([64, 4, PAST], BF16, "s_ksT")
    kT_b = Buf()
    wst = A.sb([128, 4, 512], F32, "s_wst")
    wbf = A.sb([128, 4, 512], BF16, "s_wbf")
    kwT = A.sb([64, 4, 512], BF16, "s_kwT")
    wst_b, wbf_b, kwT_b = Buf(), Buf(), Buf()
    vnew = A.sb([8, 512], BF16, "s_vnew")
    vnew_b = Buf()
    ckT = A.sb([64, 4, 128], BF16, "s_ckT")
    cvx = A.sb([128, 4, 98], BF16, "s_cvx")
    ckT_b, cvx_b = Buf(), Buf()
    tmp = A.sb([128, 64], F32, "s_tmp")
    tmpb = A.sb([128, 64], BF16, "s_tmpb")
    junk = A.sb([128, 64], F32, "s_junk")
    stt = A.sb([128, 2], F32, "s_stt")
    b_tmp, b_stt, b_junk = Buf(), Buf(), Buf()
    ps_tr = [A.ps([64, 8, 128], BF16, "s_pstr") for _ in range(2)]
    ps_trb = [Buf(), Buf()]
    ps_s_bank = A.ps([128, 512], F32, "s_pss")
    ps_s = [ps_s_bank[:, 0:32], ps_s_bank[:, 256:288]]
    ps_sb = [Buf(), Buf()]
    ps_o = [A.ps([128, 512], F32, "s_pso")[0:32, 0:128] for _ in range(2)]
    ps_ob = [Buf(), Buf()]
    ps_m = misc[0:64, 256:320]
    ps_mb = Buf()
    sst = [A.sb([128, 32], F32, "s_sst") for _ in range(2)]
    sst_b = [Buf(), Buf()]
    spt = [A.sb([128, 32], BF16, "s_spt") for _ in range(2)]
    spt_b = [Buf(), Buf()]
    accn = A.sb([32, 34], F32, "s_accn")
    ors = A.sb([32, 2], F32, "s_ors")
    osb = [A.sb([32, 64], F32, "s_osb") for _ in range(2)]
    osb_b = [Buf(), Buf()]
    accn_b, ors_b = Buf(), Buf()
    val = A.sb([8, 64], F32, "s_val")
    wrk = A.sb([8, 64], F32, "s_wrk")
    mx = A.sb([8, 8], F32, "s_mx")
    nm8 = A.sb([8, 64], BF16, "s_nm8")
    nmT = A.sb([64, 8], BF16, "s_nmT")
    nmT4 = A.sb([64, 4, 8], BF16, "s_nmT4")
    tk_b, nmT_b = Buf(), Buf()
    P.op("pool", lambda e: e.memset(cvx[:], 0.0), writes=[cvx_b])
    P.op("pool", lambda e: e.memset(ckT[:], 0.0), writes=[ckT_b])
    for g in range(4):
        P.op("pool", lambda e: e.memset(cvx[:, g, 64:65], 1.0), reads=[nb], writes=[cvx_b])
        P.op("dve", lambda e: e.tensor_copy(out=cvx[:, g, 65:98], in_=smf[:, 0:33]), reads=[nb], writes=[cvx_b])
    BIG = 1.0e9
    ntr = 0
    nst = 0
    nos = 0

    def attn_small(lhsT, nk, q_ap, bias_ap, mask, pv_list, acc, acc_b_, first, last):
        nonlocal nst
        i = nst % 2
        nst += 1
        sp, spb = ps_s[i], ps_sb[i]
        P.op("pe", lambda e: e.matmul(sp[0:nk, :], lhsT=lhsT, rhs=q_ap, start=True, stop=(mask is None)),
             reads=[kT_b, nb, kwT_b, ckT_b], writes=[spb])
        if mask is not None:
            P.op("pe", lambda e: e.matmul(sp[0:nk, :], lhsT=mask[0], rhs=mask[1], start=False, stop=True), reads=[W_b, nmT_b, nb], pw=[spb])
        P.op("dve", lambda e: e.tensor_tensor(out=sst[i][0:nk, :].rearrange("p (r t) -> p r t", t=8), in0=sp[0:nk, :].rearrange("p (r t) -> p r t", t=8),
                                              in1=bias_ap, op=ALU.add), reads=[spb, ball_b], writes=[sst_b[i]])
        P.op("act", lambda e: e.activation(out=spt[i][0:nk, :], in_=sst[i][0:nk, :], func=AF.Exp), reads=[sst_b[i]], writes=[spt_b[i]])
        for j, (rhs, c0, ncol) in enumerate(pv_list):
            P.op("pe", lambda e: e.matmul(acc[:, c0:c0 + ncol], lhsT=spt[i][0:nk, :], rhs=rhs, start=(first and j == 0), stop=last, skip_group_check=True),
                 reads=[spt_b[i], pgb_b, wbf_b, vnew_b, cvx_b, nb], pw=[acc_b_] if not (first and j == 0) else (), writes=[acc_b_] if (first and j == 0) else ())

    def finish(acc, acc_b_, dst_dram, s, g):
        nonlocal nos
        P.op("dve", lambda e: e.tensor_scalar(out=ors[:, 0:1], in0=acc[:, 64:65], scalar1=1e-30, scalar2=None, op0=ALU.max), reads=[acc_b_], writes=[ors_b])
        P.op("dve", lambda e: e.reciprocal(out=ors[:, 0:1], in_=ors[:, 0:1]), writes=[ors_b])
        i = nos % 2
        nos += 1
        P.op("dve", lambda e: e.tensor_scalar(out=osb[i][:], in0=acc[:, 0:64], scalar1=ors[:, 0:1], scalar2=None, op0=ALU.mult),
             reads=[acc_b_, ors_b], writes=[osb_b[i]])
        for r in range(4):
            h = 4 * g + r
            P.dma(dst_dram[s * 8:(s + 1) * 8, h * 64:(h + 1) * 64], osb[i][r * 8:(r + 1) * 8, :], reads=[osb_b[i]], q="pool" if r % 2 else "sp")

    for s in range(nseq):
        for pg in range(NPG):
            i = pg % 2
            P.dma_custom("pool", lambda e: e.indirect_dma_start(out=stg[i][:], out_offset=None, in_=pool_ap,
                                                                in_offset=bass.IndirectOffsetOnAxis(ap=idx[:, s * NPG + pg:s * NPG + pg + 1], axis=0)),
                         reads=[idx_b], writes=[stg_b[i]])
            P.op("act" if pg % 2 else "dve",
                 (lambda e: e.copy(out=pgb[:, pg, :], in_=stg[i][:])) if pg % 2 else (lambda e: e.tensor_copy(out=pgb[:, pg, :], in_=stg[i][:])),
                 reads=[stg_b[i]], pw=[pgb_b] if pg else (), writes=() if pg else [pgb_b])
        P.dma(wst[:], cwin_ap[s].rearrange("(wt p) c -> p wt c", p=128), writes=[wst_b])
        P.op("pool", lambda e: e.tensor_copy(out=wbf[:], in_=wst[:]), reads=[wst_b], writes=[wbf_b])
        P.dma(vnew[:, 0:256], scr["vs"][s * 8:(s + 1) * 8, :], reads=scr["bufs_all"], writes=[vnew_b])
        P.dma(vnew[:, 256:512], scr["vw"][s * 8:(s + 1) * 8, :], reads=scr["bufs_all"], pw=[vnew_b])
        for si, dstT in ((0, kcT), (1, vcT), (2, ksT)):
            for pg2 in range(0, NPG, 2):
                i = ntr % 2
                ntr += 1
                for j in range(8):
                    pg, g = pg2 + j // 4, j % 4
                    P.op("pe", lambda e: e.transpose(out=ps_tr[i][:, j, :], in_=pgb[:, pg, si * 256 + g * 64:si * 256 + (g + 1) * 64], identity=ident[:]),
                         reads=[pgb_b, ib], pw=[ps_trb[i]] if j else (), writes=() if j else [ps_trb[i]])
                for jj in range(2):
                    pg = pg2 + jj
                    P.op("act" if jj else "dve",
                         (lambda e: e.copy(out=dstT[:, :, pg * 128:(pg + 1) * 128], in_=ps_tr[i][:, jj * 4:(jj + 1) * 4, :])) if jj else
                         (lambda e: e.tensor_copy(out=dstT[:, :, pg * 128:(pg + 1) * 128], in_=ps_tr[i][:, jj * 4:(jj + 1) * 4, :])),
                         reads=[ps_trb[i]], pw=[kT_b])
        for wt2 in range(0, 4, 2):
            i = ntr % 2
            ntr += 1
            for j in range(8):
                wt, g = wt2 + j // 4, j % 4
                P.op("pe", lambda e: e.transpose(out=ps_tr[i][:, j, :], in_=wbf[:, wt, g * 64:(g + 1) * 64], identity=ident[:]),
                     reads=[wbf_b, ib], pw=[ps_trb[i]] if j else (), writes=() if j else [ps_trb[i]])
            for jj in range(2):
                wt = wt2 + jj
                P.op("dve", lambda e: e.tensor_copy(out=kwT[:, :, wt * 128:(wt + 1) * 128], in_=ps_tr[i][:, jj * 4:(jj + 1) * 4, :]), reads=[ps_trb[i]], pw=[kwT_b])
        for g in range(4):
            for i_kv in range(2):
                def out_fn(ct, nrow, op_, opb):
                    if i_kv == 1:
                        P.op("dve", lambda e: e.tensor_copy(out=cvx[0:nrow, g, 0:64], in_=op_[0:nrow, :]), reads=[opb], pw=[cvx_b])
                        return
                    P.op("act", lambda e: e.activation(out=junk[0:nrow, :], in_=op_[0:nrow, :], func=AF.Square, accum_out=stt[0:nrow, 0:1]),
                         reads=[opb], writes=[b_junk, b_stt])
                    rms_rstd(P, stt[0:nrow, 0:1], stt[0:nrow, 1:2], HD, [b_stt], [b_stt])
                    P.op("dve", lambda e: e.tensor_scalar(out=tmp[0:nrow, :], in0=op_[0:nrow, :], scalar1=stt[0:nrow, 1:2], scalar2=None, op0=ALU.mult),
                         reads=[opb, b_stt], writes=[b_tmp])
                    P.op("dve", lambda e: e.tensor_tensor(out=tmpb[0:nrow, :], in0=tmp[0:nrow, :], in1=cw["kg0"][0:nrow, :], op=ALU.mult),
                         reads=[cw["kg0b"]], writes=[b_tmp])
                    P.op("pe", lambda e: e.transpose(out=ps_tr[0][:, 0, 0:nrow], in_=tmpb[0:nrow, :], identity=ident[0:nrow, 0:nrow]),
                         reads=[b_tmp, ib], writes=[ps_trb[0]])
                    P.op("act", lambda e: e.copy(out=ckT[:, g, 0:nrow], in_=ps_tr[0][:, 0, 0:nrow]), reads=[ps_trb[0]], pw=[ckT_b])
                compress_run(k, c, cw, (kcT if i_kv == 0 else vcT)[:, g, :], kT_b, 127, i_kv, out_fn)
            q_ap = qn[:, 4 * g:4 * g + 4, s * 8:(s + 1) * 8]
            bias = lambda off, n=128: ball[0:n, 4 * g:4 * g + 4, off:off + 8]
            acc, accb = ps_o[0], ps_ob[0]
            attn_small(ckT[:, g, :], 128, q_ap, bias(0), None, [(cvx[:, g, :], 0, 98)], acc, accb, True, True)
            finish(acc, accb, o_c, s, g)
            P.op("dve", lambda e: e.tensor_scalar(out=accn[:, 0:33], in0=acc[:, 65:98], scalar1=ors[:, 0:1], scalar2=None, op0=ALU.mult),
                 reads=[accb, ors_b], writes=[accn_b])
            P.op("pe", lambda e: e.matmul(ps_m[0:8, 0:33], lhsT=Rsel[:], rhs=accn[:, 0:33], start=True, stop=True), reads=[accn_b, nb], writes=[ps_mb])
            P.op("dve", lambda e: e.memset(val[:], -1.0), writes=[tk_b])
            P.op("dve", lambda e: e.tensor_copy(out=val[:, 0:33], in_=ps_m[0:8, 0:33]), reads=[ps_mb], writes=[tk_b])
            P.op("dve", lambda e: e.memset(val[:, 0:1], BIG), writes=[tk_b])
            P.op("dve", lambda e: e.memset(val[:, 31:33], BIG), writes=[tk_b])
            srcv = val
            for rnd in range(2):
                P.op("dve", lambda e: e.max(out=mx[:], in_=srcv[:]), writes=[tk_b])
                P.op("dve", lambda e: e.match_replace(out=wrk[:], in_to_replace=mx[:], in_values=srcv[:], imm_value=-2.0), writes=[tk_b])
                srcv = wrk
            P.op("dve", lambda e: e.tensor_scalar(out=wrk[:], in0=wrk[:], scalar1=-2.0, scalar2=None, op0=ALU.is_equal), writes=[tk_b])
            P.op("dve", lambda e: e.tensor_scalar(out=nm8[:], in0=wrk[:], scalar1=-NEG, scalar2=NEG, op0=ALU.mult, op1=ALU.add), writes=[tk_b])
            P.op("dve", lambda e: e.memset(nm8[:, 33:64], NEG), writes=[tk_b])
            P.op("pe", lambda e: e.transpose(out=ps_tr[1][:, 0, 0:8], in_=nm8[:], identity=ident[0:8, 0:8]), reads=[tk_b, ib], writes=[ps_trb[1]])
            P.op("dve", lambda e: e.tensor_copy(out=nmT[:], in_=ps_tr[1][:, 0, 0:8]), reads=[ps_trb[1]], writes=[nmT_b])
            for r in range(4):
                P.op("dve", lambda e: e.tensor_copy(out=nmT4[:, r, :], in_=nmT[:]), writes=[nmT_b])
            acc, accb = ps_o[1], ps_ob[1]
            for kt in range(NPG):
                attn_small(ksT[:, g, kt * 128:(kt + 1) * 128], 128, q_ap, bias(8 + (15 - kt) * 8), (W[0:64, kt * 128:(kt + 1) * 128], nmT4[:]),
                           [(pgb[:, kt, 768 + g * 64:768 + (g + 1) * 64], 0, 64), (ones[:], 64, 1)], acc, accb, kt == 0, False)
            attn_small(ksn[:, g, s * 8:(s + 1) * 8], 8, q_ap, bias(136, 8), (Wn[:], nmT4[:]),
                       [(vnew[:, g * 64:(g + 1) * 64], 0, 64), (ones[0:8, :], 64, 1)], acc, accb, False, True)
            finish(acc, accb, o_s, s, g)
            acc, accb = ps_o[0], ps_ob[0]
            for wt in range(4):
                attn_small(kwT[:, g, wt * 128:(wt + 1) * 128], 128, q_ap, bias(144 + (3 - wt) * 8), None,
                           [(wbf[:, wt, 256 + g * 64:256 + (g + 1) * 64], 0, 64), (ones[:], 64, 1)], acc, accb, wt == 0, False)
            attn_small(kwn[:, g, s * 8:(s + 1) * 8], 8, q_ap, bias(176, 8), None,
                       [(vnew[:, 256 + g * 64:256 + (g + 1) * 64], 0, 64), (ones[0:8, :], 64, 1)], acc, accb, False, True)
            finish(acc, accb, o_w, s, g)
    P.barrier()
    A.release(m0)


def sample_gate_sum(k, c, gates_dram, o_c, o_s, o_w, og_dram, T=128):
    P, A = k.P, k.A
    m0 = A.mark()
    gt = A.sb([128, 48], F32, "sg_g")
    ot = [A.sb([128, 16, 64], F32, "sg_o") for _ in range(3)]
    acc = A.sb([128, 16, 64], F32, "sg_acc")
    accb = A.sb([128, D], BF16, "sg_accb")
    b = Buf()
    P.dma(gt[:], gates_dram[0:T, :], pw=[b])
    for i, od in enumerate((o_c, o_s, o_w)):
        P.dma(ot[i][:], od[0:T, :].rearrange("p (h d) -> p h d", d=64), pw=[b])
    for i in range(3):
        P.op("dve", lambda e: e.tensor_tensor(out=ot[i][:], in0=ot[i][:], in1=gt[:, i * 16:(i + 1) * 16].unsqueeze(2).to_broadcast([128, 16, 64]), op=ALU.mult),
             reads=[b], writes=[b])
    P.op("dve", lambda e: e.tensor_tensor(out=acc[:], in0=ot[0][:], in1=ot[1][:], op=ALU.add), reads=[b], writes=[b])
    P.op("dve", lambda e: e.tensor_tensor(out=accb[:].rearrange("p (h d) -> p h d", d=64), in0=acc[:], in1=ot[2][:], op=ALU.add), reads=[b], writes=[b])
    P.dma(og_dram[0:T, :], accb[:], reads=[b], writes=[b])
    P.barrier()
    A.release(m0)


DIL = ((128, 1), (512, 4), (2048, 16))


def dil_project_phase(k, c, x_ap, T, norm_ap, wq_ap, qnorm_ap, qdT, x_bufs=()):
    P, A = k.P, k.A
    m0 = A.mark()
    ident, ib = c["ident"], c["ident_b"]
    gcol, gb = load_gcol(k, norm_ap, D, "dq_g")
    wt, wb = load_weight_bf16(k, wq_ap, D, 3072, gcol, gb, "w_q")
    qg = A.sb([128, 3, 64], F32, "dq_qg")
    qgb = Buf()
    P.dma(qg[:], bass.AP(tensor=qnorm_ap.tensor, offset=qnorm_ap.offset, ap=[[0, 128], [64, 3], [1, 64]]), writes=[qgb])
    NB = 2
    xt = [A.sb([128, D], F32, "dq_x") for _ in range(NB)]
    xn = [A.sb([128, D], BF16, "dq_xn") for _ in range(NB)]
    junk = A.sb([128, D], BF16, "dq_junk")
    xnT = [A.sb([128, D], BF16, "dq_xnT") for _ in range(NB)]
    z = [A.sb([128, 3072], F32, "dq_z") for _ in range(NB)]
    sq = A.sb([128, 3072], F32, "dq_sq")
    qb = [A.sb([128, 3072], BF16, "dq_qb") for _ in range(NB)]
    st = [A.sb([128, 50], F32, "dq_st") for _ in range(NB)]
    qTs = [A.sb([64, 48, 128], BF16, "dq_qTs") for _ in range(NB)]
    ps_t = A.ps([128, D], BF16, "dq_pst")
    ps_z = [A.ps([128, 512], F32, "dq_psz") for _ in range(2)]
    ps_q = [A.ps([64, 8, 128], BF16, "dq_psq") for _ in range(2)]
    B2 = lambda: [Buf(), Buf()]
    b_x, b_xn, b_xnT, b_z, b_qb, b_st, b_qTs, b_psz, b_psq = B2(), B2(), B2(), B2(), B2(), B2(), B2(), B2(), B2()
    b_junk, b_sq, b_pst = Buf(), Buf(), Buf()
    ob = Buf()
    zc = 0
    for t in range(T // 128):
        i = t % NB
        r0 = t * 128
        P.dma(xt[i][:], x_ap[r0:r0 + 128, :], reads=list(x_bufs), writes=[b_x[i]])
        P.op("act", lambda e: e.activation(out=junk[:], in_=xt[i][:], func=AF.Square, accum_out=st[i][:, 0:1]), reads=[b_x[i]], writes=[b_junk, b_st[i]])
        rms_rstd(P, st[i][:, 0:1], st[i][:, 1:2], D, [b_st[i]], [b_st[i]])
        P.op("dve", lambda e: e.tensor_scalar(out=xn[i][:], in0=xt[i][:], scalar1=st[i][:, 1:2], scalar2=None, op0=ALU.mult),
             reads=[b_x[i], b_st[i]], writes=[b_xn[i]])
        for kc in range(8):
            P.op("pe", lambda e: e.transpose(out=ps_t[:, kc * 128:(kc + 1) * 128], in_=xn[i][:, kc * 128:(kc + 1) * 128], identity=ident[:]),
                 reads=[b_xn[i], ib], pw=[b_pst] if kc else (), writes=() if kc else [b_pst])
        P.op("act", lambda e: e.copy(out=xnT[i][:], in_=ps_t[:]), reads=[b_pst], writes=[b_xnT[i]])
        for c0 in range(0, 3072, 512):
            pz, bz = ps_z[zc % 2], b_psz[zc % 2]
            zc += 1
            for kc in range(8):
                P.op("pe", lambda e: e.matmul(pz[:], lhsT=xnT[i][:, kc * 128:(kc + 1) * 128], rhs=wt[:, kc, c0:c0 + 512], start=(kc == 0), stop=(kc == 7)),
                     reads=[b_xnT[i], wb], pw=[bz] if kc else (), writes=() if kc else [bz])
            P.op("dve" if (c0 // 512) % 2 == 0 else "act",
                 (lambda e: e.tensor_copy(out=z[i][:, c0:c0 + 512], in_=pz[:])) if (c0 // 512) % 2 == 0 else (lambda e: e.copy(out=z[i][:, c0:c0 + 512], in_=pz[:])),
                 reads=[bz], pw=[b_z[i]] if c0 else (), writes=() if c0 else [b_z[i]])
        zi = z[i]
        P.op("pool", lambda e: e.tensor_tensor(out=sq[:], in0=zi[:], in1=zi[:], op=ALU.mult), reads=[b_z[i]], writes=[b_sq])
        P.op("dve", lambda e: e.tensor_reduce(out=st[i][:, 2:50], in_=sq[:].rearrange("p (h d) -> p h d", d=64), axis=AX.X, op=ALU.add),
             reads=[b_sq], pw=[b_st[i]])
        rms_rstd(P, st[i][:, 2:50], st[i][:, 2:50], HD, [b_st[i]], [b_st[i]], scale=HD ** -0.5)
        P.op("dve", lambda e: e.tensor_tensor(out=sq[:].rearrange("p (h d) -> p h d", d=64), in0=zi[:].rearrange("p (h d) -> p h d", d=64),
                                              in1=st[i][:, 2:50].unsqueeze(2).to_broadcast([128, 48, 64]), op=ALU.mult),
             reads=[b_z[i], b_st[i]], writes=[b_sq])
        for grp in range(3):
            P.op("pool", lambda e: e.tensor_tensor(out=qb[i][:, grp * 1024:(grp + 1) * 1024].rearrange("p (h d) -> p h d", d=64),
                                                   in0=sq[:, grp * 1024:(grp + 1) * 1024].rearrange("p (h d) -> p h d", d=64),
                                                   in1=qg[:, grp, :].unsqueeze(1).to_broadcast([128, 16, 64]), op=ALU.mult),
                 reads=[b_sq, qgb], pw=[b_qb[i]] if grp else (), writes=() if grp else [b_qb[i]])
        for hh in range(6):
            pq, pqb = ps_q[hh % 2], b_psq[hh % 2]
            for h8 in range(8):
                h = hh * 8 + h8
                P.op("pe", lambda e: e.transpose(out=pq[:, h8, :], in_=qb[i][:, h * 64:(h + 1) * 64], identity=ident[:]),
                     reads=[b_qb[i], ib], pw=[pqb] if h8 else (), writes=() if h8 else [pqb])
            P.op("dve" if hh % 2 == 0 else "act",
                 (lambda e: e.tensor_copy(out=qTs[i][:, hh * 8:(hh + 1) * 8, :], in_=pq[:])) if hh % 2 == 0 else (lambda e: e.copy(out=qTs[i][:, hh * 8:(hh + 1) * 8, :], in_=pq[:])),
                 reads=[pqb], pw=[b_qTs[i]] if hh else (), writes=() if hh else [b_qTs[i]])
        for grp in range(3):
            P.dma(qdT[grp][:, :, r0:r0 + 128].rearrange("h d t -> d h t"), qTs[i][:, grp * 16:(grp + 1) * 16, :], reads=[b_qTs[i]], pw=[ob], q="sp" if grp != 1 else "pool")
    P.barrier()
    A.release(m0)


def dil_attention_prompt(k, c, ftabs, scr_d, qdT, T, accd):
    P, A = k.P, k.A
    m0 = A.mark()
    st = attn_state(k)
    accs = [A.ps([128, 2, 256], F32, "dacc") for _ in range(4)]
    acc_b = [Buf(), Buf()]
    osg = [A.sb([128, 4, 65], F32, "dosg") for _ in range(2)]
    osg_b = [Buf(), Buf()]
    stage = A.sb([128, 1024], F32, "dstage")
    stage_b = Buf()
    strips = [A.sb([128, 1024], F32, "dstrip") for _ in range(3)]
    strip_b = [Buf(), Buf(), Buf()]
    qT = [A.sb([64, T], BF16, "dqT") for _ in range(2)]
    qTb = [Buf(), Buf()]
    nq = 0
    na = 0
    for g in range(4):
        m1 = A.mark()
        kT = A.sb([64, T], BF16, "dkT")
        kb_ = Buf()
        P.dma(kT[:], scr_d["kdT"][g], writes=[kb_])
        vxs = []
        for gi, (wnd, d) in enumerate(DIL):
            L = T // d
            vx = A.sb([128, T // 128, 65], BF16, "dvx%d" % gi)
            vb = Buf()
            P.op("pool", lambda e: e.memset(vx[:, :, 64:65], 1.0), pw=[vb])
            nkt = L // 128
            for r in range(d):
                for k0 in range(0, nkt, 8):
                    k1 = min(nkt, k0 + 8)
                    src = bass.AP(tensor=scr_d["vd"].tensor, offset=scr_d["vd"].offset + (r + d * k0 * 128) * 256 + g * 64,
                                  ap=[[d * 256, 128], [d * 128 * 256, k1 - k0], [1, 64]])
                    P.dma(vx[:, r * nkt + k0:r * nkt + k1, 0:64], src, pw=[vb], q="act" if (r + k0) % 2 else "sp")
            vxs.append((vx, vb))
        for rr in range(4):
            h = 4 * g + rr
            for gi, (wnd, d) in enumerate(DIL):
                L = T // d
                nkt = L // 128
                qw = min(512, L)
                q, qb = qT[nq % 2], qTb[nq % 2]
                nq += 1
                P.dma(q[:], qdT[gi][h], writes=[qb])
                build_strip(k, c, ftabs, "dil%d" % d, h, strips[gi], strip_b[gi], stage, stage_b, st["s_ps"], st["s_b"])
                vx, vb = vxs[gi]
                for r in range(d):
                    for m0_ in range(0, L, qw):
                        a = na % 2
                        na += 1
                        acc = [accs[2 * a + qs // 2][:, qs % 2, 0:65] for qs in range(4)]
                        kts = list(range(max(0, m0_ - 128) // 128, (m0_ + qw) // 128))
                        q_ap = q[:, r + d * m0_:r + d * (m0_ + qw - 1) + 1:d]
                        attn_tiles(k, st, lambda kt: kT[:, r + d * kt * 128:r + d * (kt * 128 + 127) + 1:d], q_ap, strips[gi],
                                   lambda kt: m0_ - kt * 128 + 384, lambda kt: vx[:, r * nkt + kt, :], kts, acc, acc_b[a], 65,
                                   [kb_, vb, qb, strip_b[gi]], qw=qw)
                        nqs = qw // 128
                        for qs in range(nqs):
                            P.op("dve" if qs % 2 else "act",
                                 (lambda e: e.tensor_copy(out=osg[a][:, qs, :], in_=acc[qs])) if qs % 2 else (lambda e: e.copy(out=osg[a][:, qs, :], in_=acc[qs])),
                                 reads=[acc_b[a]], pw=[osg_b[a]] if qs else (), writes=() if qs else [osg_b[a]])
                        dst = bass.AP(tensor=accd.tensor, offset=accd.offset + ((gi * 16 + h) * T + r + d * m0_) * 65,
                                      ap=[[d * 65, 128], [d * 128 * 65, nqs], [1, 65]])
                        P.dma(dst, osg[a][:, 0:nqs, :], reads=[osg_b[a]], q="pool")
        P.barrier()
        A.release(m1)
    P.barrier()
    A.release(m0)


def dil_merge(k, c, accd, T, og_dram):
    P, A = k.P, k.A
    m0 = A.mark()
    at = [[A.sb([128, 16, 65], F32, "dm_a") for _ in range(3)] for _ in range(2)]
    rc = [A.sb([128, 16, 1], F32, "dm_r") for _ in range(2)]
    ob = [A.sb([128, 16, 64], BF16, "dm_o") for _ in range(2)]
    bb = [Buf(), Buf()]
    for t in range(T // 128):
        i = t % 2
        for gi in range(3):
            src = bass.AP(tensor=accd.tensor, offset=accd.offset + (gi * 16 * T + t * 128) * 65, ap=[[65, 128], [T * 65, 16], [1, 65]])
            P.dma(at[i][gi][:], src, writes=[bb[i]] if gi == 0 else (), pw=[bb[i]] if gi else (), q=("sp", "act", "pool")[gi])
        P.op("dve", lambda e: e.tensor_tensor(out=at[i][0][:], in0=at[i][0][:], in1=at[i][1][:], op=ALU.add), reads=[bb[i]], writes=[bb[i]])
        P.op("dve", lambda e: e.tensor_tensor(out=at[i][0][:], in0=at[i][0][:], in1=at[i][2][:], op=ALU.add), writes=[bb[i]])
        P.op("dve", lambda e: e.tensor_scalar(out=rc[i][:], in0=at[i][0][:, :, 64:65], scalar1=1e-30, scalar2=None, op0=ALU.max), writes=[bb[i]])
        P.op("dve", lambda e: e.reciprocal(out=rc[i][:], in_=rc[i][:]), writes=[bb[i]])
        P.op("dve", lambda e: e.tensor_tensor(out=ob[i][:], in0=at[i][0][:, :, 0:64], in1=rc[i][:].to_broadcast([128, 16, 64]), op=ALU.mult), writes=[bb[i]])
        P.dma(og_dram[t * 128:(t + 1) * 128, :], ob[i][:].rearrange("p h d -> p (h d)"), reads=[bb[i]], q="pool")
    P.barrier()
    A.release(m0)


SD_TILES = ((1, [15]), (4, [12, 13, 14, 15]), (16, list(range(16))))


def build_sample_dil_bias(k, c, ftabs):
    P, A = k.P, k.A
    balld = A.sb([128, 16, 192], F32, "balld")
    bb = Buf()
    m0 = A.mark()
    stage = [A.sb([128, 192], F32, "sdb_stage") for _ in range(2)]
    ps = [A.ps([128, 192], F32, "sdb_ps") for _ in range(2)]
    sb_, pb_ = [Buf(), Buf()], [Buf(), Buf()]
    offs = {}
    for h in range(16):
        i = h % 2
        st = stage[i]
        col = 0
        first = True
        for (d, tiles) in SD_TILES:
            nm = "sd%d" % d
            s_, OFF, U0, off, L, base = variant_geom(nm)
            ft, fb = ftabs[nm]
            for kt in tiles:
                kk = 15 - kt
                a = bass.AP(tensor=ft.tensor, offset=ft.offset + h * L + base + 512 + 128 * kk, ap=[[1, 128], [1, 8]])
                P.dma(st[:, col:col + 8], a, reads=[fb], writes=[sb_[i]] if first else (), pw=() if first else [sb_[i]], q="sp" if col % 16 else "act")
                first = False
                offs[(d, kt)] = col
                col += 8
            a = bass.AP(tensor=ft.tensor, offset=ft.offset + h * L + base + 384, ap=[[1, 128], [1, 8]])
            P.dma(st[:, col:col + 8], a, reads=[fb], pw=[sb_[i]])
            offs[(d, "new")] = col
            col += 8
        P.op("pe", lambda e: e.matmul(ps[i][:], lhsT=c["J"][:], rhs=st[:], start=True, stop=True), reads=[sb_[i], c["J_b"]], writes=[pb_[i]])
        P.op("dve", lambda e: e.tensor_copy(out=balld[:, h, :], in_=ps[i][:]), reads=[pb_[i]], pw=[bb])
    P.barrier()
    A.release(m0)
    return balld, bb, offs


def sample_dil_attention(k, c, scr_sd, qdT_s, nseq, cdil_ap, balld, balld_b, offs, o_dram):
    P, A = k.P, k.A
    m0 = A.mark()
    ident, ib = c["ident"], c["ident_b"]
    TS = nseq * 8
    qn = A.sb([64, 48, TS], BF16, "sd_qn")
    kn = A.sb([64, 4, TS], BF16, "sd_kn")
    nb = Buf()
    for gi in range(3):
        P.dma(qn[:, gi * 16:(gi + 1) * 16, :], qdT_s[gi].rearrange("h d t -> d h t"), pw=[nb])
    P.dma(kn[:], scr_sd["kdT"].rearrange("h d t -> d h t"), pw=[nb])
    ones = A.sb([128, 1], BF16, "sd_ones")
    P.op("pool", lambda e: e.memset(ones[:], 1.0), pw=[nb])
    stg = A.sb([128, 16, 512], F32, "sd_stg")
    cb = A.sb([128, 16, 512], BF16, "sd_cb")
    kT = A.sb([64, 4, PAST], BF16, "sd_kT")
    vnew = A.sb([8, 256], BF16, "sd_vnew")
    stg_b, cb_b, kT_b, vnew_b = Buf(), Buf(), Buf(), Buf()
    ps_tr = [A.ps([64, 8, 128], BF16, "sd_pstr") for _ in range(2)]
    ps_trb = [Buf(), Buf()]
    ps_s_bank = A.ps([128, 512], F32, "sd_pss")
    ps_s = [ps_s_bank[:, 0:32], ps_s_bank[:, 256:288]]
    ps_sb = [Buf(), Buf()]
    ps_o = [A.ps([128, 512], F32, "sd_pso")[0:32, 0:128] for _ in range(2)]
    ps_ob = [Buf(), Buf()]
    sst = [A.sb([128, 32], F32, "sd_sst") for _ in range(2)]
    sst_b = [Buf(), Buf()]
    spt = [A.sb([128, 32], BF16, "sd_spt") for _ in range(2)]
    spt_b = [Buf(), Buf()]
    ors = A.sb([32, 2], F32, "sd_ors")
    ors_b = Buf()
    osb = [A.sb([32, 64], F32, "sd_osb") for _ in range(2)]
    osb_b = [Buf(), Buf()]
    ntr = 0
    nst = 0
    nacc = 0
    for s in range(nseq):
        for hf in range(2):
            P.dma(stg[:, hf * 8:(hf + 1) * 8, :], cdil_ap[s, hf * 1024:(hf + 1) * 1024, :].rearrange("(kt p) c -> p kt c", p=128),
                  writes=[stg_b] if hf == 0 else (), pw=[stg_b] if hf else (), q="sp" if hf else "act")
        P.op("dve", lambda e: e.tensor_copy(out=cb[:, 0:8, :], in_=stg[:, 0:8, :]), reads=[stg_b], writes=[cb_b])
        P.op("pool", lambda e: e.tensor_copy(out=cb[:, 8:16, :], in_=stg[:, 8:16, :]), reads=[stg_b], pw=[cb_b])
        P.dma(vnew[:], scr_sd["vd"][s * 8:(s + 1) * 8, :], writes=[vnew_b])
        for kt2 in range(0, 16, 2):
            i = ntr % 2
            ntr += 1
            for j in range(8):
                kt, g = kt2 + j // 4, j % 4
                P.op("pe", lambda e: e.transpose(out=ps_tr[i][:, j, :], in_=cb[:, kt, g * 64:(g + 1) * 64], identity=ident[:]),
                     reads=[cb_b, ib], pw=[ps_trb[i]] if j else (), writes=() if j else [ps_trb[i]])
            for jj in range(2):
                kt = kt2 + jj
                P.op("act" if jj else "dve",
                     (lambda e: e.copy(out=kT[:, :, kt * 128:(kt + 1) * 128], in_=ps_tr[i][:, jj * 4:(jj + 1) * 4, :])) if jj else
                     (lambda e: e.tensor_copy(out=kT[:, :, kt * 128:(kt + 1) * 128], in_=ps_tr[i][:, jj * 4:(jj + 1) * 4, :])),
                     reads=[ps_trb[i]], pw=[kT_b])
        for g in range(4):
            acc, accb = ps_o[nacc % 2], ps_ob[nacc % 2]
            nacc += 1
            steps = []
            for gi, (d, tiles) in enumerate(SD_TILES):
                for kt in tiles:
                    steps.append((gi, d, kt))
                steps.append((gi, d, "new"))
            for n, (gi, d, kt) in enumerate(steps):
                i = nst % 2
                nst += 1
                sp, spb = ps_s[i], ps_sb[i]
                q_ap = qn[:, gi * 16 + 4 * g:gi * 16 + 4 * g + 4, s * 8:(s + 1) * 8]
                if kt == "new":
                    nk, lhsT = 8, kn[:, g, s * 8:(s + 1) * 8]
                    pv = [(vnew[:, g * 64:(g + 1) * 64], 0, 64), (ones[0:8, :], 64, 1)]
                else:
                    nk, lhsT = 128, kT[:, g, kt * 128:(kt + 1) * 128]
                    pv = [(cb[:, kt, 256 + g * 64:256 + (g + 1) * 64], 0, 64), (ones[:], 64, 1)]
                off = offs[(d, kt)]
                P.op("pe", lambda e: e.matmul(sp[0:nk, :], lhsT=lhsT, rhs=q_ap, start=True, stop=True), reads=[kT_b, nb], writes=[spb])
                P.op("dve", lambda e: e.tensor_tensor(out=sst[i][0:nk, :].rearrange("p (r t) -> p r t", t=8), in0=sp[0:nk, :].rearrange("p (r t) -> p r t", t=8),
                                                      in1=balld[0:nk, 4 * g:4 * g + 4, off:off + 8], op=ALU.add), reads=[spb, balld_b], writes=[sst_b[i]])
                P.op("act", lambda e: e.activation(out=spt[i][0:nk, :], in_=sst[i][0:nk, :], func=AF.Exp), reads=[sst_b[i]], writes=[spt_b[i]])
                for j, (rhs, c0, ncol) in enumerate(pv):
                    first = (n == 0 and j == 0)
                    P.op("pe", lambda e: e.matmul(acc[:, c0:c0 + ncol], lhsT=spt[i][0:nk, :], rhs=rhs, start=first, stop=(n == len(steps) - 1), skip_group_check=True),
                         reads=[spt_b[i], cb_b, vnew_b, nb], pw=() if first else [accb], writes=[accb] if first else ())
            P.op("dve", lambda e: e.tensor_scalar(out=ors[:, 0:1], in0=acc[:, 64:65], scalar1=1e-30, scalar2=None, op0=ALU.max), reads=[accb], writes=[ors_b])
            P.op("dve", lambda e: e.reciprocal(out=ors[:, 0:1], in_=ors[:, 0:1]), writes=[ors_b])
            oi = nacc % 2
            P.op("dve", lambda e: e.tensor_scalar(out=osb[oi][:], in0=acc[:, 0:64], scalar1=ors[:, 0:1], scalar2=None, op0=ALU.mult),
                 reads=[accb, ors_b], writes=[osb_b[oi]])
            for r in range(4):
                h = 4 * g + r
                P.dma(o_dram[s * 8:(s + 1) * 8, h * 64:(h + 1) * 64], osb[oi][r * 8:(r + 1) * 8, :], reads=[osb_b[oi]], q="pool" if r % 2 else "sp")
    P.barrier()
    A.release(m0)


def cast_phase(k, src_f32, dst_bf16, T=128):
    P, A = k.P, k.A
    m0 = A.mark()
    a = A.sb([128, D], F32, "cast_a")
    b_ = A.sb([128, D], BF16, "cast_b")
    bb = Buf()
    P.dma(a[:], src_f32[0:T, :], writes=[bb])
    P.op("dve", lambda e: e.tensor_copy(out=b_[:], in_=a[:]), reads=[bb], writes=[bb])
    P.dma(dst_bf16[0:T, :], b_[:], reads=[bb], writes=[bb])
    P.barrier()
    A.release(m0)
```

```python
import math
import numpy as np
import ml_dtypes
import concourse.bass as bass
import concourse.mybir as mybir
from concourse.bass_utils import run_bass_kernel_spmd

F32 = mybir.dt.float32
BF16 = mybir.dt.bfloat16
I32 = mybir.dt.int32
AF = mybir.ActivationFunctionType
ALU = mybir.AluOpType
AX = mybir.AxisListType

D = 1024
NH = 16
HD = 64
KVH = 4
NSA_IN = 2608
DFF = 4096
EPS = 1e-6
NEG = -30000.0


class Buf:
    __slots__ = ("w", "r")

    def __init__(self):
        self.w = {}
        self.r = {}


class Prog:
    def __init__(self, nc):
        self.nc = nc
        self.eng = {"pe": nc.tensor, "act": nc.scalar, "dve": nc.vector, "pool": nc.gpsimd, "sp": nc.sync}
        self.sems = {}
        self.cnt = {}
        self.waited = {e: {} for e in self.eng}
        self._stack = []
        for e in ("pe", "act", "dve", "pool"):
            self.sems[e] = self._sem("s_" + e)
            self.cnt[e] = 0
        self.dslots = {"sp": 10, "pool": 6, "act": 4}
        self.dnext = {k: 0 for k in self.dslots}
        for q, n in self.dslots.items():
            for i in range(n):
                k = ("d", q, i)
                self.sems[k] = self._sem("d_%s%d" % (q, i))
                self.cnt[k] = 0
        self.n_ins = 0

    def _sem(self, name):
        cm = self.nc.semaphore(name)
        s = cm.__enter__()
        self._stack.append(cm)
        return s

    def close(self):
        for cm in reversed(self._stack):
            cm.__exit__(None, None, None)

    def _wait(self, e, deps):
        w = self.waited[e]
        for k, v in deps.items():
            if k == e and e == "pe":
                continue
            if w.get(k, 0) >= v:
                continue
            self.eng[e].wait_ge(self.sems[k], v)
            w[k] = v

    @staticmethod
    def _merge(d, s):
        for k, v in s.items():
            if d.get(k, 0) < v:
                d[k] = v

    def _deps(self, reads, writes, pw=()):
        deps = {}
        for b in reads:
            self._merge(deps, b.w)
        for b in writes:
            self._merge(deps, b.w)
            self._merge(deps, b.r)
        for b in pw:
            self._merge(deps, b.w)
            self._merge(deps, b.r)
        return deps

    def _commit(self, key, val, reads, writes, pw=()):
        for b in reads:
            if b.r.get(key, 0) < val:
                b.r[key] = val
        for b in writes:
            b.w = {key: val}
            b.r = {}
        for b in pw:
            b.w[key] = val
            b.r = {}

    def op(self, e, fn, reads=(), writes=(), pw=()):
        deps = self._deps(reads, writes, pw)
        self._wait(e, deps)
        ins = fn(self.eng[e])
        self.cnt[e] += 1
        ins.then_inc(self.sems[e], 1)
        self._commit(e, self.cnt[e], reads, writes, pw)
        self.n_ins += 1
        return ins

    def dma(self, out, in_, reads=(), writes=(), q="sp", pw=(), **kw):
        slot = self.dnext[q]
        self.dnext[q] = (slot + 1) % self.dslots[q]
        key = ("d", q, slot)
        deps = self._deps(reads, writes, pw)
        if self.cnt[key] > 0:
            deps[key] = max(deps.get(key, 0), self.cnt[key])
        self._wait(q, deps)
        ins = self.eng[q].dma_start(out=out, in_=in_, **kw)
        self.cnt[key] += 16
        ins.then_inc(self.sems[key], 16)
        self._commit(key, self.cnt[key], reads, writes, pw)
        self.n_ins += 1
        return ins

    def dma_custom(self, q, fn, reads=(), writes=(), pw=()):
        slot = self.dnext[q]
        self.dnext[q] = (slot + 1) % self.dslots[q]
        key = ("d", q, slot)
        deps = self._deps(reads, writes, pw)
        if self.cnt[key] > 0:
            deps[key] = max(deps.get(key, 0), self.cnt[key])
        self._wait(q, deps)
        ins = fn(self.eng[q])
        self.cnt[key] += 16
        ins.then_inc(self.sems[key], 16)
        self._commit(key, self.cnt[key], reads, writes, pw)
        self.n_ins += 1
        return ins

    def all_tokens(self):
        d = {}
        for k, v in self.cnt.items():
            if v > 0:
                d[k] = v
        return d

    def barrier(self):
        d = self.all_tokens()
        for e in self.eng:
            self._wait(e, dict(d))


class Alloc:
    def __init__(self, nc):
        self.nc = nc
        self.stack = []
        self.i = 0

    def sb(self, shape, dt, name=None):
        self.i += 1
        cm = self.nc.sbuf_tensor("%s_%d" % (name or "sb", self.i), list(shape), dt)
        t = cm.__enter__()
        self.stack.append(cm)
        return t

    def ps(self, shape, dt, name=None):
        self.i += 1
        cm = self.nc.psum_tensor("%s_%d" % (name or "ps", self.i), list(shape), dt)
        t = cm.__enter__()
        self.stack.append(cm)
        return t

    def mark(self):
        return len(self.stack)

    def release(self, m):
        while len(self.stack) > m:
            self.stack.pop().__exit__(None, None, None)


def rel_bucket_np(d):
    d = np.maximum(np.asarray(d, np.int64), 0)
    ratio = (np.log(np.maximum(d, 1).astype(np.float32) / np.float32(16)) / np.float32(math.log(2048 / 16))).astype(np.float32)
    large = 16 + (ratio * np.float32(16)).astype(np.int32)
    return np.where(d < 16, d, np.minimum(large, 31))


class K:
    def __init__(self, S, NSEQ):
        self.S = S
        self.NSEQ = NSEQ
        self.nc = bass.Bass("TRN2", target_bir_lowering=False)
        self.P = Prog(self.nc)
        self.A = Alloc(self.nc)
        self.dram = {}

    def din(self, name, shape, dt=F32):
        t = self.nc.dram_tensor(name, list(shape), dt, kind="ExternalInput").ap()
        self.dram[name] = t
        return t

    def dout(self, name, shape, dt=F32):
        t = self.nc.dram_tensor(name, list(shape), dt, kind="ExternalOutput").ap()
        self.dram[name] = t
        return t

    def dscr(self, name, shape, dt=F32):
        t = self.nc.dram_tensor(name, list(shape), dt, kind="Internal").ap()
        self.dram[name] = t
        return t


def load_consts(k):
    P, A, nc = k.P, k.A, k.nc
    c = {}
    identf = A.sb([128, 128], F32, "identf")
    ident = A.sb([128, 128], BF16, "ident")
    b = Buf()
    P.op("pool", lambda e: e.memset(identf[:], 0.0), writes=[b])
    P.op("pool", lambda e: e.affine_select(out=identf[:], in_=identf[:], pattern=[[-1, 128]], compare_op=ALU.not_equal,
                                           fill=1.0, base=0, channel_multiplier=1), reads=[b], writes=[b])
    P.op("dve", lambda e: e.tensor_copy(out=ident[:], in_=identf[:]), reads=[b], writes=[b])
    c["ident"] = ident
    c["identf"] = identf
    c["ident_b"] = b
    return c


def load_weight_bf16(k, w_ap, rows, cols, gcol=None, gbuf=None, name="w"):
    P, A = k.P, k.A
    nk = rows // 128
    wt = A.sb([128, nk, cols], BF16, name)
    wb = Buf()
    CH = 2048
    m0 = A.mark()
    stg = [A.sb([128, min(CH, cols)], F32, name + "_stg") for _ in range(2)]
    sb = [Buf(), Buf()]
    i = 0
    for kc in range(nk):
        for c0 in range(0, cols, CH):
            cw = min(CH, cols - c0)
            s, b = stg[i % 2], sb[i % 2]
            P.dma(s[:, 0:cw], w_ap[kc * 128:(kc + 1) * 128, c0:c0 + cw], writes=[b], q="sp" if i % 2 == 0 else "act")
            if gcol is not None:
                P.op("dve" if i % 2 == 0 else "pool",
                     lambda e, s=s, cw=cw, kc=kc, c0=c0: e.tensor_scalar(out=wt[:, kc, c0:c0 + cw], in0=s[:, 0:cw],
                                                                         scalar1=gcol[:, kc:kc + 1], scalar2=None, op0=ALU.mult),
                     reads=[b] + ([gbuf] if gbuf else []), pw=[wb])
            else:
                P.op("dve" if i % 2 == 0 else "pool",
                     lambda e, s=s, cw=cw, kc=kc, c0=c0: e.tensor_copy(out=wt[:, kc, c0:c0 + cw], in_=s[:, 0:cw]),
                     reads=[b], pw=[wb])
            i += 1
    P.barrier()
    A.release(m0)
    return wt, wb


def load_gcol(k, g_ap, n, name="g"):
    P, A = k.P, k.A
    t = A.sb([128, n // 128], F32, name)
    b = Buf()
    P.dma(t[:], g_ap.rearrange("(kc p) -> p kc", p=128), writes=[b], allow_slow_non_contiguous=True)
    return t, b


def load_bcast(k, v_ap, n, name="bc"):
    P, A = k.P, k.A
    t = A.sb([128, n], F32, name)
    b = Buf()
    src = bass.AP(tensor=v_ap.tensor, offset=v_ap.offset, ap=[[0, 128], [1, n]])
    P.dma(t[:], src, writes=[b])
    return t, b


def rms_rstd(P, ss, rstd, n, bufs_r, bufs_w, eng="dve", scale=1.0):
    P.op(eng, lambda e: e.tensor_scalar(out=rstd, in0=ss, scalar1=1.0 / n, scalar2=EPS, op0=ALU.mult, op1=ALU.add),
         reads=bufs_r, writes=bufs_w)
    P.op("act", lambda e: e.sqrt(out=rstd, in_=rstd), reads=bufs_w, writes=bufs_w)
    P.op(eng, lambda e: e.reciprocal(out=rstd, in_=rstd), reads=bufs_w, writes=bufs_w)
    if scale != 1.0:
        P.op(eng, lambda e: e.tensor_scalar(out=rstd, in0=rstd, scalar1=scale, scalar2=None, op0=ALU.mult),
             reads=bufs_w, writes=bufs_w)


def nsa_project_phase(k, c, x_ap, T, w, dst):
    P, A = k.P, k.A
    m0 = A.mark()
    ident, ib = c["ident"], c["ident_b"]
    gcol, gb = load_gcol(k, w["attn_norm"], D, "gcol")
    wt, wb = load_weight_bf16(k, w["w_in"], D, NSA_IN, gcol, gb, "w_in")
    qg, qgb = load_bcast(k, w["q_norm"], HD, "qg")
    kg1, kg1b = load_bcast(k, w["k_norm1"], HD, "kg1")
    kg2, kg2b = load_bcast(k, w["k_norm2"], HD, "kg2")
    NB = 2
    xt = [A.sb([128, D], F32, "xt") for _ in range(NB)]
    xn = [A.sb([128, D], BF16, "xn") for _ in range(NB)]
    junk = A.sb([128, D], BF16, "junk")
    xnT = [A.sb([128, D], BF16, "xnT") for _ in range(NB)]
    z = [A.sb([128, NSA_IN], F32, "z") for _ in range(NB)]
    sq = A.sb([128, D], F32, "sq")
    qb = [A.sb([128, D], BF16, "qb") for _ in range(NB)]
    kb = [A.sb([128, 6 * 256], BF16, "kb") for _ in range(NB)]
    st = [A.sb([128, 40], F32, "st") for _ in range(NB)]
    gat = [A.sb([128, 48], F32, "gat") for _ in range(NB)]
    qTs = [A.sb([64, 16, 128], BF16, "qTs") for _ in range(NB)]
    kTs = [A.sb([64, 16, 128], BF16, "kTs") for _ in range(NB)]
    ps_t = A.ps([128, D], BF16, "ps_t")
    ps_z = [A.ps([128, 512], F32, "ps_z") for _ in range(2)]
    ps_q = [A.ps([64, 8, 128], BF16, "ps_q") for _ in range(2)]
    ps_k = [A.ps([64, 8, 128], BF16, "ps_k") for _ in range(2)]
    B = lambda: [Buf() for _ in range(NB)]
    b_xt, b_xn, b_xnT, b_z, b_qb, b_kb, b_st, b_gat, b_qTs, b_kTs = B(), B(), B(), B(), B(), B(), B(), B(), B(), B()
    b_junk, b_sq, b_pst = Buf(), Buf(), Buf()
    b_psz = [Buf(), Buf()]
    b_psq = [Buf(), Buf()]
    b_psk = [Buf(), Buf()]
    nt = T // 128
    zc = 0
    for t in range(nt):
        i = t % NB
        r0 = t * 128
        P.dma(xt[i][:], x_ap[r0:r0 + 128, :], writes=[b_xt[i]])
        P.op("act", lambda e: e.activation(out=junk[:], in_=xt[i][:], func=AF.Square, accum_out=st[i][:, 0:1]),
             reads=[b_xt[i]], writes=[b_junk, b_st[i]])
        rms_rstd(P, st[i][:, 0:1], st[i][:, 1:2], D, [b_st[i]], [b_st[i]])
        P.op("dve", lambda e: e.tensor_scalar(out=xn[i][:], in0=xt[i][:], scalar1=st[i][:, 1:2], scalar2=None, op0=ALU.mult),
             reads=[b_xt[i], b_st[i]], writes=[b_xn[i]])
        for kc in range(8):
            P.op("pe", lambda e, kc=kc: e.transpose(out=ps_t[:, kc * 128:(kc + 1) * 128], in_=xn[i][:, kc * 128:(kc + 1) * 128], identity=ident[:]),
                 reads=[b_xn[i], ib], pw=[b_pst] if kc else (), writes=() if kc else [b_pst])
        P.op("act", lambda e: e.copy(out=xnT[i][:], in_=ps_t[:]), reads=[b_pst], writes=[b_xnT[i]])
        for c0 in range(0, NSA_IN, 512):
            cw = min(512, NSA_IN - c0)
            pz, bz = ps_z[zc % 2], b_psz[zc % 2]
            zc += 1
            for kc in range(8):
                P.op("pe", lambda e, kc=kc, c0=c0, cw=cw, pz=pz: e.matmul(pz[:, 0:cw], lhsT=xnT[i][:, kc * 128:(kc + 1) * 128],
                                                                        rhs=wt[:, kc, c0:c0 + cw], start=(kc == 0), stop=(kc == 7)),
                     reads=[b_xnT[i], wb], pw=[bz] if kc else (), writes=() if kc else [bz])
            P.op("dve" if (c0 // 512) % 2 == 0 else "act",
                 (lambda e, c0=c0, cw=cw, pz=pz: e.tensor_copy(out=z[i][:, c0:c0 + cw], in_=pz[:, 0:cw])) if (c0 // 512) % 2 == 0 else
                 (lambda e, c0=c0, cw=cw, pz=pz: e.copy(out=z[i][:, c0:c0 + cw], in_=pz[:, 0:cw])),
                 reads=[bz], pw=[b_z[i]] if c0 else (), writes=() if c0 else [b_z[i]])
        zi = z[i]
        P.op("pool", lambda e: e.tensor_tensor(out=sq[:], in0=zi[:, 0:1024], in1=zi[:, 0:1024], op=ALU.mult), reads=[b_z[i]], writes=[b_sq])
        P.op("dve", lambda e: e.tensor_reduce(out=st[i][:, 2:18], in_=sq[:].rearrange("p (h d) -> p h d", d=64), axis=AX.X, op=ALU.add),
             reads=[b_sq], pw=[b_st[i]])
        P.op("pool", lambda e: e.tensor_tensor(out=sq[:, 0:256], in0=zi[:, 1536:1792], in1=zi[:, 1536:1792], op=ALU.mult), reads=[b_z[i]], writes=[b_sq])
        P.op("pool", lambda e: e.tensor_tensor(out=sq[:, 256:512], in0=zi[:, 2048:2304], in1=zi[:, 2048:2304], op=ALU.mult), reads=[b_z[i]], pw=[b_sq])
        P.op("dve", lambda e: e.tensor_reduce(out=st[i][:, 18:26], in_=sq[:, 0:512].rearrange("p (h d) -> p h d", d=64), axis=AX.X, op=ALU.add),
             reads=[b_sq], pw=[b_st[i]])
        rms_rstd(P, st[i][:, 2:18], st[i][:, 2:18], HD, [b_st[i]], [b_st[i]], scale=HD ** -0.5)
        rms_rstd(P, st[i][:, 18:26], st[i][:, 18:26], HD, [b_st[i]], [b_st[i]])
        P.op("dve", lambda e: e.tensor_tensor(out=sq[:].rearrange("p (h d) -> p h d", d=64), in0=zi[:, 0:1024].rearrange("p (h d) -> p h d", d=64),
                                              in1=st[i][:, 2:18].unsqueeze(2).to_broadcast([128, 16, 64]), op=ALU.mult),
             reads=[b_z[i], b_st[i]], writes=[b_sq])
        P.op("pool", lambda e: e.tensor_tensor(out=qb[i][:].rearrange("p (h d) -> p h d", d=64), in0=sq[:].rearrange("p (h d) -> p h d", d=64),
                                               in1=qg[:].unsqueeze(1).to_broadcast([128, 16, 64]), op=ALU.mult),
             reads=[b_sq, qgb], writes=[b_qb[i]])
        for (c0, sc, g_t, g_b) in ((1536, 18, kg1, kg1b), (2048, 22, kg2, kg2b)):
            P.op("dve", lambda e, c0=c0, sc=sc: e.tensor_tensor(out=zi[:, c0:c0 + 256].rearrange("p (h d) -> p h d", d=64),
                                                                in0=zi[:, c0:c0 + 256].rearrange("p (h d) -> p h d", d=64),
                                                                in1=st[i][:, sc:sc + 4].unsqueeze(2).to_broadcast([128, 4, 64]), op=ALU.mult),
                 reads=[b_st[i]], writes=[b_z[i]])
            P.op("dve", lambda e, c0=c0, g_t=g_t: e.tensor_tensor(out=zi[:, c0:c0 + 256].rearrange("p (h d) -> p h d", d=64),
                                                                  in0=zi[:, c0:c0 + 256].rearrange("p (h d) -> p h d", d=64),
                                                                  in1=g_t[:].unsqueeze(1).to_broadcast([128, 4, 64]), op=ALU.mult),
                 reads=[g_b], writes=[b_z[i]])
        P.op("act", lambda e: e.activation(out=gat[i][:], in_=zi[:, 2560:2608], func=AF.Sigmoid), reads=[b_z[i]], writes=[b_gat[i]])
        P.op("act", lambda e: e.copy(out=kb[i][:], in_=zi[:, 1024:2560]), reads=[b_z[i]], writes=[b_kb[i]])
        db = dst["bufs"]
        P.dma(dst["rows"][r0:r0 + 128, :], zi[:, 1024:2048], reads=[b_z[i]], writes=[db["rows"][t]], q="pool")
        for (ti, ap, lo, hi) in dst.get("win", []):
            if ti == t:
                P.dma(ap, zi[lo:hi, 2048:2560], reads=[b_z[i]], pw=[db["win"][t]], q="pool")
        P.dma(dst["gates"][r0:r0 + 128, :], gat[i][:], reads=[b_gat[i]], writes=[db["gates"][t]], q="pool")
        P.dma(dst["vs"][r0:r0 + 128, :], kb[i][:, 768:1024], reads=[b_kb[i]], writes=[db["vs"][t]], q="pool")
        P.dma(dst["vw"][r0:r0 + 128, :], kb[i][:, 1280:1536], reads=[b_kb[i]], writes=[db["vw"][t]], q="pool")
        for hh in range(2):
            for h8 in range(8):
                h = hh * 8 + h8
                P.op("pe", lambda e, h=h, h8=h8, hh=hh: e.transpose(out=ps_q[hh][:, h8, :], in_=qb[i][:, h * 64:(h + 1) * 64], identity=ident[:]),
                     reads=[b_qb[i], ib], pw=[b_psq[hh]] if h8 else (), writes=() if h8 else [b_psq[hh]])
            P.op("dve" if hh == 0 else "act",
                 (lambda e, hh=hh: e.tensor_copy(out=qTs[i][:, hh * 8:(hh + 1) * 8, :], in_=ps_q[hh][:])) if hh == 0 else
                 (lambda e, hh=hh: e.copy(out=qTs[i][:, hh * 8:(hh + 1) * 8, :], in_=ps_q[hh][:])),
                 reads=[b_psq[hh]], pw=[b_qTs[i]] if hh else (), writes=() if hh else [b_qTs[i]])
        P.dma(dst["qT"][:, :, r0:r0 + 128].rearrange("h d t -> d h t"), qTs[i][:], reads=[b_qTs[i]], writes=[db["qT"][t]], q="sp")
        srcs = [0, 256, 512, 1024]
        for hh in range(2):
            for h8 in range(8):
                j = hh * 8 + h8
                off = srcs[j // 4] + (j % 4) * 64
                P.op("pe", lambda e, off=off, h8=h8, hh=hh: e.transpose(out=ps_k[hh][:, h8, :], in_=kb[i][:, off:off + 64], identity=ident[:]),
                     reads=[b_kb[i], ib], pw=[b_psk[hh]] if h8 else (), writes=() if h8 else [b_psk[hh]])
            P.op("dve" if hh == 0 else "act",
                 (lambda e, hh=hh: e.tensor_copy(out=kTs[i][:, hh * 8:(hh + 1) * 8, :], in_=ps_k[hh][:])) if hh == 0 else
                 (lambda e, hh=hh: e.copy(out=kTs[i][:, hh * 8:(hh + 1) * 8, :], in_=ps_k[hh][:])),
                 reads=[b_psk[hh]], pw=[b_kTs[i]] if hh else (), writes=() if hh else [b_kTs[i]])
        for j, nm in enumerate(("kcT", "vcT", "ksT", "kwT")):
            P.dma(dst[nm][:, :, r0:r0 + 128].rearrange("h d t -> d h t"), kTs[i][:, j * 4:(j + 1) * 4, :], reads=[b_kTs[i]], writes=[db[nm][t]], q="sp")
    P.barrier()
    A.release(m0)


N_CORES = 8
S_FULL = 8192
NSEQ = 16


def nsa_weights(k, l):
    g = k.dram
    return {"attn_norm": g["a_attn_norm"][l], "w_in": g["a_w_in"][l], "q_norm": g["a_q_norm"][l],
            "k_norm0": g["a_k_norm"][l, 0], "k_norm1": g["a_k_norm"][l, 1], "k_norm2": g["a_k_norm"][l, 2],
            "cmp_pe": g["a_cmp_pe"][l], "cmp_w1": g["a_cmp_w1"][l], "cmp_w2": g["a_cmp_w2"][l], "w_out": g["a_w_out"][l]}


def make_scratch(k, pfx, T):
    d = {"qT": k.dscr(pfx + "qT", [16, 64, T], BF16), "vs": k.dscr(pfx + "vs", [T, 256], BF16), "vw": k.dscr(pfx + "vw", [T, 256], BF16),
         "gates": k.dscr(pfx + "gates", [T, 48], F32)}
    for nm in ("kcT", "vcT", "ksT", "kwT"):
        d[nm] = k.dscr(pfx + nm, [4, 64, T], BF16)
    return d


def fresh_bufs(T):
    nt = T // 128
    return {nm: [Buf() for _ in range(nt)] for nm in ("rows", "gates", "qT", "vs", "vw", "kcT", "vcT", "ksT", "kwT", "win")}


def build(S=S_FULL, stage=5, npool=2560):
    k = K(S, NSEQ)
    P, A = k.P, k.A
    TS = NSEQ * 8
    ncmp = S // 16 - 1
    nct = (ncmp + 127) // 128
    k.din("xp", [S, D])
    k.din("xs", [TS, D])
    k.din("cache_win", [2, NSEQ, 512, 512])
    k.din("cache_dil", [NSEQ, 2048, 512])
    k.din("rel_bias", [32, 16])
    k.din("a_attn_norm", [2, D]); k.din("a_w_in", [2, D, NSA_IN]); k.din("a_q_norm", [2, HD]); k.din("a_k_norm", [2, 3, HD])
    k.din("a_cmp_pe", [2, 2, 32, HD]); k.din("a_cmp_w1", [2, 2, 32 * HD, 128]); k.din("a_cmp_w2", [2, 2, 128, HD]); k.din("a_w_out", [2, D, D])
    k.din("mlp_norm", [4, D]); k.din("mlp_w1", [4, D, DFF]); k.din("mlp_w2", [4, DFF, D])
    k.din("kv_norm", [D]); k.din("w_kv_shared", [D, 512]); k.din("k_norm_shared", [HD])
    names = ["causal", "win", "cmp", "dil1", "dil4", "dil16", "sd1", "sd4", "sd16"]
    for nm in names:
        k.din("oh_" + nm, list(onehot_np(nm).shape))
    k.din("selmap", [nct * 128, 128])
    k.din("b_attn_norm", [2, D]); k.din("b_w_q", [2, D, 3072]); k.din("b_q_norm", [2, 3, HD]); k.din("b_w_out", [2, D, D])
    if stage >= 3:
        k.din("pool0", [npool * 128, 1024]); k.din("pool1", [npool * 128, 1024]); k.din("pt", [NSEQ * NPG], I32); k.din("piota", [128, 1], I32); k.din("selmap_s", [128, 33])
    yp = k.dout("yp", [S, D]); ys = k.dout("ys", [TS, D])
    rows_p = k.dout("rows_p", [2, S, 1024]); rows_s = k.dout("rows_s", [2, TS, 1024])
    win_p = k.dout("win_p", [2, 512, 512]); win_s = k.dout("win_s", [2, NSEQ, 512, 512])
    dil_p = k.dout("dil_p", [2048, 512]); dil_s = k.dout("dil_s", [NSEQ, 2048, 512])
    c = load_consts(k)
    make_antiident(k, c)
    g = k.dram
    cb = Buf()
    for l in range(2):
        for s in range(NSEQ):
            P.dma(win_s[l, s, 0:504, :], g["cache_win"][l, s, 8:512, :], pw=[cb], q="act")
    for s in range(NSEQ):
        P.dma(dil_s[s, 0:2040, :], g["cache_dil"][s, 8:2048, :], pw=[cb], q="act")
    ftabs = build_ftabs(k, c, g["rel_bias"], names)
    W, W_b = make_W(k, c, max(S, 2048))
    scr_p = make_scratch(k, "p_", S)
    scr_s = make_scratch(k, "s_", TS)
    oc = k.dscr("oc", [S, D], BF16); osw = k.dscr("osw", [S, D], BF16)
    xa = k.dscr("xa", [S, D]); xb = k.dscr("xb", [S, D])
    xp_cur, xs_cur = g["xp"], g["xs"]
    xp_bufs = []
    if stage >= 3:
        ball, ball_b = build_sample_bias(k, c, ftabs)
        idx, idx_b = build_page_idx(k, c, g["pt"], NSEQ)
        mk = k.dout if DEBUG_SAMPLE else k.dscr
        so_c = mk("so_c", [TS, D]); so_s = mk("so_s", [TS, D]); so_w = mk("so_w", [TS, D])
        sog = k.dscr("sog", [TS, D], BF16)
        xsa = k.dscr("xsa", [TS, D]); xsb = k.dscr("xsb", [TS, D])
    for l in range(2):
        w = nsa_weights(k, l)
        dst = dict(scr_p)
        dst["rows"] = rows_p[l]
        nt = S // 128
        nwin = min(4, nt)
        dst["win"] = [(nt - nwin + j, win_p[l, (4 - nwin + j) * 128:(4 - nwin + j + 1) * 128, :], 0, 128) for j in range(nwin)]
        dst["bufs"] = fresh_bufs(S)
        nsa_project_phase(k, c, xp_cur, S, w, dst)
        dsts = dict(scr_s)
        dsts["rows"] = rows_s[l]
        dsts["win"] = [(0, win_s[l, s_, 504:512, :], s_ * 8, s_ * 8 + 8) for s_ in range(NSEQ)]
        dsts["bufs"] = fresh_bufs(TS)
        nsa_project_phase(k, c, xs_cur, TS, w, dsts)
        if stage < 2:
            break
        dst["bufs_all"] = []
        nsa_attention(k, c, ftabs, w, dst, S, g["selmap"], oc, osw, W, W_b)
        outproj_phase(k, c, xp_cur, [oc, osw], [], w["w_out"], xa, S)
        mlp_phase(k, c, xa, [], g["mlp_norm"][l], g["mlp_w1"][l], g["mlp_w2"][l], [xb] if l == 0 else [xa, yp], S)
        xp_cur = xb if l == 0 else xa
        if stage >= 3:
            dsts["bufs_all"] = []
            sample_nsa_attention(k, c, w, dsts, NSEQ, g["pool%d" % l], g["cache_win"][l], idx, idx_b, ball, ball_b, W, W_b, so_c, so_s, so_w, g["selmap_s"])
            sample_gate_sum(k, c, dsts["gates"], so_c, so_s, so_w, sog)
            outproj_phase(k, c, xs_cur, [sog], [], w["w_out"], xsa, TS, wname="w_out_s")
            mlp_phase(k, c, xsa, [], g["mlp_norm"][l], g["mlp_w1"][l], g["mlp_w2"][l], [xsb] if l == 0 else [xsa, ys], TS)
            xs_cur = xsb if l == 0 else xsa
    if stage >= 3:
        scr_sd = {"vd": k.dscr("s_vd", [TS, 256], BF16), "kdT": k.dscr("s_kdT", [4, 64, TS], BF16)}
        shared_kv_phase(k, c, xs_cur, TS, g["kv_norm"], g["w_kv_shared"], g["k_norm_shared"],
                        lambda t: [(dil_s[s_, 2040:2048, :], s_ * 8, s_ * 8 + 8) for s_ in range(NSEQ)], scr_sd)
    if stage >= 5:
        balld, balld_b, sd_offs = build_sample_dil_bias(k, c, ftabs)
        qdT_s = k.dscr("s_qdT", [3, 16, 64, TS], BF16)
        for i in range(2):
            dil_project_phase(k, c, xs_cur, TS, g["b_attn_norm"][i], g["b_w_q"][i], g["b_q_norm"][i], qdT_s)
            sample_dil_attention(k, c, scr_sd, qdT_s, NSEQ, g["cache_dil"], balld, balld_b, sd_offs, so_c)
            cast_phase(k, so_c, sog, TS)
            x1 = xsb if xs_cur is xsa else xsa
            outproj_phase(k, c, xs_cur, [sog], [], g["b_w_out"][i], x1, TS, wname="b_w_out_s")
            x2 = xs_cur
            mlp_phase(k, c, x1, [], g["mlp_norm"][2 + i], g["mlp_w1"][2 + i], g["mlp_w2"][2 + i], [x2] if i == 0 else [x2, ys], TS)
            xs_cur = x2
    if stage >= 2:
        ntl = S // 128
        nd = min(16, ntl)
        scr_d = {"vd": k.dscr("p_vd", [S, 256], BF16), "kdT": k.dscr("p_kdT", [4, 64, S], BF16)}
        shared_kv_phase(k, c, xp_cur, S, g["kv_norm"], g["w_kv_shared"], g["k_norm_shared"],
                        lambda t: [(dil_p[(16 - nd + t - (ntl - nd)) * 128:(16 - nd + t - (ntl - nd) + 1) * 128, :], 0, 128)] if t >= ntl - nd else [], scr_d)
    if stage >= 4:
        qdT = k.dscr("p_qdT", [3, 16, 64, S], BF16)
        accd = k.dscr("p_accd", [3, 16, S, 65], F32)
        for i in range(2):
            dil_project_phase(k, c, xp_cur, S, g["b_attn_norm"][i], g["b_w_q"][i], g["b_q_norm"][i], qdT)
            dil_attention_prompt(k, c, ftabs, scr_d, qdT, S, accd)
            dil_merge(k, c, accd, S, oc)
            x1 = xb if xp_cur is xa else xa
            outproj_phase(k, c, xp_cur, [oc], [], g["b_w_out"][i], x1, S, wname="b_w_out")
            x2 = xp_cur
            mlp_phase(k, c, x1, [], g["mlp_norm"][2 + i], g["mlp_w1"][2 + i], g["mlp_w2"][2 + i], [x2] if i == 0 else [x2, yp], S)
            xp_cur = x2
    P.barrier()
    return k


def const_inputs(S):
    ncmp = S // 16 - 1
    nct = (ncmp + 127) // 128
    d = {"oh_" + nm: onehot_np(nm) for nm in ("causal", "win", "cmp", "dil1", "dil4", "dil16", "sd1", "sd4", "sd16")}
    smn = np.zeros((nct * 128, 128), np.float32)
    nsel = min(128, S // 64)
    smn[:ncmp, :nsel] = selmap_np(ncmp, nsel)
    d["selmap"] = smn
    d["piota"] = np.arange(128, dtype=np.int32).reshape(128, 1)
    d["selmap_s"] = selmap_np(128, 33)
    return d


def _prep_inputs(inputs):
    f = lambda a: np.ascontiguousarray(np.asarray(a))
    shared = {
        "a_attn_norm": f(inputs["a_attn_norm"]), "a_w_in": f(inputs["a_w_in"]), "a_q_norm": f(inputs["a_q_norm"]),
        "a_k_norm": f(inputs["a_k_norm"]), "a_cmp_pe": f(inputs["a_cmp_pe"]),
        "a_cmp_w1": f(inputs["a_cmp_w1"]).reshape(2, 2, 32 * HD, 128), "a_cmp_w2": f(inputs["a_cmp_w2"]), "a_w_out": f(inputs["a_w_out"]),
        "rel_bias": f(inputs["rel_bias"]), "mlp_norm": f(inputs["mlp_norm"]), "mlp_w1": f(inputs["mlp_w1"]), "mlp_w2": f(inputs["mlp_w2"]),
        "kv_norm": f(inputs["kv_norm"]), "w_kv_shared": f(inputs["w_kv_shared"]), "k_norm_shared": f(inputs["k_norm_shared"]),
        "b_attn_norm": f(inputs["b_attn_norm"]), "b_w_q": f(inputs["b_w_q"]), "b_q_norm": f(inputs["b_q_norm"]), "b_w_out": f(inputs["b_w_out"]),
    }
    shared.update(const_inputs(S_FULL))
    maps = []
    xs = f(inputs["x_sample"]).reshape(N_CORES, NSEQ * 8, D)
    cw = f(inputs["cache_win_kv"]).reshape(2, N_CORES, NSEQ, 512, 512)
    cd = f(inputs["cache_dil_kv"]).reshape(N_CORES, NSEQ, 2048, 512)
    pool = f(inputs["cache_nsa_kv"]).reshape(2, 2560 * 128, 1024)
    shared["pool0"] = pool[0]
    shared["pool1"] = pool[1]
    pt = f(inputs["page_table"]).astype(np.int32).reshape(N_CORES, NSEQ * NPG)
    for cidx in range(N_CORES):
        m = dict(shared)
        m["xp"] = f(inputs["x_prompt"][cidx % 2])
        m["xs"] = xs[cidx]
        m["cache_win"] = np.ascontiguousarray(cw[:, cidx])
        m["cache_dil"] = cd[cidx]
        m["pt"] = pt[cidx]
        maps.append(m)
    return maps


def kernel(**inputs):
    S = S_FULL
    k = build(S, stage=5)
    maps = _prep_inputs(inputs)
    res = run_bass_kernel_spmd(k.nc, maps, core_ids=list(range(N_CORES)))
    r = res.results
    cat = lambda nm: [np.asarray(r[cidx][nm]) for cidx in range(N_CORES)]
    yp = np.stack(cat("yp")[:2]).reshape(2, S, D)
    ys = np.concatenate(cat("ys"), 0).reshape(128, 8, D)
    rp = np.stack(cat("rows_p")[:2], axis=1).reshape(2, 2, S, 4, 4, 64)
    rs = np.concatenate([a.reshape(2, NSEQ, 8, 4, 4, 64) for a in cat("rows_s")], axis=1)
    wp = np.stack(cat("win_p")[:2], axis=1).reshape(2, 2, 512, 2, 4, 64)
    ws = np.concatenate([a.reshape(2, NSEQ, 512, 2, 4, 64) for a in cat("win_s")], axis=1)
    dp = np.stack(cat("dil_p")[:2]).reshape(2, 2048, 2, 4, 64)
    ds = np.concatenate([a.reshape(NSEQ, 2048, 2, 4, 64) for a in cat("dil_s")], axis=0)
    return (yp.astype(np.float32), ys.astype(np.float32), rp.astype(np.float32), rs.astype(np.float32),
            wp.astype(np.float32), ws.astype(np.float32), dp.astype(np.float32), ds.astype(np.float32))


def selmap_np(ncb, nsel):
    c = np.arange(ncb)
    cs, ce = c * 16, c * 16 + 31
    j0 = np.arange(nsel) * 64
    return ((cs[:, None] <= j0[None] + 63) & (ce[:, None] >= j0[None])).astype(np.float32)


def compress_setup(k, c, w):
    P, A = k.P, k.A
    cw = {}
    m0 = A.mark()
    w1 = [A.sb([64, 32, 128], BF16, "cw1") for _ in range(2)]
    w2 = [A.sb([128, 64], BF16, "cw2") for _ in range(2)]
    hb = A.sb([128, 2], F32, "chb")
    kg0, kg0b = load_bcast(k, w["k_norm0"], HD, "kg0")
    b = Buf()
    m1 = A.mark()
    stg = A.sb([64, 32, 128], F32, "cw1s")
    stg2 = A.sb([128, 64], F32, "cw2s")
    peT = A.sb([64, 32], F32, "peT")
    peTb = A.sb([64, 32], BF16, "peTb")
    ps = A.ps([128, 2], F32, "ps_hb")
    bs, bs2, bp, bps = Buf(), Buf(), Buf(), Buf()
    for i in range(2):
        P.dma(stg[:], w["cmp_w1"][i].rearrange("(l d) h -> d l h", d=64), writes=[bs])
        P.op("dve", lambda e: e.tensor_copy(out=w1[i][:], in_=stg[:]), reads=[bs], pw=[b])
        P.dma(stg2[:], w["cmp_w2"][i], writes=[bs2])
        P.op("dve", lambda e: e.tensor_copy(out=w2[i][:], in_=stg2[:]), reads=[bs2], pw=[b])
        P.dma(peT[:], w["cmp_pe"][i].rearrange("l d -> d l"), writes=[bp], allow_slow_non_contiguous=True)
        P.op("dve", lambda e: e.tensor_copy(out=peTb[:], in_=peT[:]), reads=[bp], writes=[bp])
        for l in range(32):
            P.op("pe", lambda e: e.matmul(ps[:, i:i + 1], lhsT=w1[i][:, l, :], rhs=peTb[:, l:l + 1], start=(l == 0), stop=(l == 31)),
                 reads=[b, bp], pw=[bps])
    P.op("dve", lambda e: e.tensor_copy(out=hb[:], in_=ps[:]), reads=[bps], pw=[b])
    P.barrier()
    A.release(m1)
    cw.update(w1=w1, w2=w2, hb=hb, kg0=kg0, b=b, kg0b=kg0b, mark=m0)
    return cw


def compress_run(k, c, cw, kT_sb, kT_buf, ncmp, i, out_fn):
    P, A = k.P, k.A
    st = cw["st"]
    hid_ps, hid_b = st["hid_ps"], st["hid_b"]
    sil, sil_b = st["sil"], st["sil_b"]
    o_ps, o_b = st["o_ps"], st["o_b"]
    n = st["n"]
    st["n"] += 1
    hp, hpb = hid_ps[n % 2], hid_b[n % 2]
    sl, slb = sil[n % 2], sil_b[n % 2]
    for l in range(32):
        P.op("pe", lambda e: e.matmul(hp[:, 0:ncmp], lhsT=cw["w1"][i][:, l, :], rhs=kT_sb[:, l:l + 16 * (ncmp - 1) + 1:16], start=(l == 0), stop=(l == 31)),
             reads=[cw["b"], kT_buf], pw=[hpb] if l else (), writes=() if l else [hpb])
    P.op("act", lambda e: e.activation(out=sl[:, 0:ncmp], in_=hp[:, 0:ncmp], func=AF.Silu, bias=cw["hb"][:, i:i + 1]),
         reads=[hpb, cw["b"]], writes=[slb])
    for ct in range((ncmp + 127) // 128):
        nrow = min(128, ncmp - ct * 128)
        m = st["m"]
        st["m"] += 1
        op_, opb = o_ps[m % 2], o_b[m % 2]
        P.op("pe", lambda e: e.matmul(op_[0:nrow, :], lhsT=sl[:, ct * 128:ct * 128 + nrow], rhs=cw["w2"][i][:], start=True, stop=True),
             reads=[slb, cw["b"]], writes=[opb])
        out_fn(ct, nrow, op_, opb)


def compress_state(k, misc=None):
    A = k.A
    st = {"hid_ps": [A.ps([128, 512], F32, "hid") for _ in range(2)], "hid_b": [Buf(), Buf()],
          "sil": [A.sb([128, 512], BF16, "sil") for _ in range(2)], "sil_b": [Buf(), Buf()],
          "o_ps": [A.ps([128, 64], F32, "cps") for _ in range(2)] if misc is None else [misc[:, 0:64], misc[:, 64:128]],
          "o_b": [Buf(), Buf()], "n": 0, "m": 0}
    return st


def compress_prompt(k, c, w, scr, T, ckT, ckT_b, cvx, cvx_b, selmap_ap):
    P, A = k.P, k.A
    ncmp = T // 16 - 1
    nct = (ncmp + 127) // 128
    m0 = A.mark()
    cw = compress_setup(k, c, w)
    cw["st"] = compress_state(k)
    kT = [A.sb([64, T], BF16, "ckT_in") for _ in range(2)]
    kTb = [Buf(), Buf()]
    tmp = A.sb([128, 64], F32, "ctmp")
    tmpb = A.sb([128, 64], BF16, "ctmpb")
    junk = A.sb([128, 64], F32, "cjunk")
    stt = A.sb([128, 2], F32, "cstt")
    smf = A.sb([128, nct, 128], F32, "smf")
    ps_tr = A.ps([64, 128], BF16, "ps_ctr")
    b_tmp, b_stt, b_ptr, b_sm, b_junk = Buf(), Buf(), Buf(), Buf(), Buf()
    ident, ib = c["ident"], c["ident_b"]
    P.op("pool", lambda e: e.memset(cvx[:], 0.0), writes=[cvx_b])
    P.op("pool", lambda e: e.memset(ckT[:], 0.0), writes=[ckT_b])
    P.dma(smf[:], selmap_ap.rearrange("(ct p) j -> p ct j", p=128), writes=[b_sm])
    for g in range(4):
        P.op("pool", lambda e: e.memset(cvx[:, :, g, 64:65], 1.0), pw=[cvx_b])
        P.op("dve", lambda e: e.tensor_copy(out=cvx[:, :, g, 65:193], in_=smf[:]), reads=[b_sm], pw=[cvx_b])
    n = 0
    for i in range(2):
        for g in range(4):
            kt, ktb = kT[n % 2], kTb[n % 2]
            n += 1
            P.dma(kt[:], scr["kcT" if i == 0 else "vcT"][g], reads=scr["bufs_all"], writes=[ktb])

            def out_fn(ct, nrow, op_, opb):
                if i == 1:
                    P.op("dve", lambda e: e.tensor_copy(out=cvx[0:nrow, ct, g, 0:64], in_=op_[0:nrow, :]), reads=[opb], pw=[cvx_b])
                    return
                P.op("act", lambda e: e.activation(out=junk[0:nrow, :], in_=op_[0:nrow, :], func=AF.Square, accum_out=stt[0:nrow, 0:1]),
                     reads=[opb], writes=[b_junk, b_stt])
                rms_rstd(P, stt[0:nrow, 0:1], stt[0:nrow, 1:2], HD, [b_stt], [b_stt])
                P.op("dve", lambda e: e.tensor_scalar(out=tmp[0:nrow, :], in0=op_[0:nrow, :], scalar1=stt[0:nrow, 1:2], scalar2=None, op0=ALU.mult),
                     reads=[opb, b_stt], writes=[b_tmp])
                P.op("dve", lambda e: e.tensor_tensor(out=tmpb[0:nrow, :], in0=tmp[0:nrow, :], in1=cw["kg0"][0:nrow, :], op=ALU.mult),
                     reads=[cw["kg0b"]], writes=[b_tmp])
                P.op("pe", lambda e: e.transpose(out=ps_tr[:, 0:nrow], in_=tmpb[0:nrow, :], identity=ident[0:nrow, 0:nrow]),
                     reads=[b_tmp, ib], writes=[b_ptr])
                P.op("act", lambda e: e.copy(out=ckT[:, g, ct * 128:ct * 128 + nrow], in_=ps_tr[:, 0:nrow]), reads=[b_ptr], pw=[ckT_b])

            compress_run(k, c, cw, kt, ktb, ncmp, i, out_fn)
    P.barrier()
    A.release(m0)


VARIANTS = {
    "causal": (1, 384, 2560, 0, 10 ** 9, 1),
    "win": (1, 384, 1408, 0, 512, 1),
    "cmp": (16, 31, 4096, 0, 10 ** 9, 1),
    "dil1": (1, 384, 1024, 0, 128, 1),
    "dil4": (1, 384, 1024, 0, 128, 4),
    "dil16": (1, 384, 1024, 0, 128, 16),
    "sd1": (1, 384, 2560, 0, 128, 1, 1),
    "sd4": (1, 384, 2560, 0, 512, 1, 4),
    "sd16": (1, 384, 2560, 0, 2048, 1, 16),
}


def variant_geom(name):
    s, OFF, U, lo, hi, mul = VARIANTS[name][:6]
    off = OFF + 127 * s
    L = off - OFF + U
    L = (L + 511) // 512 * 512
    base = off - OFF - 127 * s
    return s, OFF, U, off, L, base


def onehot_np(name):
    s, OFF, U, lo, hi, mul = VARIANTS[name][:6]
    mod = VARIANTS[name][6] if len(VARIANTS[name]) > 6 else 1
    _, _, _, off, L, _ = variant_geom(name)
    delta = np.arange(L) - off
    valid = (delta >= lo) & (delta <= hi) & (delta % mod == 0)
    bk = rel_bucket_np(np.maximum(delta, 0) * mul)
    oh = np.zeros((33, L), np.float32)
    oh[bk[valid], np.nonzero(valid)[0]] = 1.0
    oh[32, ~valid] = 1.0
    return oh


def build_ftabs(k, c, rel_bias_ap, names):
    P, A = k.P, k.A
    m0 = A.mark()
    rbx = A.sb([33, 16], F32, "rbx")
    b_rb = Buf()
    P.op("pool", lambda e: e.memset(rbx[:], NEG), writes=[b_rb])
    P.dma(rbx[0:32, :], rel_bias_ap, pw=[b_rb])
    oh = [A.sb([33, 512], F32, "oh") for _ in range(2)]
    ob = [A.sb([16, 512], F32, "ftc") for _ in range(2)]
    ps = [A.ps([16, 512], F32, "ps_ft") for _ in range(2)]
    b_oh, b_ob, b_ps = [Buf(), Buf()], [Buf(), Buf()], [Buf(), Buf()]
    out = {}
    n = 0
    for nm in names:
        _, _, _, off, L, _ = variant_geom(nm)
        ft = k.dscr("ftab_" + nm, [16, L], F32)
        fb = Buf()
        src = k.dram["oh_" + nm]
        for c0 in range(0, L, 512):
            i = n % 2
            n += 1
            P.dma(oh[i][:], src[:, c0:c0 + 512], writes=[b_oh[i]])
            P.op("pe", lambda e: e.matmul(ps[i][:], lhsT=rbx[:], rhs=oh[i][:], start=True, stop=True), reads=[b_rb, b_oh[i]], writes=[b_ps[i]])
            P.op("dve", lambda e: e.tensor_copy(out=ob[i][:], in_=ps[i][:]), reads=[b_ps[i]], writes=[b_ob[i]])
            P.dma(ft[:, c0:c0 + 512], ob[i][:], reads=[b_ob[i]], pw=[fb], q="pool")
        out[nm] = (ft, fb)
    P.barrier()
    A.release(m0)
    return out


def make_antiident(k, c):
    P, A = k.P, k.A
    J = A.sb([128, 128], F32, "antiI")
    b = Buf()
    P.op("pool", lambda e: e.memset(J[:], 0.0), writes=[b])
    P.op("pool", lambda e: e.affine_select(out=J[:], in_=J[:], pattern=[[1, 128]], compare_op=ALU.not_equal,
                                           fill=1.0, base=-127, channel_multiplier=1), reads=[b], writes=[b])
    c["J"] = J
    c["J_b"] = b


def build_strip(k, c, ftabs, nm, h, strip, strip_b, stage, stage_b, ps_list, ps_bufs, U=None):
    P = k.P
    s, OFF, U0, off, L, base = variant_geom(nm)
    U = U or U0
    ft, fb = ftabs[nm]
    src = bass.AP(tensor=ft.tensor, offset=ft.offset + h * L + base, ap=[[s, 128], [1, U]])
    P.dma(stage[:, 0:U], src, reads=[fb], writes=[stage_b])
    for n, c0 in enumerate(range(0, U, 512)):
        cw = min(512, U - c0)
        ps, pb = ps_list[n % len(ps_list)], ps_bufs[n % len(ps_list)]
        P.op("pe", lambda e: e.matmul(ps[:, 0:cw], lhsT=c["J"][:], rhs=stage[:, c0:c0 + cw], start=True, stop=True),
             reads=[stage_b, c["J_b"]], writes=[pb])
        P.op("act" if n % 2 else "dve",
             (lambda e: e.copy(out=strip[:, c0:c0 + cw], in_=ps[:, 0:cw])) if n % 2 else (lambda e: e.tensor_copy(out=strip[:, c0:c0 + cw], in_=ps[:, 0:cw])),
             reads=[pb], pw=[strip_b] if n else (), writes=() if n else [strip_b])


def make_W(k, c, T):
    P, A = k.P, k.A
    W = A.sb([128, T], BF16, "W")
    m0 = A.mark()
    Wf = A.sb([128, T], F32, "Wf")
    b = Buf()
    P.op("pool", lambda e: e.memset(Wf[:], 1.0), writes=[b])
    P.op("pool", lambda e: e.affine_select(out=Wf[:], in_=Wf[:], pattern=[[1, T]], compare_op=ALU.is_ge, fill=0.0, base=0, channel_multiplier=-64),
         reads=[b], writes=[b])
    P.op("pool", lambda e: e.affine_select(out=Wf[:], in_=Wf[:], pattern=[[-1, T]], compare_op=ALU.is_ge, fill=0.0, base=63, channel_multiplier=64),
         reads=[b], writes=[b])
    P.op("dve", lambda e: e.tensor_copy(out=W[:], in_=Wf[:]), reads=[b], writes=[b])
    P.barrier()
    A.release(m0)
    return W, b


def attn_tiles(k, st, lhsT_fn, q_ap, strip, strip_u0_fn, vx_fn, kts, acc, acc_b, ncol, bufs_r, mask_fn=None, qw=512):
    P = k.P
    nk = len(kts)
    for n, kt in enumerate(kts):
        i = st["n"] % NPIPE
        st["n"] += 1
        sp, spb = st["s_ps"][i], st["s_b"][i]
        tmp, tb = st["tmp"][i], st["tmp_b"][i]
        pt, pb = st["pt"][i], st["pt_b"][i]
        P.op("pe", lambda e: e.matmul(sp[:, 0:qw], lhsT=lhsT_fn(kt), rhs=q_ap, start=True, stop=(mask_fn is None)), reads=bufs_r, writes=[spb])
        if mask_fn is not None:
            ml, mr = mask_fn(kt)
            P.op("pe", lambda e: e.matmul(sp[:, 0:qw], lhsT=ml, rhs=mr, start=False, stop=True), reads=bufs_r, pw=[spb])
        u0 = strip_u0_fn(kt)
        P.op("dve", lambda e: e.tensor_tensor(out=tmp[:, 0:qw], in0=sp[:, 0:qw], in1=strip[:, u0:u0 + qw], op=ALU.add), reads=[spb] + bufs_r, writes=[tb])
        P.op("act", lambda e: e.activation(out=pt[:, 0:qw], in_=tmp[:, 0:qw], func=AF.Exp), reads=[tb], writes=[pb])
        for qs in range(qw // 128):
            P.op("pe", lambda e: e.matmul(acc[qs], lhsT=pt[:, qs * 128:(qs + 1) * 128], rhs=vx_fn(kt), start=(n == 0 and qs % 2 == 0), stop=(n == nk - 1), skip_group_check=True),
                 reads=[pb] + bufs_r, pw=[acc_b] if (n or qs) else (), writes=() if (n or qs) else [acc_b])


NPIPE = 4


def attn_state(k):
    A = k.A
    return {"n": 0, "s_ps": [A.ps([128, 512], F32, "s_ps") for _ in range(NPIPE)], "s_b": [Buf() for _ in range(NPIPE)],
            "tmp": [A.sb([128, 512], F32, "atmp") for _ in range(NPIPE)], "tmp_b": [Buf() for _ in range(NPIPE)],
            "pt": [A.sb([128, 512], BF16, "apt") for _ in range(NPIPE)], "pt_b": [Buf() for _ in range(NPIPE)]}


def load_gates(k, scr, T):
    P, A = k.P, k.A
    gat = A.sb([128, T // 128, 48], F32, "gat_all")
    b = Buf()
    for k0 in range(0, T // 128, 8):
        k1 = min(T // 128, k0 + 8)
        P.dma(gat[:, k0:k1, :], scr["gates"][k0 * 128:k1 * 128, :].rearrange("(tt p) c -> p tt c", p=128), reads=scr["bufs_all"], pw=[b])
    return gat, b


def nsa_pass1(k, c, ftabs, scr, T, g, ckT, ckT_b, cvx, cvx_b, gat, gat_b, imp, imp_b, oc_dram, oc_bufs):
    P, A = k.P, k.A
    m0 = A.mark()
    ncmp = T // 16 - 1
    nct = (ncmp + 127) // 128
    st = attn_state(k)
    qT = [A.sb([64, T], BF16, "qT_h") for _ in range(2)]
    qTb = [Buf(), Buf()]
    U = min(VARIANTS["cmp"][2], max(1024, T))
    strip = A.sb([128, U], F32, "strip_c")
    stage = A.sb([128, U], F32, "stage_c")
    strip_b, stage_b = Buf(), Buf()
    accs = [A.ps([128, 2, 256], F32, "acc") for _ in range(4)]
    acc_b = [Buf(), Buf()]
    osg = [A.sb([128, 4, 64], BF16, "osg") for _ in range(2)]
    osg_b = [Buf(), Buf()]
    rs = [A.sb([128, 8], F32, "rs") for _ in range(2)]
    rs_b = [Buf(), Buf()]
    umax = U - 512
    na = 0
    for r in range(4):
        h = 4 * g + r
        q, qb = qT[r % 2], qTb[r % 2]
        P.dma(q[:], scr["qT"][h], reads=scr["bufs_all"], writes=[qb])
        build_strip(k, c, ftabs, "cmp", h, strip, strip_b, stage, stage_b, st["s_ps"], st["s_b"], U=U)
        for qt in range(T // 512):
            t0 = qt * 512
            kts = [ct for ct in range(nct) if t0 - 2048 * ct >= 0]
            a = na % 2
            na += 1
            acc = [accs[2 * a + qs // 2][:, qs % 2, 0:193] for qs in range(4)]
            attn_tiles(k, st, lambda ct: ckT[:, g, ct * 128:(ct + 1) * 128], q[:, t0:t0 + 512], strip,
                       lambda ct: min(t0 - 2048 * ct, umax), lambda ct: cvx[:, ct, g, :], kts, acc, acc_b[a], 193,
                       [ckT_b, cvx_b, qb, strip_b])
            for qs in range(4):
                tt = qt * 4 + qs
                rr = rs[a][:, qs:qs + 1]
                P.op("dve", lambda e: e.tensor_scalar(out=rr, in0=acc[qs][:, 64:65], scalar1=1e-30, scalar2=None, op0=ALU.max),
                     reads=[acc_b[a]], pw=[rs_b[a]] if qs else (), writes=() if qs else [rs_b[a]])
                P.op("dve", lambda e: e.reciprocal(out=rr, in_=rr), pw=[rs_b[a]])
                P.op("dve", lambda e: e.tensor_scalar(out=osg[a][:, qs, :], in0=acc[qs][:, 0:64], scalar1=rr, scalar2=gat[:, tt, h:h + 1],
                                                      op0=ALU.mult, op1=ALU.mult),
                     reads=[acc_b[a], rs_b[a], gat_b], pw=[osg_b[a]] if qs else (), writes=() if qs else [osg_b[a]])
                if r == 0:
                    P.op("dve", lambda e: e.tensor_scalar(out=imp[:, tt, :], in0=acc[qs][:, 65:193], scalar1=rr, scalar2=None, op0=ALU.mult),
                         reads=[acc_b[a], rs_b[a]], pw=[imp_b])
                else:
                    P.op("dve", lambda e: e.scalar_tensor_tensor(out=imp[:, tt, :], in0=acc[qs][:, 65:193], scalar=rr, in1=imp[:, tt, :],
                                                                 op0=ALU.mult, op1=ALU.add),
                         reads=[acc_b[a], rs_b[a]], pw=[imp_b])
            P.dma(oc_dram[t0:t0 + 512, h * 64:(h + 1) * 64].rearrange("(qs p) d -> p qs d", p=128), osg[a][:], reads=[osg_b[a]],
                  pw=[oc_bufs[qt]], q="pool")
    P.barrier()
    A.release(m0)


def topk_masks(k, c, T, imp, imp_b, nmT, nmT_b):
    P, A = k.P, k.A
    m0 = A.mark()
    ntt = T // 128
    BIG = 1.0e9
    Ab = A.sb([128, 384], F32, "Abase")
    Bb = A.sb([128, 384], F32, "Bbase")
    NFb = A.sb([128, 384], F32, "NFbase")
    pb = Buf()
    for half in range(2):
        ps_ = slice(half * 64, half * 64 + 64)
        e0 = 128 + half
        P.op("pool", lambda e: e.memset(Ab[ps_, :], 0.0), pw=[pb])
        P.op("pool", lambda e: e.memset(Ab[ps_, 0:e0 - 1], 1.0), pw=[pb])
        P.op("pool", lambda e: e.memset(Bb[ps_, :], 0.0), pw=[pb])
        P.op("pool", lambda e: e.memset(Bb[ps_, e0 - 1:e0 + 1], BIG), pw=[pb])
        P.op("pool", lambda e: e.memset(Bb[ps_, e0 + 1:384], -1.0), pw=[pb])
        P.op("pool", lambda e: e.memset(NFb[ps_, :], 0.0), pw=[pb])
        P.op("pool", lambda e: e.memset(NFb[ps_, 0:e0 + 1], 1.0), pw=[pb])
    val = [A.sb([128, 128], F32, "tk_val") for _ in range(2)]
    wrk = [A.sb([128, 128], F32, "tk_wrk") for _ in range(2)]
    mx = [A.sb([128, 8], F32, "tk_mx") for _ in range(2)]
    nmb = [A.sb([128, 128], BF16, "tk_nm") for _ in range(2)]
    ps = [A.ps([128, 128], BF16, "tk_ps") for _ in range(2)]
    vb, wb, mb, nb, psb = [Buf(), Buf()], [Buf(), Buf()], [Buf(), Buf()], [Buf(), Buf()], [Buf(), Buf()]
    ident, ib = c["ident"], c["ident_b"]
    for tt in range(ntt):
        i = tt % 2
        o = 128 - 2 * tt
        if o < 0:
            raise ValueError("T too large for topk pattern")
        P.op("dve", lambda e: e.tensor_tensor(out=val[i][:], in0=imp[:, tt, :], in1=Ab[:, o:o + 128], op=ALU.mult), reads=[imp_b, pb], writes=[vb[i]])
        P.op("dve", lambda e: e.tensor_tensor(out=val[i][:], in0=val[i][:], in1=Bb[:, o:o + 128], op=ALU.add), reads=[pb], writes=[vb[i]])
        P.op("dve", lambda e: e.memset(val[i][:, 0:1], BIG), writes=[vb[i]])
        src = val[i]
        for rnd in range(2):
            P.op("dve", lambda e: e.max(out=mx[i][:], in_=src[:]), reads=[vb[i], wb[i]], writes=[mb[i]])
            P.op("dve", lambda e: e.match_replace(out=wrk[i][:], in_to_replace=mx[i][:], in_values=src[:], imm_value=-2.0),
                 reads=[mb[i], vb[i]], writes=[wb[i]])
            src = wrk[i]
        P.op("dve", lambda e: e.tensor_scalar(out=wrk[i][:], in0=wrk[i][:], scalar1=-2.0, scalar2=None, op0=ALU.is_equal), writes=[wb[i]])
        P.op("dve", lambda e: e.tensor_tensor(out=wrk[i][:], in0=wrk[i][:], in1=NFb[:, o:o + 128], op=ALU.mult), reads=[pb], writes=[wb[i]])
        P.op("dve", lambda e: e.tensor_scalar(out=nmb[i][:], in0=wrk[i][:], scalar1=-NEG, scalar2=NEG, op0=ALU.mult, op1=ALU.add),
             reads=[wb[i]], writes=[nb[i]])
        P.op("pe", lambda e: e.transpose(out=ps[i][:], in_=nmb[i][:], identity=ident[:]), reads=[nb[i], ib], writes=[psb[i]])
        P.op("act", lambda e: e.copy(out=nmT[:, tt * 128:(tt + 1) * 128], in_=ps[i][:]), reads=[psb[i]], pw=[nmT_b])
    P.barrier()
    A.release(m0)


def load_vx(k, v_dram, g, T, bufs_r, name):
    P, A = k.P, k.A
    vx = A.sb([128, T // 128, 65], BF16, name)
    b = Buf()
    P.op("pool", lambda e: e.memset(vx[:, :, 64:65], 1.0), pw=[b])
    for k0 in range(0, T // 128, 8):
        k1 = min(T // 128, k0 + 8)
        P.dma(vx[:, k0:k1, 0:64], v_dram[k0 * 128:k1 * 128, g * 64:(g + 1) * 64].rearrange("(kt p) d -> p kt d", p=128), reads=bufs_r, pw=[b])
    return vx, b


def nsa_pass2(k, c, ftabs, scr, T, g, W, W_b, nmT, nmT_b, gat, gat_b, osw_dram, osw_bufs):
    P, A = k.P, k.A
    m0 = A.mark()
    st = attn_state(k)
    ksT = A.sb([64, T], BF16, "ksT_g")
    kwT = A.sb([64, T], BF16, "kwT_g")
    kb_ = Buf()
    P.dma(ksT[:], scr["ksT"][g], reads=scr["bufs_all"], pw=[kb_])
    P.dma(kwT[:], scr["kwT"][g], reads=scr["bufs_all"], pw=[kb_])
    vsx, vsb = load_vx(k, scr["vs"], g, T, scr["bufs_all"], "vsx")
    vwx, vwb = load_vx(k, scr["vw"], g, T, scr["bufs_all"], "vwx")
    qT = [A.sb([64, T], BF16, "qT_h2") for _ in range(2)]
    qTb = [Buf(), Buf()]
    Uc = min(VARIANTS["causal"][2], T + 512 + 384)
    Uw = min(VARIANTS["win"][2], T + 512 + 384)
    strip_c = A.sb([128, Uc], F32, "strip_s")
    strip_w = A.sb([128, Uw], F32, "strip_w")
    stage = A.sb([128, max(Uc, Uw)], F32, "stage_s")
    sc_b, sw_b, stage_b = Buf(), Buf(), Buf()
    accs = [A.ps([128, 2, 256], F32, "acc2") for _ in range(4)]
    acc_b = [Buf(), Buf()]
    osg = [A.sb([128, 4, 64], F32, "osg2") for _ in range(2)]
    osgb = [A.sb([128, 4, 64], BF16, "osg2b") for _ in range(2)]
    osg_b = [Buf(), Buf()]
    rs = [A.sb([128, 8], F32, "rs2") for _ in range(2)]
    rs_b = [Buf(), Buf()]
    na = 0
    no = 0
    ucmax = Uc - 512
    for r in range(4):
        h = 4 * g + r
        q, qb = qT[r % 2], qTb[r % 2]
        P.dma(q[:], scr["qT"][h], reads=scr["bufs_all"], writes=[qb])
        build_strip(k, c, ftabs, "causal", h, strip_c, sc_b, stage, stage_b, st["s_ps"], st["s_b"], U=Uc)
        build_strip(k, c, ftabs, "win", h, strip_w, sw_b, stage, stage_b, st["s_ps"], st["s_b"], U=Uw)
        for qt in range(T // 512):
            t0 = qt * 512
            o_i = no % 2
            no += 1
            for br in range(2):
                a = na % 2
                na += 1
                acc = [accs[2 * a + qs // 2][:, qs % 2, 0:65] for qs in range(4)]
                if br == 0:
                    kts = list(range(0, (t0 + 512) // 128))
                    attn_tiles(k, st, lambda kt: ksT[:, kt * 128:(kt + 1) * 128], q[:, t0:t0 + 512], strip_c,
                               lambda kt: min(t0 - kt * 128 + 384, ucmax), lambda kt: vsx[:, kt, :], kts, acc, acc_b[a], 65,
                               [kb_, vsb, qb, sc_b, W_b, nmT_b],
                               mask_fn=lambda kt: (W[:, kt * 128:(kt + 1) * 128], nmT[:, t0:t0 + 512]))
                else:
                    kts = list(range(max(0, t0 - 512) // 128, (t0 + 512) // 128))
                    attn_tiles(k, st, lambda kt: kwT[:, kt * 128:(kt + 1) * 128], q[:, t0:t0 + 512], strip_w,
                               lambda kt: t0 - kt * 128 + 384, lambda kt: vwx[:, kt, :], kts, acc, acc_b[a], 65,
                               [kb_, vwb, qb, sw_b])
                for qs in range(4):
                    tt = qt * 4 + qs
                    rr = rs[a][:, qs:qs + 1]
                    gcol = gat[:, tt, (1 + br) * 16 + h:(1 + br) * 16 + h + 1]
                    P.op("dve", lambda e: e.tensor_scalar(out=rr, in0=acc[qs][:, 64:65], scalar1=1e-30, scalar2=None, op0=ALU.max),
                         reads=[acc_b[a]], pw=[rs_b[a]] if qs else (), writes=() if qs else [rs_b[a]])
                    P.op("dve", lambda e: e.reciprocal(out=rr, in_=rr), pw=[rs_b[a]])
                    if br == 0:
                        P.op("dve", lambda e: e.tensor_scalar(out=osg[o_i][:, qs, :], in0=acc[qs][:, 0:64], scalar1=rr, scalar2=gcol,
                                                              op0=ALU.mult, op1=ALU.mult),
                             reads=[acc_b[a], rs_b[a], gat_b], pw=[osg_b[o_i]] if qs else (), writes=() if qs else [osg_b[o_i]])
                    else:
                        P.op("dve", lambda e: e.tensor_tensor(out=rr, in0=rr, in1=gcol, op=ALU.mult), reads=[gat_b], pw=[rs_b[a]])
                        P.op("dve", lambda e: e.scalar_tensor_tensor(out=osgb[o_i][:, qs, :], in0=acc[qs][:, 0:64], scalar=rr, in1=osg[o_i][:, qs, :],
                                                                     op0=ALU.mult, op1=ALU.add),
                             reads=[acc_b[a], rs_b[a]], pw=[osg_b[o_i]])
            P.dma(osw_dram[t0:t0 + 512, h * 64:(h + 1) * 64].rearrange("(qs p) d -> p qs d", p=128), osgb[o_i][:], reads=[osg_b[o_i]],
                  pw=[osw_bufs[qt]], q="pool")
    P.barrier()
    A.release(m0)


def nsa_attention(k, c, ftabs, w, scr, T, selmap_ap, oc_dram, osw_dram, W, W_b):
    P, A = k.P, k.A
    m0 = A.mark()
    ncmp = T // 16 - 1
    nct = (ncmp + 127) // 128
    ckT = A.sb([64, 4, nct * 128], BF16, "ckT")
    cvx = A.sb([128, nct, 4, 193], BF16, "cvx")
    ckT_b, cvx_b = Buf(), Buf()
    compress_prompt(k, c, w, scr, T, ckT, ckT_b, cvx, cvx_b, selmap_ap)
    gat, gat_b = load_gates(k, scr, T)
    nmT = A.sb([128, T], BF16, "nmT")
    nmT_b = Buf()
    nq = T // 512
    oc_bufs = [Buf() for _ in range(nq)]
    osw_bufs = [Buf() for _ in range(nq)]
    for g in range(4):
        m1 = A.mark()
        imp = A.sb([128, T // 128, 128], F32, "imp")
        imp_b = Buf()
        nsa_pass1(k, c, ftabs, scr, T, g, ckT, ckT_b, cvx, cvx_b, gat, gat_b, imp, imp_b, oc_dram, oc_bufs)
        topk_masks(k, c, T, imp, imp_b, nmT, nmT_b)
        A.release(m1)
        nsa_pass2(k, c, ftabs, scr, T, g, W, W_b, nmT, nmT_b, gat, gat_b, osw_dram, osw_bufs)
    P.barrier()
    A.release(m0)
    return oc_bufs + osw_bufs


def outproj_phase(k, c, x_in, srcs, src_bufs, w_out_ap, x_out, T, xin_bufs=(), wname="w_out"):
    P, A = k.P, k.A
    m0 = A.mark()
    ident, ib = c["ident"], c["ident_b"]
    wt, wb = load_weight_bf16(k, w_out_ap, D, D, None, None, wname)
    NB = 2
    xt = [A.sb([128, D], F32, "op_x") for _ in range(NB)]
    ot = [[A.sb([128, D], BF16, "op_o") for _ in range(NB)] for _ in srcs]
    oT = [A.sb([128, D], BF16, "op_oT") for _ in range(NB)]
    ps_t = A.ps([128, D], BF16, "op_pst")
    ps_y = [A.ps([128, 512], F32, "op_psy") for _ in range(2)]
    b_x, b_oT = [Buf(), Buf()], [Buf(), Buf()]
    b_o = [[Buf(), Buf()] for _ in srcs]
    b_pst, b_psy = Buf(), [Buf(), Buf()]
    out_bufs = []
    for t in range(T // 128):
        i = t % NB
        r0 = t * 128
        P.dma(xt[i][:], x_in[r0:r0 + 128, :], reads=list(xin_bufs), writes=[b_x[i]])
        for s_i, sd in enumerate(srcs):
            P.dma(ot[s_i][i][:], sd[r0:r0 + 128, :], reads=list(src_bufs), writes=[b_o[s_i][i]], q="act")
        for s_i in range(1, len(srcs)):
            P.op("pool", lambda e: e.tensor_tensor(out=ot[0][i][:], in0=ot[0][i][:], in1=ot[s_i][i][:], op=ALU.add),
                 reads=[b_o[s_i][i]], writes=[b_o[0][i]])
        for kc in range(8):
            P.op("pe", lambda e: e.transpose(out=ps_t[:, kc * 128:(kc + 1) * 128], in_=ot[0][i][:, kc * 128:(kc + 1) * 128], identity=ident[:]),
                 reads=[b_o[0][i], ib], pw=[b_pst] if kc else (), writes=() if kc else [b_pst])
        P.op("act", lambda e: e.copy(out=oT[i][:], in_=ps_t[:]), reads=[b_pst], writes=[b_oT[i]])
        for nc_ in range(2):
            for kc in range(8):
                P.op("pe", lambda e: e.matmul(ps_y[nc_][:], lhsT=oT[i][:, kc * 128:(kc + 1) * 128], rhs=wt[:, kc, nc_ * 512:(nc_ + 1) * 512],
                                              start=(kc == 0), stop=(kc == 7)),
                     reads=[b_oT[i], wb], pw=[b_psy[nc_]] if kc else (), writes=() if kc else [b_psy[nc_]])
            P.op("dve", lambda e: e.tensor_tensor(out=xt[i][:, nc_ * 512:(nc_ + 1) * 512], in0=xt[i][:, nc_ * 512:(nc_ + 1) * 512], in1=ps_y[nc_][:], op=ALU.add),
                 reads=[b_psy[nc_]], writes=[b_x[i]])
        ob = Buf()
        P.dma(x_out[r0:r0 + 128, :], xt[i][:], reads=[b_x[i]], writes=[ob], q="pool")
        out_bufs.append(ob)
    P.barrier()
    A.release(m0)
    return out_bufs


def mlp_phase(k, c, x_in, xin_bufs, norm_ap, w1_ap, w2_ap, x_outs, T):
    P, A = k.P, k.A
    m0 = A.mark()
    ident, ib = c["ident"], c["ident_b"]
    gcol, gb = load_gcol(k, norm_ap, D, "mgcol")
    w1, w1b = load_weight_bf16(k, w1_ap, D, DFF, gcol, gb, "mw1")
    w2, w2b = load_weight_bf16(k, w2_ap, DFF, D, None, None, "mw2")
    TT = 256 if T % 256 == 0 else 128
    ns = TT // 128
    xt = A.sb([128, ns, D], F32, "m_x")
    xn = A.sb([128, D], BF16, "m_xn")
    junk = A.sb([128, D], BF16, "m_junk")
    xnT = A.sb([128, 8, TT], BF16, "m_xnT")
    hT = A.sb([128, 32, TT], BF16, "m_hT")
    hr = [A.sb([128, TT], F32, "m_hr") for _ in range(2)]
    stt = A.sb([128, 4], F32, "m_st")
    ps_t = A.ps([128, D], BF16, "m_pst")
    ps_h = [A.ps([128, TT], F32, "m_psh") for _ in range(2)]
    ps_y = [A.ps([128, 512], F32, "m_psy") for _ in range(2 * ns)]
    b_x, b_xn, b_junk, b_xnT, b_hT, b_st, b_pst = Buf(), Buf(), Buf(), Buf(), Buf(), Buf(), Buf()
    b_hr, b_psh = [Buf(), Buf()], [Buf(), Buf()]
    b_psy = [Buf() for _ in range(2 * ns)]
    out_bufs = []
    nh = 0
    for t in range(T // TT):
        r0 = t * TT
        P.dma(xt[:], x_in[r0:r0 + TT, :].rearrange("(j p) d -> p j d", p=128), reads=list(xin_bufs), writes=[b_x])
        for j in range(ns):
            P.op("act", lambda e: e.activation(out=junk[:], in_=xt[:, j, :], func=AF.Square, accum_out=stt[:, 0:1]), reads=[b_x], writes=[b_junk, b_st])
            rms_rstd(P, stt[:, 0:1], stt[:, 1:2], D, [b_st], [b_st])
            P.op("dve", lambda e: e.tensor_scalar(out=xn[:], in0=xt[:, j, :], scalar1=stt[:, 1:2], scalar2=None, op0=ALU.mult),
                 reads=[b_x, b_st], writes=[b_xn])
            for kc in range(8):
                P.op("pe", lambda e: e.transpose(out=ps_t[:, kc * 128:(kc + 1) * 128], in_=xn[:, kc * 128:(kc + 1) * 128], identity=ident[:]),
                     reads=[b_xn, ib], pw=[b_pst] if kc else (), writes=() if kc else [b_pst])
            P.op("act", lambda e: e.copy(out=xnT[:, :, j * 128:(j + 1) * 128], in_=ps_t[:].rearrange("p (kc t) -> p kc t", t=128)),
                 reads=[b_pst], pw=[b_xnT] if j else (), writes=() if j else [b_xnT])
        for fc in range(32):
            i = nh % 2
            nh += 1
            for kc in range(8):
                P.op("pe", lambda e: e.matmul(ps_h[i][:], lhsT=w1[:, kc, fc * 128:(fc + 1) * 128], rhs=xnT[:, kc, :], start=(kc == 0), stop=(kc == 7)),
                     reads=[b_xnT, w1b], pw=[b_psh[i]] if kc else (), writes=() if kc else [b_psh[i]])
            P.op("act", lambda e: e.activation(out=hr[i][:], in_=ps_h[i][:], func=AF.Relu), reads=[b_psh[i]], writes=[b_hr[i]])
            P.op("dve" if fc % 2 else "pool", lambda e: e.tensor_tensor(out=hT[:, fc, :], in0=hr[i][:], in1=hr[i][:], op=ALU.mult),
                 reads=[b_hr[i]], pw=[b_hT] if fc else (), writes=() if fc else [b_hT])
        for j in range(ns):
            for nc_ in range(2):
                py, pyb = ps_y[j * 2 + nc_], b_psy[j * 2 + nc_]
                for fc in range(32):
                    P.op("pe", lambda e: e.matmul(py[:], lhsT=hT[:, fc, j * 128:(j + 1) * 128], rhs=w2[:, fc, nc_ * 512:(nc_ + 1) * 512],
                                                  start=(fc == 0), stop=(fc == 31)),
                         reads=[b_hT, w2b], pw=[pyb] if fc else (), writes=() if fc else [pyb])
                P.op("dve", lambda e: e.tensor_tensor(out=xt[:, j, nc_ * 512:(nc_ + 1) * 512], in0=xt[:, j, nc_ * 512:(nc_ + 1) * 512], in1=py[:], op=ALU.add),
                     reads=[pyb], writes=[b_x])
        ob = Buf()
        for xo in x_outs:
            P.dma(xo[r0:r0 + TT, :].rearrange("(j p) d -> p j d", p=128), xt[:], reads=[b_x], pw=[ob], q="pool")
        out_bufs.append(ob)
    P.barrier()
    A.release(m0)
    return out_bufs


def shared_kv_phase(k, c, x_ap, T, kv_norm_ap, w_kv_ap, k_norm_ap, out_rows_fn, scr=None):
    P, A = k.P, k.A
    m0 = A.mark()
    ident, ib = c["ident"], c["ident_b"]
    gcol, gb = load_gcol(k, kv_norm_ap, D, "kvg")
    wt, wb = load_weight_bf16(k, w_kv_ap, D, 512, gcol, gb, "w_kv")
    kg, kgb = load_bcast(k, k_norm_ap, HD, "kgs")
    NB = 2
    xt = [A.sb([128, D], F32, "kv_x") for _ in range(NB)]
    xn = [A.sb([128, D], BF16, "kv_xn") for _ in range(NB)]
    junk = A.sb([128, D], BF16, "kv_junk")
    xnT = [A.sb([128, D], BF16, "kv_xnT") for _ in range(NB)]
    z = [A.sb([128, 512], F32, "kv_z") for _ in range(NB)]
    zb = [A.sb([128, 512], BF16, "kv_zb") for _ in range(NB)]
    sq = A.sb([128, 256], F32, "kv_sq")
    st = [A.sb([128, 8], F32, "kv_st") for _ in range(NB)]
    kTs = [A.sb([64, 4, 128], BF16, "kv_kTs") for _ in range(NB)]
    ps_t = A.ps([128, D], BF16, "kv_pst")
    ps_z = [A.ps([128, 512], F32, "kv_psz") for _ in range(2)]
    ps_k = A.ps([64, 4, 128], BF16, "kv_psk")
    B2 = lambda: [Buf(), Buf()]
    b_x, b_xn, b_xnT, b_z, b_zb, b_st, b_kTs, b_psz = B2(), B2(), B2(), B2(), B2(), B2(), B2(), B2()
    b_junk, b_sq, b_pst, b_psk = Buf(), Buf(), Buf(), Buf()
    outb = Buf()
    for t in range(T // 128):
        i = t % NB
        r0 = t * 128
        P.dma(xt[i][:], x_ap[r0:r0 + 128, :], writes=[b_x[i]])
        P.op("act", lambda e: e.activation(out=junk[:], in_=xt[i][:], func=AF.Square, accum_out=st[i][:, 0:1]), reads=[b_x[i]], writes=[b_junk, b_st[i]])
        rms_rstd(P, st[i][:, 0:1], st[i][:, 1:2], D, [b_st[i]], [b_st[i]])
        P.op("dve", lambda e: e.tensor_scalar(out=xn[i][:], in0=xt[i][:], scalar1=st[i][:, 1:2], scalar2=None, op0=ALU.mult),
             reads=[b_x[i], b_st[i]], writes=[b_xn[i]])
        for kc in range(8):
            P.op("pe", lambda e: e.transpose(out=ps_t[:, kc * 128:(kc + 1) * 128], in_=xn[i][:, kc * 128:(kc + 1) * 128], identity=ident[:]),
                 reads=[b_xn[i], ib], pw=[b_pst] if kc else (), writes=() if kc else [b_pst])
        P.op("act", lambda e: e.copy(out=xnT[i][:], in_=ps_t[:]), reads=[b_pst], writes=[b_xnT[i]])
        pz, bz = ps_z[t % 2], b_psz[t % 2]
        for kc in range(8):
            P.op("pe", lambda e: e.matmul(pz[:], lhsT=xnT[i][:, kc * 128:(kc + 1) * 128], rhs=wt[:, kc, :], start=(kc == 0), stop=(kc == 7)),
                 reads=[b_xnT[i], wb], pw=[bz] if kc else (), writes=() if kc else [bz])
        P.op("dve", lambda e: e.tensor_copy(out=z[i][:], in_=pz[:]), reads=[bz], writes=[b_z[i]])
        zi = z[i]
        P.op("pool", lambda e: e.tensor_tensor(out=sq[:], in0=zi[:, 0:256], in1=zi[:, 0:256], op=ALU.mult), reads=[b_z[i]], writes=[b_sq])
        P.op("dve", lambda e: e.tensor_reduce(out=st[i][:, 2:6], in_=sq[:].rearrange("p (h d) -> p h d", d=64), axis=AX.X, op=ALU.add),
             reads=[b_sq], pw=[b_st[i]])
        rms_rstd(P, st[i][:, 2:6], st[i][:, 2:6], HD, [b_st[i]], [b_st[i]])
        P.op("dve", lambda e: e.tensor_tensor(out=zi[:, 0:256].rearrange("p (h d) -> p h d", d=64), in0=zi[:, 0:256].rearrange("p (h d) -> p h d", d=64),
                                              in1=st[i][:, 2:6].unsqueeze(2).to_broadcast([128, 4, 64]), op=ALU.mult),
             reads=[b_st[i]], writes=[b_z[i]])
        P.op("dve", lambda e: e.tensor_tensor(out=zi[:, 0:256].rearrange("p (h d) -> p h d", d=64), in0=zi[:, 0:256].rearrange("p (h d) -> p h d", d=64),
                                              in1=kg[:].unsqueeze(1).to_broadcast([128, 4, 64]), op=ALU.mult),
             reads=[kgb], writes=[b_z[i]])
        for (ap, lo, hi) in out_rows_fn(t):
            P.dma(ap, zi[lo:hi, :], reads=[b_z[i]], pw=[outb], q="pool")
        if scr is not None:
            P.op("act", lambda e: e.copy(out=zb[i][:], in_=zi[:]), reads=[b_z[i]], writes=[b_zb[i]])
            P.dma(scr["vd"][r0:r0 + 128, :], zb[i][:, 256:512], reads=[b_zb[i]], pw=[outb], q="pool")
            for h in range(4):
                P.op("pe", lambda e: e.transpose(out=ps_k[:, h, :], in_=zb[i][:, h * 64:(h + 1) * 64], identity=ident[:]),
                     reads=[b_zb[i], ib], pw=[b_psk] if h else (), writes=() if h else [b_psk])
            P.op("dve", lambda e: e.tensor_copy(out=kTs[i][:], in_=ps_k[:]), reads=[b_psk], writes=[b_kTs[i]])
            P.dma(scr["kdT"][:, :, r0:r0 + 128].rearrange("h d t -> d h t"), kTs[i][:], reads=[b_kTs[i]], pw=[outb], q="sp")
    P.barrier()
    A.release(m0)


PAST = 2048
NPG = 16
DEBUG_SAMPLE = False


def build_sample_bias(k, c, ftabs):
    P, A = k.P, k.A
    ball = A.sb([128, 16, 184], F32, "ball")
    bb = Buf()
    m0 = A.mark()
    stage = [A.sb([128, 184], F32, "sb_stage") for _ in range(2)]
    ps = [A.ps([128, 184], F32, "sb_ps") for _ in range(2)]
    sb_, pb_ = [Buf(), Buf()], [Buf(), Buf()]
    for h in range(16):
        i = h % 2
        st = stage[i]

        def src(nm, uoff, dims):
            s, OFF, U0, off, L, base = variant_geom(nm)
            ft, fb = ftabs[nm]
            return bass.AP(tensor=ft.tensor, offset=ft.offset + h * L + base + uoff, ap=[[s, 128]] + dims), fb
        a, fb = src("cmp", PAST, [[1, 8]])
        P.dma(st[:, 0:8], a, reads=[fb], writes=[sb_[i]])
        a, fb = src("causal", 512, [[128, 16], [1, 8]])
        P.dma(st[:, 8:136].rearrange("p (a b) -> p a b", b=8), a, reads=[fb], pw=[sb_[i]])
        a, fb = src("causal", 384, [[1, 8]])
        P.dma(st[:, 136:144], a, reads=[fb], pw=[sb_[i]])
        a, fb = src("win", 512, [[128, 4], [1, 8]])
        P.dma(st[:, 144:176].rearrange("p (a b) -> p a b", b=8), a, reads=[fb], pw=[sb_[i]])
        a, fb = src("win", 384, [[1, 8]])
        P.dma(st[:, 176:184], a, reads=[fb], pw=[sb_[i]])
        P.op("pe", lambda e: e.matmul(ps[i][:], lhsT=c["J"][:], rhs=st[:], start=True, stop=True), reads=[sb_[i], c["J_b"]], writes=[pb_[i]])
        P.op("dve", lambda e: e.tensor_copy(out=ball[:, h, :], in_=ps[i][:]), reads=[pb_[i]], pw=[bb])
    P.barrier()
    A.release(m0)
    return ball, bb


def build_page_idx(k, c, pt_ap, nseq):
    P, A = k.P, k.A
    n = nseq * NPG
    idx = A.sb([128, n], I32, "pg_idx")
    b = Buf()
    m0 = A.mark()
    io = A.sb([128, 1], I32, "pg_iota")
    iof = A.sb([128, 1], F32, "pg_iotaf")
    idf = A.sb([128, n], F32, "pg_idf")
    src = bass.AP(tensor=pt_ap.tensor, offset=pt_ap.offset, ap=[[0, 128], [1, n]])
    P.dma(idx[:], src, writes=[b])
    P.dma(io[:], k.dram["piota"], writes=[b])
    P.op("dve", lambda e: e.tensor_copy(out=iof[:], in_=io[:]), writes=[b])
    P.op("dve", lambda e: e.tensor_copy(out=idf[:], in_=idx[:]), writes=[b])
    P.op("dve", lambda e: e.tensor_scalar(out=idf[:], in0=idf[:], scalar1=128.0, scalar2=iof[:, 0:1], op0=ALU.mult, op1=ALU.add), writes=[b])
    P.op("dve", lambda e: e.tensor_copy(out=idx[:], in_=idf[:]), writes=[b])
    P.barrier()
    A.release(m0)
    return idx, b


def sample_nsa_attention(k, c, w, scr, nseq, pool_ap, cwin_ap, idx, idx_b, ball, ball_b, W, W_b, o_c, o_s, o_w, selmap_ap):
    P, A = k.P, k.A
    m0 = A.mark()
    ident, ib = c["ident"], c["ident_b"]
    TS = nseq * 8
    cw = compress_setup(k, c, w)
    misc = A.ps([128, 512], F32, "s_misc")
    cw["st"] = compress_state(k, misc)
    qn = A.sb([64, 16, TS], BF16, "s_qn")
    ksn = A.sb([64, 4, TS], BF16, "s_ksn")
    kwn = A.sb([64, 4, TS], BF16, "s_kwn")
    nb = Buf()
    P.dma(qn[:], scr["qT"].rearrange("h d t -> d h t"), reads=scr["bufs_all"], pw=[nb])
    P.dma(ksn[:], scr["ksT"].rearrange("h d t -> d h t"), reads=scr["bufs_all"], pw=[nb])
    P.dma(kwn[:], scr["kwT"].rearrange("h d t -> d h t"), reads=scr["bufs_all"], pw=[nb])
    ones = A.sb([128, 1], BF16, "s_ones")
    P.op("pool", lambda e: e.memset(ones[:], 1.0), pw=[nb])
    Rsel = A.sb([32, 8], F32, "s_rsel")
    P.op("pool", lambda e: e.memset(Rsel[:], 0.0), pw=[nb])
    for r in range(4):
        P.op("pool", lambda e: e.affine_select(out=Rsel[:], in_=Rsel[:], pattern=[[-1, 8]], compare_op=ALU.not_equal, fill=1.0,
                                               base=-8 * r, channel_multiplier=1), pw=[nb])
    smf = A.sb([128, 34], F32, "s_smf")
    P.dma(smf[:, 0:33], selmap_ap[0:128, 0:33], pw=[nb])
    Wn = A.sb([64, 8], BF16, "s_Wn")
    P.op("pool", lambda e: e.memset(Wn[:], 0.0), pw=[nb])
    P.op("pool", lambda e: e.memset(Wn[32:33, :], 1.0), pw=[nb])
    stg = [A.sb([128, 1024], F32, "s_stg") for _ in range(2)]
    stg_b = [Buf(), Buf()]
    pgb = A.sb([128, NPG, 1024], BF16, "s_pgb")
    pgb_b = Buf()
    kcT = A.sb([64, 4, PAST], BF16, "s_kcT")
    vcT = A.sb([64, 4, PAST], BF16, "s_vcT")
    ksT = A.sb([64, 4, PAST], BF16, "s_ksT")
    kT_b = Buf()
    wst = A.sb([128, 4, 512], F32, "s_wst")
    wbf = A.sb([128, 4, 512], BF16, "s_wbf")
    kwT = A.sb([64, 4, 512], BF16, "s_kwT")
    wst_b, wbf_b, kwT_b = Buf(), Buf(), Buf()
    vnew = A.sb([8, 512], BF16, "s_vnew")
    vnew_b = Buf()
    ckT = A.sb([64, 4, 128], BF16, "s_ckT")
    cvx = A.sb([128, 4, 98], BF16, "s_cvx")
    ckT_b, cvx_b = Buf(), Buf()
    tmp = A.sb([128, 64], F32, "s_tmp")
    tmpb = A.sb([128, 64], BF16, "s_tmpb")
    junk = A.sb([128, 64], F32, "s_junk")
    stt = A.sb([128, 2], F32, "s_stt")
    b_tmp, b_stt, b_junk = Buf(), Buf(), Buf()
    ps_tr = [A.ps([64, 8, 128], BF16, "s_pstr") for _ in range(2)]
    ps_trb = [Buf(), Buf()]
    ps_s_bank = A.ps([128, 512], F32, "s_pss")
    ps_s = [ps_s_bank[:, 0:32], ps_s_bank[:, 256:288]]
    ps_sb = [Buf(), Buf()]
    ps_o = [A.ps([128, 512], F32, "s_pso")[0:32, 0:128] for _ in range(2)]
    ps_ob = [Buf(), Buf()]
    ps_m = misc[0:64, 256:320]
    ps_mb = Buf()
    sst = [A.sb([128, 32], F32, "s_sst") for _ in range(2)]
    sst_b = [Buf(), Buf()]
    spt = [A.sb([128, 32], BF16, "s_spt") for _ in range(2)]
    spt_b = [Buf(), Buf()]
    accn = A.sb([32, 34], F32, "s_accn")
    ors = A.sb([32, 2], F32, "s_ors")
    osb = [A.sb([32, 64], F32, "s_osb") for _ in range(2)]
    osb_b = [Buf(), Buf()]
    accn_b, ors_b = Buf(), Buf()
    val = A.sb([8, 64], F32, "s_val")
    wrk = A.sb([8, 64], F32, "s_wrk")
    mx = A.sb([8, 8], F32, "s_mx")
    nm8 = A.sb([8, 64], BF16, "s_nm8")
    nmT = A.sb([64, 8], BF16, "s_nmT")
    nmT4 = A.sb([64, 4, 8], BF16, "s_nmT4")
    tk_b, nmT_b = Buf(), Buf()
    P.op("pool", lambda e: e.memset(cvx[:], 0.0), writes=[cvx_b])
    P.op("pool", lambda e: e.memset(ckT[:], 0.0), writes=[ckT_b])
    for g in range(4):
        P.op("pool", lambda e: e.memset(cvx[:, g, 64:65], 1.0), reads=[nb], writes=[cvx_b])
        P.op("dve", lambda e: e.tensor_copy(out=cvx[:, g, 65:98], in_=smf[:, 0:33]), reads=[nb], writes=[cvx_b])
    BIG = 1.0e9
    ntr = 0
    nst = 0
    nos = 0

    def attn_small(lhsT, nk, q_ap, bias_ap, mask, pv_list, acc, acc_b_, first, last):
        nonlocal nst
        i = nst % 2
        nst += 1
        sp, spb = ps_s[i], ps_sb[i]
        P.op("pe", lambda e: e.matmul(sp[0:nk, :], lhsT=lhsT, rhs=q_ap, start=True, stop=(mask is None)),
             reads=[kT_b, nb, kwT_b, ckT_b], writes=[spb])
        if mask is not None:
            P.op("pe", lambda e: e.matmul(sp[0:nk, :], lhsT=mask[0], rhs=mask[1], start=False, stop=True), reads=[W_b, nmT_b, nb], pw=[spb])
        P.op("dve", lambda e: e.tensor_tensor(out=sst[i][0:nk, :].rearrange("p (r t) -> p r t", t=8), in0=sp[0:nk, :].rearrange("p (r t) -> p r t", t=8),
                                              in1=bias_ap, op=ALU.add), reads=[spb, ball_b], writes=[sst_b[i]])
        P.op("act", lambda e: e.activation(out=spt[i][0:nk, :], in_=sst[i][0:nk, :], func=AF.Exp), reads=[sst_b[i]], writes=[spt_b[i]])
        for j, (rhs, c0, ncol) in enumerate(pv_list):
            P.op("pe", lambda e: e.matmul(acc[:, c0:c0 + ncol], lhsT=spt[i][0:nk, :], rhs=rhs, start=(first and j == 0), stop=last, skip_group_check=True),
                 reads=[spt_b[i], pgb_b, wbf_b, vnew_b, cvx_b, nb], pw=[acc_b_] if not (first and j == 0) else (), writes=[acc_b_] if (first and j == 0) else ())

    def finish(acc, acc_b_, dst_dram, s, g):
        nonlocal nos
        P.op("dve", lambda e: e.tensor_scalar(out=ors[:, 0:1], in0=acc[:, 64:65], scalar1=1e-30, scalar2=None, op0=ALU.max), reads=[acc_b_], writes=[ors_b])
        P.op("dve", lambda e: e.reciprocal(out=ors[:, 0:1], in_=ors[:, 0:1]), writes=[ors_b])
        i = nos % 2
        nos += 1
        P.op("dve", lambda e: e.tensor_scalar(out=osb[i][:], in0=acc[:, 0:64], scalar1=ors[:, 0:1], scalar2=None, op0=ALU.mult),
             reads=[acc_b_, ors_b], writes=[osb_b[i]])
        for r in range(4):
            h = 4 * g + r
            P.dma(dst_dram[s * 8:(s + 1) * 8, h * 64:(h + 1) * 64], osb[i][r * 8:(r + 1) * 8, :], reads=[osb_b[i]], q="pool" if r % 2 else "sp")

    for s in range(nseq):
        for pg in range(NPG):
            i = pg % 2
            P.dma_custom("pool", lambda e: e.indirect_dma_start(out=stg[i][:], out_offset=None, in_=pool_ap,
                                                                in_offset=bass.IndirectOffsetOnAxis(ap=idx[:, s * NPG + pg:s * NPG + pg + 1], axis=0)),
                         reads=[idx_b], writes=[stg_b[i]])
            P.op("act" if pg % 2 else "dve",
                 (lambda e: e.copy(out=pgb[:, pg, :], in_=stg[i][:])) if pg % 2 else (lambda e: e.tensor_copy(out=pgb[:, pg, :], in_=stg[i][:])),
                 reads=[stg_b[i]], pw=[pgb_b] if pg else (), writes=() if pg else [pgb_b])
        P.dma(wst[:], cwin_ap[s].rearrange("(wt p) c -> p wt c", p=128), writes=[wst_b])
        P.op("pool", lambda e: e.tensor_copy(out=wbf[:], in_=wst[:]), reads=[wst_b], writes=[wbf_b])
        P.dma(vnew[:, 0:256], scr["vs"][s * 8:(s + 1) * 8, :], reads=scr["bufs_all"], writes=[vnew_b])
        P.dma(vnew[:, 256:512], scr["vw"][s * 8:(s + 1) * 8, :], reads=scr["bufs_all"], pw=[vnew_b])
        for si, dstT in ((0, kcT), (1, vcT), (2, ksT)):
            for pg2 in range(0, NPG, 2):
                i = ntr % 2
                ntr += 1
                for j in range(8):
                    pg, g = pg2 + j // 4, j % 4
                    P.op("pe", lambda e: e.transpose(out=ps_tr[i][:, j, :], in_=pgb[:, pg, si * 256 + g * 64:si * 256 + (g + 1) * 64], identity=ident[:]),
                         reads=[pgb_b, ib], pw=[ps_trb[i]] if j else (), writes=() if j else [ps_trb[i]])
                for jj in range(2):
                    pg = pg2 + jj
                    P.op("act" if jj else "dve",
                         (lambda e: e.copy(out=dstT[:, :, pg * 128:(pg + 1) * 128], in_=ps_tr[i][:, jj * 4:(jj + 1) * 4, :])) if jj else
                         (lambda e: e.tensor_copy(out=dstT[:, :, pg * 128:(pg + 1) * 128], in_=ps_tr[i][:, jj * 4:(jj + 1) * 4, :])),
                         reads=[ps_trb[i]], pw=[kT_b])
        for wt2 in range(0, 4, 2):
            i = ntr % 2
            ntr += 1
            for j in range(8):
                wt, g = wt2 + j // 4, j % 4
                P.op("pe", lambda e: e.transpose(out=ps_tr[i][:, j, :], in_=wbf[:, wt, g * 64:(g + 1) * 64], identity=ident[:]),
                     reads=[wbf_b, ib], pw=[ps_trb[i]] if j else (), writes=() if j else [ps_trb[i]])
            for jj in range(2):
                wt = wt2 + jj
                P.op("dve", lambda e: e.tensor_copy(out=kwT[:, :, wt * 128:(wt + 1) * 128], in_=ps_tr[i][:, jj * 4:(jj + 1) * 4, :]), reads=[ps_trb[i]], pw=[kwT_b])
        for g in range(4):
            for i_kv in range(2):
                def out_fn(ct, nrow, op_, opb):
                    if i_kv == 1:
                        P.op("dve", lambda e: e.tensor_copy(out=cvx[0:nrow, g, 0:64], in_=op_[0:nrow, :]), reads=[opb], pw=[cvx_b])
                        return
                    P.op("act", lambda e: e.activation(out=junk[0:nrow, :], in_=op_[0:nrow, :], func=AF.Square, accum_out=stt[0:nrow, 0:1]),
                         reads=[opb], writes=[b_junk, b_stt])
                    rms_rstd(P, stt[0:nrow, 0:1], stt[0:nrow, 1:2], HD, [b_stt], [b_stt])
                    P.op("dve", lambda e: e.tensor_scalar(out=tmp[0:nrow, :], in0=op_[0:nrow, :], scalar1=stt[0:nrow, 1:2], scalar2=None, op0=ALU.mult),
                         reads=[opb, b_stt], writes=[b_tmp])
                    P.op("dve", lambda e: e.tensor_tensor(out=tmpb[0:nrow, :], in0=tmp[0:nrow, :], in1=cw["kg0"][0:nrow, :], op=ALU.mult),
                         reads=[cw["kg0b"]], writes=[b_tmp])
                    P.op("pe", lambda e: e.transpose(out=ps_tr[0][:, 0, 0:nrow], in_=tmpb[0:nrow, :], identity=ident[0:nrow, 0:nrow]),
                         reads=[b_tmp, ib], writes=[ps_trb[0]])
                    P.op("act", lambda e: e.copy(out=ckT[:, g, 0:nrow], in_=ps_tr[0][:, 0, 0:nrow]), reads=[ps_trb[0]], pw=[ckT_b])
                compress_run(k, c, cw, (kcT if i_kv == 0 else vcT)[:, g, :], kT_b, 127, i_kv, out_fn)
            q_ap = qn[:, 4 * g:4 * g + 4, s * 8:(s + 1) * 8]
            bias = lambda off, n=128: ball[0:n, 4 * g:4 * g + 4, off:off + 8]
            acc, accb = ps_o[0], ps_ob[0]
            attn_small(ckT[:, g, :], 128, q_ap, bias(0), None, [(cvx[:, g, :], 0, 98)], acc, accb, True, True)
            finish(acc, accb, o_c, s, g)
            P.op("dve", lambda e: e.tensor_scalar(out=accn[:, 0:33], in0=acc[:, 65:98], scalar1=ors[:, 0:1], scalar2=None, op0=ALU.mult),
                 reads=[accb, ors_b], writes=[accn_b])
            P.op("pe", lambda e: e.matmul(ps_m[0:8, 0:33], lhsT=Rsel[:], rhs=accn[:, 0:33], start=True, stop=True), reads=[accn_b, nb], writes=[ps_mb])
            P.op("dve", lambda e: e.memset(val[:], -1.0), writes=[tk_b])
            P.op("dve", lambda e: e.tensor_copy(out=val[:, 0:33], in_=ps_m[0:8, 0:33]), reads=[ps_mb], writes=[tk_b])
            P.op("dve", lambda e: e.memset(val[:, 0:1], BIG), writes=[tk_b])
            P.op("dve", lambda e: e.memset(val[:, 31:33], BIG), writes=[tk_b])
            srcv = val
            for rnd in range(2):
                P.op("dve", lambda e: e.max(out=mx[:], in_=srcv[:]), writes=[tk_b])
                P.op("dve", lambda e: e.match_replace(out=wrk[:], in_to_replace=mx[:], in_values=srcv[:], imm_value=-2.0), writes=[tk_b])
                srcv = wrk
            P.op("dve", lambda e: e.tensor_scalar(out=wrk[:], in0=wrk[:], scalar1=-2.0, scalar2=None, op0=ALU.is_equal), writes=[tk_b])
            P.op("dve", lambda e: e.tensor_scalar(out=nm8[:], in0=wrk[:], scalar1=-NEG, scalar2=NEG, op0=ALU.mult, op1=ALU.add), writes=[tk_b])
            P.op("dve", lambda e: e.memset(nm8[:, 33:64], NEG), writes=[tk_b])
            P.op("pe", lambda e: e.transpose(out=ps_tr[1][:, 0, 0:8], in_=nm8[:], identity=ident[0:8, 0:8]), reads=[tk_b, ib], writes=[ps_trb[1]])
            P.op("dve", lambda e: e.tensor_copy(out=nmT[:], in_=ps_tr[1][:, 0, 0:8]), reads=[ps_trb[1]], writes=[nmT_b])
            for r in range(4):
                P.op("dve", lambda e: e.tensor_copy(out=nmT4[:, r, :], in_=nmT[:]), writes=[nmT_b])
            acc, accb = ps_o[1], ps_ob[1]
            for kt in range(NPG):
                attn_small(ksT[:, g, kt * 128:(kt + 1) * 128], 128, q_ap, bias(8 + (15 - kt) * 8), (W[0:64, kt * 128:(kt + 1) * 128], nmT4[:]),
                           [(pgb[:, kt, 768 + g * 64:768 + (g + 1) * 64], 0, 64), (ones[:], 64, 1)], acc, accb, kt == 0, False)
            attn_small(ksn[:, g, s * 8:(s + 1) * 8], 8, q_ap, bias(136, 8), (Wn[:], nmT4[:]),
                       [(vnew[:, g * 64:(g + 1) * 64], 0, 64), (ones[0:8, :], 64, 1)], acc, accb, False, True)
            finish(acc, accb, o_s, s, g)
            acc, accb = ps_o[0], ps_ob[0]
            for wt in range(4):
                attn_small(kwT[:, g, wt * 128:(wt + 1) * 128], 128, q_ap, bias(144 + (3 - wt) * 8), None,
                           [(wbf[:, wt, 256 + g * 64:256 + (g + 1) * 64], 0, 64), (ones[:], 64, 1)], acc, accb, wt == 0, False)
            attn_small(kwn[:, g, s * 8:(s + 1) * 8], 8, q_ap, bias(176, 8), None,
                       [(vnew[:, 256 + g * 64:256 + (g + 1) * 64], 0, 64), (ones[0:8, :], 64, 1)], acc, accb, False, True)
            finish(acc, accb, o_w, s, g)
    P.barrier()
    A.release(m0)


def sample_gate_sum(k, c, gates_dram, o_c, o_s, o_w, og_dram, T=128):
    P, A = k.P, k.A
    m0 = A.mark()
    gt = A.sb([128, 48], F32, "sg_g")
    ot = [A.sb([128, 16, 64], F32, "sg_o") for _ in range(3)]
    acc = A.sb([128, 16, 64], F32, "sg_acc")
    accb = A.sb([128, D], BF16, "sg_accb")
    b = Buf()
    P.dma(gt[:], gates_dram[0:T, :], pw=[b])
    for i, od in enumerate((o_c, o_s, o_w)):
        P.dma(ot[i][:], od[0:T, :].rearrange("p (h d) -> p h d", d=64), pw=[b])
    for i in range(3):
        P.op("dve", lambda e: e.tensor_tensor(out=ot[i][:], in0=ot[i][:], in1=gt[:, i * 16:(i + 1) * 16].unsqueeze(2).to_broadcast([128, 16, 64]), op=ALU.mult),
             reads=[b], writes=[b])
    P.op("dve", lambda e: e.tensor_tensor(out=acc[:], in0=ot[0][:], in1=ot[1][:], op=ALU.add), reads=[b], writes=[b])
    P.op("dve", lambda e: e.tensor_tensor(out=accb[:].rearrange("p (h d) -> p h d", d=64), in0=acc[:], in1=ot[2][:], op=ALU.add), reads=[b], writes=[b])
    P.dma(og_dram[0:T, :], accb[:], reads=[b], writes=[b])
    P.barrier()
    A.release(m0)


DIL = ((128, 1), (512, 4), (2048, 16))


def dil_project_phase(k, c, x_ap, T, norm_ap, wq_ap, qnorm_ap, qdT, x_bufs=()):
    P, A = k.P, k.A
    m0 = A.mark()
    ident, ib = c["ident"], c["ident_b"]
    gcol, gb = load_gcol(k, norm_ap, D, "dq_g")
    wt, wb = load_weight_bf16(k, wq_ap, D, 3072, gcol, gb, "w_q")
    qg = A.sb([128, 3, 64], F32, "dq_qg")
    qgb = Buf()
    P.dma(qg[:], bass.AP(tensor=qnorm_ap.tensor, offset=qnorm_ap.offset, ap=[[0, 128], [64, 3], [1, 64]]), writes=[qgb])
    NB = 2
    xt = [A.sb([128, D], F32, "dq_x") for _ in range(NB)]
    xn = [A.sb([128, D], BF16, "dq_xn") for _ in range(NB)]
    junk = A.sb([128, D], BF16, "dq_junk")
    xnT = [A.sb([128, D], BF16, "dq_xnT") for _ in range(NB)]
    z = [A.sb([128, 3072], F32, "dq_z") for _ in range(NB)]
    sq = A.sb([128, 3072], F32, "dq_sq")
    qb = [A.sb([128, 3072], BF16, "dq_qb") for _ in range(NB)]
    st = [A.sb([128, 50], F32, "dq_st") for _ in range(NB)]
    qTs = [A.sb([64, 48, 128], BF16, "dq_qTs") for _ in range(NB)]
    ps_t = A.ps([128, D], BF16, "dq_pst")
    ps_z = [A.ps([128, 512], F32, "dq_psz") for _ in range(2)]
    ps_q = [A.ps([64, 8, 128], BF16, "dq_psq") for _ in range(2)]
    B2 = lambda: [Buf(), Buf()]
    b_x, b_xn, b_xnT, b_z, b_qb, b_st, b_qTs, b_psz, b_psq = B2(), B2(), B2(), B2(), B2(), B2(), B2(), B2(), B2()
    b_junk, b_sq, b_pst = Buf(), Buf(), Buf()
    ob = Buf()
    zc = 0
    for t in range(T // 128):
        i = t % NB
        r0 = t * 128
        P.dma(xt[i][:], x_ap[r0:r0 + 128, :], reads=list(x_bufs), writes=[b_x[i]])
        P.op("act", lambda e: e.activation(out=junk[:], in_=xt[i][:], func=AF.Square, accum_out=st[i][:, 0:1]), reads=[b_x[i]], writes=[b_junk, b_st[i]])
        rms_rstd(P, st[i][:, 0:1], st[i][:, 1:2], D, [b_st[i]], [b_st[i]])
        P.op("dve", lambda e: e.tensor_scalar(out=xn[i][:], in0=xt[i][:], scalar1=st[i][:, 1:2], scalar2=None, op0=ALU.mult),
             reads=[b_x[i], b_st[i]], writes=[b_xn[i]])
        for kc in range(8):
            P.op("pe", lambda e: e.transpose(out=ps_t[:, kc * 128:(kc + 1) * 128], in_=xn[i][:, kc * 128:(kc + 1) * 128], identity=ident[:]),
                 reads=[b_xn[i], ib], pw=[b_pst] if kc else (), writes=() if kc else [b_pst])
        P.op("act", lambda e: e.copy(out=xnT[i][:], in_=ps_t[:]), reads=[b_pst], writes=[b_xnT[i]])
        for c0 in range(0, 3072, 512):
            pz, bz = ps_z[zc % 2], b_psz[zc % 2]
            zc += 1
            for kc in range(8):
                P.op("pe", lambda e: e.matmul(pz[:], lhsT=xnT[i][:, kc * 128:(kc + 1) * 128], rhs=wt[:, kc, c0:c0 + 512], start=(kc == 0), stop=(kc == 7)),
                     reads=[b_xnT[i], wb], pw=[bz] if kc else (), writes=() if kc else [bz])
            P.op("dve" if (c0 // 512) % 2 == 0 else "act",
                 (lambda e: e.tensor_copy(out=z[i][:, c0:c0 + 512], in_=pz[:])) if (c0 // 512) % 2 == 0 else (lambda e: e.copy(out=z[i][:, c0:c0 + 512], in_=pz[:])),
                 reads=[bz], pw=[b_z[i]] if c0 else (), writes=() if c0 else [b_z[i]])
        zi = z[i]
        P.op("pool", lambda e: e.tensor_tensor(out=sq[:], in0=zi[:], in1=zi[:], op=ALU.mult), reads=[b_z[i]], writes=[b_sq])
        P.op("dve", lambda e: e.tensor_reduce(out=st[i][:, 2:50], in_=sq[:].rearrange("p (h d) -> p h d", d=64), axis=AX.X, op=ALU.add),
             reads=[b_sq], pw=[b_st[i]])
        rms_rstd(P, st[i][:, 2:50], st[i][:, 2:50], HD, [b_st[i]], [b_st[i]], scale=HD ** -0.5)
        P.op("dve", lambda e: e.tensor_tensor(out=sq[:].rearrange("p (h d) -> p h d", d=64), in0=zi[:].rearrange("p (h d) -> p h d", d=64),
                                              in1=st[i][:, 2:50].unsqueeze(2).to_broadcast([128, 48, 64]), op=ALU.mult),
             reads=[b_z[i], b_st[i]], writes=[b_sq])
        for grp in range(3):
            P.op("pool", lambda e: e.tensor_tensor(out=qb[i][:, grp * 1024:(grp + 1) * 1024].rearrange("p (h d) -> p h d", d=64),
                                                   in0=sq[:, grp * 1024:(grp + 1) * 1024].rearrange("p (h d) -> p h d", d=64),
                                                   in1=qg[:, grp, :].unsqueeze(1).to_broadcast([128, 16, 64]), op=ALU.mult),
                 reads=[b_sq, qgb], pw=[b_qb[i]] if grp else (), writes=() if grp else [b_qb[i]])
        for hh in range(6):
            pq, pqb = ps_q[hh % 2], b_psq[hh % 2]
            for h8 in range(8):
                h = hh * 8 + h8
                P.op("pe", lambda e: e.transpose(out=pq[:, h8, :], in_=qb[i][:, h * 64:(h + 1) * 64], identity=ident[:]),
                     reads=[b_qb[i], ib], pw=[pqb] if h8 else (), writes=() if h8 else [pqb])
            P.op("dve" if hh % 2 == 0 else "act",
                 (lambda e: e.tensor_copy(out=qTs[i][:, hh * 8:(hh + 1) * 8, :], in_=pq[:])) if hh % 2 == 0 else (lambda e: e.copy(out=qTs[i][:, hh * 8:(hh + 1) * 8, :], in_=pq[:])),
                 reads=[pqb], pw=[b_qTs[i]] if hh else (), writes=() if hh else [b_qTs[i]])
        for grp in range(3):
            P.dma(qdT[grp][:, :, r0:r0 + 128].rearrange("h d t -> d h t"), qTs[i][:, grp * 16:(grp + 1) * 16, :], reads=[b_qTs[i]], pw=[ob], q="sp" if grp != 1 else "pool")
    P.barrier()
    A.release(m0)


def dil_attention_prompt(k, c, ftabs, scr_d, qdT, T, accd):
    P, A = k.P, k.A
    m0 = A.mark()
    st = attn_state(k)
    accs = [A.ps([128, 2, 256], F32, "dacc") for _ in range(4)]
    acc_b = [Buf(), Buf()]
    osg = [A.sb([128, 4, 65], F32, "dosg") for _ in range(2)]
    osg_b = [Buf(), Buf()]
    stage = A.sb([128, 1024], F32, "dstage")
    stage_b = Buf()
    strips = [A.sb([128, 1024], F32, "dstrip") for _ in range(3)]
    strip_b = [Buf(), Buf(), Buf()]
    qT = [A.sb([64, T], BF16, "dqT") for _ in range(2)]
    qTb = [Buf(), Buf()]
    nq = 0
    na = 0
    for g in range(4):
        m1 = A.mark()
        kT = A.sb([64, T], BF16, "dkT")
        kb_ = Buf()
        P.dma(kT[:], scr_d["kdT"][g], writes=[kb_])
        vxs = []
        for gi, (wnd, d) in enumerate(DIL):
            L = T // d
            vx = A.sb([128, T // 128, 65], BF16, "dvx%d" % gi)
            vb = Buf()
            P.op("pool", lambda e: e.memset(vx[:, :, 64:65], 1.0), pw=[vb])
            nkt = L // 128
            for r in range(d):
                for k0 in range(0, nkt, 8):
                    k1 = min(nkt, k0 + 8)
                    src = bass.AP(tensor=scr_d["vd"].tensor, offset=scr_d["vd"].offset + (r + d * k0 * 128) * 256 + g * 64,
                                  ap=[[d * 256, 128], [d * 128 * 256, k1 - k0], [1, 64]])
                    P.dma(vx[:, r * nkt + k0:r * nkt + k1, 0:64], src, pw=[vb], q="act" if (r + k0) % 2 else "sp")
            vxs.append((vx, vb))
        for rr in range(4):
            h = 4 * g + rr
            for gi, (wnd, d) in enumerate(DIL):
                L = T // d
                nkt = L // 128
                qw = min(512, L)
                q, qb = qT[nq % 2], qTb[nq % 2]
                nq += 1
                P.dma(q[:], qdT[gi][h], writes=[qb])
                build_strip(k, c, ftabs, "dil%d" % d, h, strips[gi], strip_b[gi], stage, stage_b, st["s_ps"], st["s_b"])
                vx, vb = vxs[gi]
                for r in range(d):
                    for m0_ in range(0, L, qw):
                        a = na % 2
                        na += 1
                        acc = [accs[2 * a + qs // 2][:, qs % 2, 0:65] for qs in range(4)]
                        kts = list(range(max(0, m0_ - 128) // 128, (m0_ + qw) // 128))
                        q_ap = q[:, r + d * m0_:r + d * (m0_ + qw - 1) + 1:d]
                        attn_tiles(k, st, lambda kt: kT[:, r + d * kt * 128:r + d * (kt * 128 + 127) + 1:d], q_ap, strips[gi],
                                   lambda kt: m0_ - kt * 128 + 384, lambda kt: vx[:, r * nkt + kt, :], kts, acc, acc_b[a], 65,
                                   [kb_, vb, qb, strip_b[gi]], qw=qw)
                        nqs = qw // 128
                        for qs in range(nqs):
                            P.op("dve" if qs % 2 else "act",
                                 (lambda e: e.tensor_copy(out=osg[a][:, qs, :], in_=acc[qs])) if qs % 2 else (lambda e: e.copy(out=osg[a][:, qs, :], in_=acc[qs])),
                                 reads=[acc_b[a]], pw=[osg_b[a]] if qs else (), writes=() if qs else [osg_b[a]])
                        dst = bass.AP(tensor=accd.tensor, offset=accd.offset + ((gi * 16 + h) * T + r + d * m0_) * 65,
                                      ap=[[d * 65, 128], [d * 128 * 65, nqs], [1, 65]])
                        P.dma(dst, osg[a][:, 0:nqs, :], reads=[osg_b[a]], q="pool")
        P.barrier()
        A.release(m1)
    P.barrier()
    A.release(m0)


def dil_merge(k, c, accd, T, og_dram):
    P, A = k.P, k.A
    m0 = A.mark()
    at = [[A.sb([128, 16, 65], F32, "dm_a") for _ in range(3)] for _ in range(2)]
    rc = [A.sb([128, 16, 1], F32, "dm_r") for _ in range(2)]
    ob = [A.sb([128, 16, 64], BF16, "dm_o") for _ in range(2)]
    bb = [Buf(), Buf()]
    for t in range(T // 128):
        i = t % 2
        for gi in range(3):
            src = bass.AP(tensor=accd.tensor, offset=accd.offset + (gi * 16 * T + t * 128) * 65, ap=[[65, 128], [T * 65, 16], [1, 65]])
            P.dma(at[i][gi][:], src, writes=[bb[i]] if gi == 0 else (), pw=[bb[i]] if gi else (), q=("sp", "act", "pool")[gi])
        P.op("dve", lambda e: e.tensor_tensor(out=at[i][0][:], in0=at[i][0][:], in1=at[i][1][:], op=ALU.add), reads=[bb[i]], writes=[bb[i]])
        P.op("dve", lambda e: e.tensor_tensor(out=at[i][0][:], in0=at[i][0][:], in1=at[i][2][:], op=ALU.add), writes=[bb[i]])
        P.op("dve", lambda e: e.tensor_scalar(out=rc[i][:], in0=at[i][0][:, :, 64:65], scalar1=1e-30, scalar2=None, op0=ALU.max), writes=[bb[i]])
        P.op("dve", lambda e: e.reciprocal(out=rc[i][:], in_=rc[i][:]), writes=[bb[i]])
        P.op("dve", lambda e: e.tensor_tensor(out=ob[i][:], in0=at[i][0][:, :, 0:64], in1=rc[i][:].to_broadcast([128, 16, 64]), op=ALU.mult), writes=[bb[i]])
        P.dma(og_dram[t * 128:(t + 1) * 128, :], ob[i][:].rearrange("p h d -> p (h d)"), reads=[bb[i]], q="pool")
    P.barrier()
    A.release(m0)


SD_TILES = ((1, [15]), (4, [12, 13, 14, 15]), (16, list(range(16))))


def build_sample_dil_bias(k, c, ftabs):
    P, A = k.P, k.A
    balld = A.sb([128, 16, 192], F32, "balld")
    bb = Buf()
    m0 = A.mark()
    stage = [A.sb([128, 192], F32, "sdb_stage") for _ in range(2)]
    ps = [A.ps([128, 192], F32, "sdb_ps") for _ in range(2)]
    sb_, pb_ = [Buf(), Buf()], [Buf(), Buf()]
    offs = {}
    for h in range(16):
        i = h % 2
        st = stage[i]
        col = 0
        first = True
        for (d, tiles) in SD_TILES:
            nm = "sd%d" % d
            s_, OFF, U0, off, L, base = variant_geom(nm)
            ft, fb = ftabs[nm]
            for kt in tiles:
                kk = 15 - kt
                a = bass.AP(tensor=ft.tensor, offset=ft.offset + h * L + base + 512 + 128 * kk, ap=[[1, 128], [1, 8]])
                P.dma(st[:, col:col + 8], a, reads=[fb], writes=[sb_[i]] if first else (), pw=() if first else [sb_[i]], q="sp" if col % 16 else "act")
                first = False
                offs[(d, kt)] = col
                col += 8
            a = bass.AP(tensor=ft.tensor, offset=ft.offset + h * L + base + 384, ap=[[1, 128], [1, 8]])
            P.dma(st[:, col:col + 8], a, reads=[fb], pw=[sb_[i]])
            offs[(d, "new")] = col
            col += 8
        P.op("pe", lambda e: e.matmul(ps[i][:], lhsT=c["J"][:], rhs=st[:], start=True, stop=True), reads=[sb_[i], c["J_b"]], writes=[pb_[i]])
        P.op("dve", lambda e: e.tensor_copy(out=balld[:, h, :], in_=ps[i][:]), reads=[pb_[i]], pw=[bb])
    P.barrier()
    A.release(m0)
    return balld, bb, offs


def sample_dil_attention(k, c, scr_sd, qdT_s, nseq, cdil_ap, balld, balld_b, offs, o_dram):
    P, A = k.P, k.A
    m0 = A.mark()
    ident, ib = c["ident"], c["ident_b"]
    TS = nseq * 8
    qn = A.sb([64, 48, TS], BF16, "sd_qn")
    kn = A.sb([64, 4, TS], BF16, "sd_kn")
    nb = Buf()
    for gi in range(3):
        P.dma(qn[:, gi * 16:(gi + 1) * 16, :], qdT_s[gi].rearrange("h d t -> d h t"), pw=[nb])
    P.dma(kn[:], scr_sd["kdT"].rearrange("h d t -> d h t"), pw=[nb])
    ones = A.sb([128, 1], BF16, "sd_ones")
    P.op("pool", lambda e: e.memset(ones[:], 1.0), pw=[nb])
    stg = A.sb([128, 16, 512], F32, "sd_stg")
    cb = A.sb([128, 16, 512], BF16, "sd_cb")
    kT = A.sb([64, 4, PAST], BF16, "sd_kT")
    vnew = A.sb([8, 256], BF16, "sd_vnew")
    stg_b, cb_b, kT_b, vnew_b = Buf(), Buf(), Buf(), Buf()
    ps_tr = [A.ps([64, 8, 128], BF16, "sd_pstr") for _ in range(2)]
    ps_trb = [Buf(), Buf()]
    ps_s_bank = A.ps([128, 512], F32, "sd_pss")
    ps_s = [ps_s_bank[:, 0:32], ps_s_bank[:, 256:288]]
    ps_sb = [Buf(), Buf()]
    ps_o = [A.ps([128, 512], F32, "sd_pso")[0:32, 0:128] for _ in range(2)]
    ps_ob = [Buf(), Buf()]
    sst = [A.sb([128, 32], F32, "sd_sst") for _ in range(2)]
    sst_b = [Buf(), Buf()]
    spt = [A.sb([128, 32], BF16, "sd_spt") for _ in range(2)]
    spt_b = [Buf(), Buf()]
    ors = A.sb([32, 2], F32, "sd_ors")
    ors_b = Buf()
    osb = [A.sb([32, 64], F32, "sd_osb") for _ in range(2)]
    osb_b = [Buf(), Buf()]
    ntr = 0
    nst = 0
    nacc = 0
    for s in range(nseq):
        for hf in range(2):
            P.dma(stg[:, hf * 8:(hf + 1) * 8, :], cdil_ap[s, hf * 1024:(hf + 1) * 1024, :].rearrange("(kt p) c -> p kt c", p=128),
                  writes=[stg_b] if hf == 0 else (), pw=[stg_b] if hf else (), q="sp" if hf else "act")
        P.op("dve", lambda e: e.tensor_copy(out=cb[:, 0:8, :], in_=stg[:, 0:8, :]), reads=[stg_b], writes=[cb_b])
        P.op("pool", lambda e: e.tensor_copy(out=cb[:, 8:16, :], in_=stg[:, 8:16, :]), reads=[stg_b], pw=[cb_b])
        P.dma(vnew[:], scr_sd["vd"][s * 8:(s + 1) * 8, :], writes=[vnew_b])
        for kt2 in range(0, 16, 2):
            i = ntr % 2
            ntr += 1
            for j in range(8):
                kt, g = kt2 + j // 4, j % 4
                P.op("pe", lambda e: e.transpose(out=ps_tr[i][:, j, :], in_=cb[:, kt, g * 64:(g + 1) * 64], identity=ident[:]),
                     reads=[cb_b, ib], pw=[ps_trb[i]] if j else (), writes=() if j else [ps_trb[i]])
            for jj in range(2):
                kt = kt2 + jj
                P.op("act" if jj else "dve",
                     (lambda e: e.copy(out=kT[:, :, kt * 128:(kt + 1) * 128], in_=ps_tr[i][:, jj * 4:(jj + 1) * 4, :])) if jj else
                     (lambda e: e.tensor_copy(out=kT[:, :, kt * 128:(kt + 1) * 128], in_=ps_tr[i][:, jj * 4:(jj + 1) * 4, :])),
                     reads=[ps_trb[i]], pw=[kT_b])
        for g in range(4):
            acc, accb = ps_o[nacc % 2], ps_ob[nacc % 2]
            nacc += 1
            steps = []
            for gi, (d, tiles) in enumerate(SD_TILES):
                for kt in tiles:
                    steps.append((gi, d, kt))
                steps.append((gi, d, "new"))
            for n, (gi, d, kt) in enumerate(steps):
                i = nst % 2
                nst += 1
                sp, spb = ps_s[i], ps_sb[i]
                q_ap = qn[:, gi * 16 + 4 * g:gi * 16 + 4 * g + 4, s * 8:(s + 1) * 8]
                if kt == "new":
                    nk, lhsT = 8, kn[:, g, s * 8:(s + 1) * 8]
                    pv = [(vnew[:, g * 64:(g + 1) * 64], 0, 64), (ones[0:8, :], 64, 1)]
                else:
                    nk, lhsT = 128, kT[:, g, kt * 128:(kt + 1) * 128]
                    pv = [(cb[:, kt, 256 + g * 64:256 + (g + 1) * 64], 0, 64), (ones[:], 64, 1)]
                off = offs[(d, kt)]
                P.op("pe", lambda e: e.matmul(sp[0:nk, :], lhsT=lhsT, rhs=q_ap, start=True, stop=True), reads=[kT_b, nb], writes=[spb])
                P.op("dve", lambda e: e.tensor_tensor(out=sst[i][0:nk, :].rearrange("p (r t) -> p r t", t=8), in0=sp[0:nk, :].rearrange("p (r t) -> p r t", t=8),
                                                      in1=balld[0:nk, 4 * g:4 * g + 4, off:off + 8], op=ALU.add), reads=[spb, balld_b], writes=[sst_b[i]])
                P.op("act", lambda e: e.activation(out=spt[i][0:nk, :], in_=sst[i][0:nk, :], func=AF.Exp), reads=[sst_b[i]], writes=[spt_b[i]])
                for j, (rhs, c0, ncol) in enumerate(pv):
                    first = (n == 0 and j == 0)
                    P.op("pe", lambda e: e.matmul(acc[:, c0:c0 + ncol], lhsT=spt[i][0:nk, :], rhs=rhs, start=first, stop=(n == len(steps) - 1), skip_group_check=True),
                         reads=[spt_b[i], cb_b, vnew_b, nb], pw=() if first else [accb], writes=[accb] if first else ())
            P.op("dve", lambda e: e.tensor_scalar(out=ors[:, 0:1], in0=acc[:, 64:65], scalar1=1e-30, scalar2=None, op0=ALU.max), reads=[accb], writes=[ors_b])
            P.op("dve", lambda e: e.reciprocal(out=ors[:, 0:1], in_=ors[:, 0:1]), writes=[ors_b])
            oi = nacc % 2
            P.op("dve", lambda e: e.tensor_scalar(out=osb[oi][:], in0=acc[:, 0:64], scalar1=ors[:, 0:1], scalar2=None, op0=ALU.mult),
                 reads=[accb, ors_b], writes=[osb_b[oi]])
            for r in range(4):
                h = 4 * g + r
                P.dma(o_dram[s * 8:(s + 1) * 8, h * 64:(h + 1) * 64], osb[oi][r * 8:(r + 1) * 8, :], reads=[osb_b[oi]], q="pool" if r % 2 else "sp")
    P.barrier()
    A.release(m0)


def cast_phase(k, src_f32, dst_bf16, T=128):
    P, A = k.P, k.A
    m0 = A.mark()
    a = A.sb([128, D], F32, "cast_a")
    b_ = A.sb([128, D], BF16, "cast_b")
    bb = Buf()
    P.dma(a[:], src_f32[0:T, :], writes=[bb])
    P.op("dve", lambda e: e.tensor_copy(out=b_[:], in_=a[:]), reads=[bb], writes=[bb])
    P.dma(dst_bf16[0:T, :], b_[:], reads=[bb], writes=[bb])
    P.barrier()
    A.release(m0)
```
